# Optimizing a Trainium2 kernel written in Bass

```python
import jax, jax.numpy as jnp
from jax import lax
import numpy as np

D_MODEL = 1024
BATCH = 8
SEQ = 4096
DEPTH = 4

N_MIXERS = 4
EPS = 1e-6
FFN_HIDDEN = -(-8 * D_MODEL // (3 * 256)) * 256

POOL_WINDOWS = (2, 4, 8, 16)
POOL_GROUP = D_MODEL // len(POOL_WINDOWS)

SB_HEAD_DIM = 128
SB_HEADS = D_MODEL // SB_HEAD_DIM
SB_BLOCK = 128

RET_HEADS = max(4, D_MODEL // 256)
RET_QK_DIM = D_MODEL // RET_HEADS
RET_V_DIM = 2 * RET_QK_DIM
RET_CHUNK = 128
ROPE_BASE = 10000.0

GDN_K_DIM = 128
GDN_HEADS = D_MODEL // GDN_K_DIM
GDN_V_DIM = 2 * GDN_K_DIM
GDN_CONV = 4
GDN_CHUNK = 64

N_POOL = (DEPTH + 3) // 4
N_SB = (DEPTH + 2) // 4
N_RET = (DEPTH + 1) // 4
N_GDN = DEPTH // 4

kernel_name = "interleaved_pool_stickbreak_retnet_gdn_trunk"


def rms_norm(x, gain):
    xf = x.astype(jnp.float32)
    y = xf * lax.rsqrt(jnp.mean(xf * xf, axis=-1, keepdims=True) + EPS)
    return (y * gain.astype(jnp.float32)).astype(x.dtype)


def l2_norm(x):
    xf = x.astype(jnp.float32)
    return xf * lax.rsqrt(jnp.sum(xf * xf, axis=-1, keepdims=True) + EPS)


def to_chunks(t, c):
    b, s, h, d = t.shape
    return t.reshape(b, s // c, c, h, d).transpose(1, 0, 3, 2, 4)


def from_chunks(t):
    n, b, h, c, d = t.shape
    return t.transpose(1, 0, 3, 2, 4).reshape(b, n * c, h, d)


def rotary(x):
    s, d = x.shape[1], x.shape[-1]
    half = d // 2
    inv = ROPE_BASE ** (-jnp.arange(half, dtype=jnp.float32) / half)
    ang = jnp.arange(s, dtype=jnp.float32)[:, None] * inv[None, :]
    cos, sin = jnp.cos(ang)[None, :, None, :], jnp.sin(ang)[None, :, None, :]
    xf = x.astype(jnp.float32)
    x1, x2 = xf[..., :half], xf[..., half:]
    return jnp.concatenate([x1 * cos - x2 * sin, x1 * sin + x2 * cos], axis=-1)


def causal_depthwise_conv(x, w):
    k, c = w.shape
    return lax.conv_general_dilated(
        x, w[:, None, :].astype(x.dtype), window_strides=(1,), padding=[(k - 1, 0)],
        dimension_numbers=('NWC', 'WIO', 'NWC'), feature_group_count=c)


def pool_mixer(h, w_group, scale):
    b, s, d = h.shape
    hg = h.astype(jnp.float32).reshape(b, s, len(POOL_WINDOWS), POOL_GROUP)
    cs = jnp.cumsum(hg, axis=1)
    pos1 = jnp.arange(1, s + 1, dtype=jnp.float32)
    outs = []
    for g, w in enumerate(POOL_WINDOWS):
        c = cs[:, :, g]
        prev = jnp.pad(c, ((0, 0), (w, 0), (0, 0)))[:, :s]
        cnt = jnp.minimum(pos1, float(w))[None, :, None]
        outs.append((c - prev) / cnt - hg[:, :, g])
    pooled = jnp.stack(outs, axis=2).astype(h.dtype)
    mixed = jnp.einsum('bsgc,gcd->bsgd', pooled, w_group)
    return mixed.reshape(b, s, d) * scale


def stick_breaking_mixer(h, w_qkv, q_gain, k_gain, w_o):
    b, s, _ = h.shape
    qkv = (h @ w_qkv).reshape(b, s, 3, SB_HEADS, SB_HEAD_DIM)
    q = rms_norm(qkv[:, :, 0], q_gain).transpose(0, 2, 1, 3) * (SB_HEAD_DIM ** -0.5)
    k = rms_norm(qkv[:, :, 1], k_gain).transpose(0, 2, 1, 3)
    v = qkv[:, :, 2].transpose(0, 2, 1, 3)
    outs = []
    for blk in range(s // SB_BLOCK):
        q0, q1 = blk * SB_BLOCK, (blk + 1) * SB_BLOCK
        z = jnp.einsum('bhqd,bhkd->bhqk', q[:, :, q0:q1], k[:, :, :q1],
                       preferred_element_type=jnp.float32)
        t_idx = q0 + jnp.arange(SB_BLOCK)[:, None]
        s_idx = jnp.arange(q1)[None, :]
        causal = s_idx < t_idx
        log_1m_beta = jnp.where(causal, jax.nn.log_sigmoid(-z), 0.0)
        after = lax.cumsum(log_1m_beta, axis=3, reverse=True) - log_1m_beta
        wts = jnp.where(causal, jnp.exp(jax.nn.log_sigmoid(z) + after), 0.0)
        outs.append(jnp.einsum('bhqk,bhkd->bhqd', wts.astype(v.dtype), v[:, :, :q1]))
    o = jnp.concatenate(outs, axis=2)
    return o.transpose(0, 2, 1, 3).reshape(b, s, SB_HEADS * SB_HEAD_DIM) @ w_o


def retention_mixer(h, w_in, gn_gain, w_o):
    b, s, _ = h.shape
    H, dk, dv, C = RET_HEADS, RET_QK_DIM, RET_V_DIM, RET_CHUNK
    proj = h @ w_in
    q, k, v, g = jnp.split(proj, [H * dk, 2 * H * dk, 2 * H * dk + H * dv], axis=-1)
    q = rotary(q.reshape(b, s, H, dk))
    k = rotary(k.reshape(b, s, H, dk)) * (dk ** -0.5)
    v = v.reshape(b, s, H, dv).astype(jnp.float32)
    log_gamma = jnp.log1p(-jnp.exp2(-5.0 - jnp.arange(H, dtype=jnp.float32)))
    idx = jnp.arange(C, dtype=jnp.float32)
    diff = idx[:, None] - idx[None, :]
    intra = jnp.where(diff >= 0, jnp.exp(jnp.maximum(diff, 0.0)[None] * log_gamma[:, None, None]), 0.0)
    q_dec = jnp.exp((idx + 1.0)[None] * log_gamma[:, None])[..., None]
    k_dec = jnp.exp((C - 1.0 - idx)[None] * log_gamma[:, None])[..., None]
    chunk_dec = jnp.exp(C * log_gamma)[:, None, None]

    def step(state, inp):
        qn, kn, vn = inp
        scores = jnp.einsum('bhcd,bhmd->bhcm', qn, kn) * intra
        o = (jnp.einsum('bhcm,bhme->bhce', scores, vn)
             + jnp.einsum('bhcd,bhde->bhce', qn * q_dec, state))
        state = state * chunk_dec + jnp.einsum('bhcd,bhce->bhde', kn * k_dec, vn)
        return state, o

    s0 = jnp.zeros((b, H, dk, dv), jnp.float32)
    _, o = lax.scan(step, s0, (to_chunks(q, C), to_chunks(k, C), to_chunks(v, C)))
    o = rms_norm(from_chunks(o), gn_gain).reshape(b, s, H * dv)
    o = o.astype(h.dtype) * jax.nn.silu(g)
    return o @ w_o


def gdn_mixer(h, w_in, conv_w, a_log, dt_bias, norm_gain, w_o):
    b, s, _ = h.shape
    H, dk, dv, C = GDN_HEADS, GDN_K_DIM, GDN_V_DIM, GDN_CHUNK
    n_qkv = 2 * H * dk + H * dv
    proj = h @ w_in
    qkv, z, a, bt = jnp.split(proj, [n_qkv, n_qkv + H * dv, n_qkv + H * dv + H], axis=-1)
    qkv = jax.nn.silu(causal_depthwise_conv(qkv, conv_w))
    q, k, v = jnp.split(qkv, [H * dk, 2 * H * dk], axis=-1)
    q = l2_norm(q.reshape(b, s, H, dk)) * (dk ** -0.5)
    k = l2_norm(k.reshape(b, s, H, dk))
    v = v.reshape(b, s, H, dv).astype(jnp.float32)
    beta = jax.nn.sigmoid(bt.astype(jnp.float32))
    g = -jnp.exp(a_log) * jax.nn.softplus(a.astype(jnp.float32) + dt_bias)

    tril = jnp.tril(jnp.ones((C, C), dtype=bool))
    strict = jnp.tril(jnp.ones((C, C), dtype=bool), -1)
    eye = jnp.eye(C, dtype=jnp.float32)

    def step(state, inp):
        qn, kn, vn, bn, gn = inp
        G = jnp.cumsum(gn, axis=-1)
        diff = G[..., :, None] - G[..., None, :]
        decay = jnp.where(tril, jnp.exp(jnp.where(tril, diff, 0.0)), 0.0)
        kb = kn * bn[..., None]
        n_mat = jnp.where(strict, jnp.einsum('bhcd,bhmd->bhcm', kb, kn) * decay, 0.0)
        t_mat = n_mat + eye
        u = lax.linalg.triangular_solve(t_mat, vn * bn[..., None], left_side=True,
                                        lower=True, unit_diagonal=True)
        w = lax.linalg.triangular_solve(t_mat, kb * jnp.exp(G)[..., None], left_side=True,
                                        lower=True, unit_diagonal=True)
        v_new = u - jnp.einsum('bhcd,bhde->bhce', w, state)
        attn = jnp.where(tril, jnp.einsum('bhcd,bhmd->bhcm', qn, kn) * decay, 0.0)
        o = (jnp.einsum('bhcd,bhde->bhce', qn * jnp.exp(G)[..., None], state)
             + jnp.einsum('bhcm,bhme->bhce', attn, v_new))
        g_last = G[..., -1:]
        state = (state * jnp.exp(g_last)[..., None]
                 + jnp.einsum('bhcd,bhce->bhde', kn * jnp.exp(g_last - G)[..., None], v_new))
        return state, o

    s0 = jnp.zeros((b, H, dk, dv), jnp.float32)
    xs = (to_chunks(q, C), to_chunks(k, C), to_chunks(v, C),
          to_chunks(beta[..., None], C)[..., 0], to_chunks(g[..., None], C)[..., 0])
    _, o = lax.scan(step, s0, xs)
    o = rms_norm(from_chunks(o), norm_gain).reshape(b, s, H * dv)
    o = o.astype(h.dtype) * jax.nn.silu(z)
    return o @ w_o


def swiglu(h, w_in, w_out):
    gate, up = jnp.split(h @ w_in, 2, axis=-1)
    return (jax.nn.silu(gate) * up) @ w_out


def setup_inputs(seed: int = 0) -> dict:
    key = jax.random.key(seed)
    ks = iter(jax.random.split(key, 32))

    def nrm(shape, fan_in):
        return jax.random.normal(next(ks), shape, jnp.float32) * (fan_in ** -0.5)

    def gain(shape):
        return 1.0 + 0.02 * jax.random.normal(next(ks), shape, jnp.float32)

    D, F = D_MODEL, FFN_HIDDEN
    x = jax.random.normal(next(ks), (BATCH, SEQ, D), jnp.float32)
    norm_mix = gain((DEPTH, D))
    norm_ffn = gain((DEPTH, D))
    ffn_w_in = nrm((DEPTH, D, 2 * F), D)
    ffn_w_out = nrm((DEPTH, F, D), F)
    pool_w = nrm((N_POOL, len(POOL_WINDOWS), POOL_GROUP, POOL_GROUP), POOL_GROUP)
    pool_scale = gain((N_POOL, D))
    sb_w_qkv = nrm((N_SB, D, 3 * SB_HEADS * SB_HEAD_DIM), D)
    sb_q_gain = gain((N_SB, SB_HEAD_DIM))
    sb_k_gain = gain((N_SB, SB_HEAD_DIM))
    sb_w_o = nrm((N_SB, SB_HEADS * SB_HEAD_DIM, D), SB_HEADS * SB_HEAD_DIM)
    ret_w_in = nrm((N_RET, D, 2 * RET_HEADS * RET_QK_DIM + 2 * RET_HEADS * RET_V_DIM), D)
    ret_gn_gain = gain((N_RET, RET_V_DIM))
    ret_w_o = nrm((N_RET, RET_HEADS * RET_V_DIM, D), RET_HEADS * RET_V_DIM)
    gdn_w_in = nrm((N_GDN, D, 2 * GDN_HEADS * GDN_K_DIM + 2 * GDN_HEADS * GDN_V_DIM + 2 * GDN_HEADS), D)
    gdn_conv_w = nrm((N_GDN, GDN_CONV, 2 * GDN_HEADS * GDN_K_DIM + GDN_HEADS * GDN_V_DIM), GDN_CONV)
    gdn_a_log = jnp.log(jax.random.uniform(next(ks), (N_GDN, GDN_HEADS), jnp.float32, 1.0, 16.0))
    dt = jnp.exp(jax.random.uniform(next(ks), (N_GDN, GDN_HEADS), jnp.float32,
                                    float(np.log(1e-3)), float(np.log(1e-1))))
    gdn_dt_bias = dt + jnp.log(-jnp.expm1(-dt))
    gdn_norm_gain = gain((N_GDN, GDN_V_DIM))
    gdn_w_o = nrm((N_GDN, GDN_HEADS * GDN_V_DIM, D), GDN_HEADS * GDN_V_DIM)
    return {"x": x, "norm_mix": norm_mix, "norm_ffn": norm_ffn,
            "ffn_w_in": ffn_w_in, "ffn_w_out": ffn_w_out,
            "pool_w": pool_w, "pool_scale": pool_scale,
            "sb_w_qkv": sb_w_qkv, "sb_q_gain": sb_q_gain, "sb_k_gain": sb_k_gain, "sb_w_o": sb_w_o,
            "ret_w_in": ret_w_in, "ret_gn_gain": ret_gn_gain, "ret_w_o": ret_w_o,
            "gdn_w_in": gdn_w_in, "gdn_conv_w": gdn_conv_w, "gdn_a_log": gdn_a_log,
            "gdn_dt_bias": gdn_dt_bias, "gdn_norm_gain": gdn_norm_gain, "gdn_w_o": gdn_w_o}


def reference(x, norm_mix, norm_ffn, ffn_w_in, ffn_w_out, pool_w, pool_scale,
              sb_w_qkv, sb_q_gain, sb_k_gain, sb_w_o, ret_w_in, ret_gn_gain, ret_w_o,
              gdn_w_in, gdn_conv_w, gdn_a_log, gdn_dt_bias, gdn_norm_gain, gdn_w_o):
    for layer in range(DEPTH):
        mixer, occ = layer % N_MIXERS, layer // N_MIXERS
        h = rms_norm(x, norm_mix[layer])
        if mixer == 0:
            y = pool_mixer(h, pool_w[occ], pool_scale[occ])
        elif mixer == 1:
            y = stick_breaking_mixer(h, sb_w_qkv[occ], sb_q_gain[occ], sb_k_gain[occ], sb_w_o[occ])
        elif mixer == 2:
            y = retention_mixer(h, ret_w_in[occ], ret_gn_gain[occ], ret_w_o[occ])
        else:
            y = gdn_mixer(h, gdn_w_in[occ], gdn_conv_w[occ], gdn_a_log[occ], gdn_dt_bias[occ],
                          gdn_norm_gain[occ], gdn_w_o[occ])
        x = x + y.astype(x.dtype)
        x = x + swiglu(rms_norm(x, norm_ffn[layer]), ffn_w_in[layer], ffn_w_out[layer]).astype(x.dtype)
    return x
```

```python
import numpy as np
import ml_dtypes
import concourse.bass as bass
import concourse.mybir as mybir
from concourse.bass_utils import run_bass_kernel_spmd

F32 = mybir.dt.float32
BF16 = mybir.dt.bfloat16
AF = mybir.ActivationFunctionType
ALU = mybir.AluOpType
AX = mybir.AxisListType

D = 1024
S = 4096
NCH = D // 128
FF = 2816
NJ = FF // 128
EPS = 1e-6
TT = 512
NTT = S // TT

ENGS = ("pe", "act", "dve", "pool", "sp")
NSLOT = 8


class Ins:
    __slots__ = ("fn", "eng", "idx", "waits", "need_inc", "epoch", "dma", "rank", "slotwait")

    def __init__(self, fn, eng, idx, epoch, dma):
        self.fn = fn
        self.eng = eng
        self.idx = idx
        self.waits = []
        self.need_inc = False
        self.epoch = epoch
        self.dma = dma
        self.rank = None
        self.slotwait = None


class Prog:
    def __init__(self):
        self.ins = {e: [] for e in ENGS}
        self.res = {}
        self.epoch = 0
        self.ndma = {e: 0 for e in ENGS}
        self.pending_barrier = {e: [] for e in ENGS}

    def new_epoch(self):
        self.epoch += 1

    def barrier(self):
        lasts = []
        for e in ENGS:
            if self.ins[e]:
                lasts.append(self.ins[e][-1])
            k = 0
            for ins in reversed(self.ins[e]):
                if ins.dma is not None:
                    lasts.append(ins)
                    k += 1
                    if k >= NSLOT:
                        break
        for e in ENGS:
            self.pending_barrier[e] = list(lasts)

    def op(self, eng, fn, r=(), w=(), dma=False):
        w = list(w) + [k for k in r if k.startswith("ps") and k[2:].isdigit() and k not in w]
        lst = self.ins[eng]
        ins = Ins(fn, eng, len(lst), self.epoch, None)
        if dma:
            ins.dma = self.ndma[eng]
            self.ndma[eng] += 1
        deps = []
        if self.pending_barrier[eng]:
            deps.extend(self.pending_barrier[eng])
            self.pending_barrier[eng] = []
        for k in r:
            st = self.res.get(k)
            if st is None:
                st = self.res[k] = [None, {}]
            if st[0] is not None:
                deps.append(st[0])
        for k in w:
            st = self.res.get(k)
            if st is None:
                st = self.res[k] = [None, {}]
            if st[0] is not None:
                deps.append(st[0])
            for rd in st[1].values():
                deps.append(rd)
        for d in deps:
            if d is ins:
                continue
            if d.eng == eng and d.dma is None and eng == "pe":
                continue
            if d.eng == eng and d.dma is None and ins.dma is not None and eng in ("sp",):
                continue
            ins.waits.append(d)
            d.need_inc = True
        for k in r:
            self.res[k][1][eng + ("D" if dma else "")] = ins
        for k in w:
            st = self.res[k]
            st[0] = ins
            st[1] = {}
        lst.append(ins)
        return ins

    def finalize(self, nc, stack):
        nep = self.epoch + 1
        self.csem = {}
        for e in ("pe", "act", "dve", "pool"):
            for ep in range(nep):
                used = any(i.need_inc and i.dma is None and i.epoch == ep for i in self.ins[e])
                if used:
                    self.csem[(e, ep)] = stack.enter_context(nc.semaphore(f"c_{e}_{ep}"))
        self.dsem = {}
        for e in ENGS:
            if self.ndma[e]:
                for s in range(NSLOT):
                    self.dsem[(e, s)] = stack.enter_context(nc.semaphore(f"d_{e}_{s}"))
        for e in ENGS:
            cnt = {}
            for i in self.ins[e]:
                if i.dma is None:
                    if i.need_inc:
                        cnt[i.epoch] = cnt.get(i.epoch, 0) + 1
                        i.rank = cnt[i.epoch]
                        assert i.rank < 60000, "semaphore overflow; add epochs"
                else:
                    i.need_inc = True

    def target(self, d):
        if d.dma is None:
            return self.csem[(d.eng, d.epoch)], d.rank
        return self.dsem[(d.eng, d.dma % NSLOT)], 16 * (d.dma // NSLOT + 1)

    def emit(self, eng, engobj):
        waited = {}
        for i in self.ins[eng]:
            ws = []
            if i.dma is not None and i.dma >= NSLOT:
                ws.append((self.dsem[(eng, i.dma % NSLOT)], 16 * (i.dma // NSLOT)))
            for d in i.waits:
                ws.append(self.target(d))
            for sem, val in ws:
                if waited.get(sem.num, 0) >= val:
                    continue
                waited[sem.num] = val
                engobj.wait_ge(sem, val)
            bi = i.fn(engobj)
            if i.dma is not None:
                bi.then_inc(self.dsem[(eng, i.dma % NSLOT)], 16)
            elif i.need_inc:
                bi.then_inc(self.csem[(eng, i.epoch)], 1)

    def final_waits(self, eng, engobj, outs):
        for d in outs:
            sem, val = self.target(d)
            engobj.wait_ge(sem, val)


class Arena:
    def __init__(self, ap, words):
        self.ap = ap
        self.words = words
        self.off = 0
        self.uid = 0

    def reset(self):
        self.off = 0

    def alloc(self, shape, dt, name):
        n = 1
        for s in shape[1:]:
            n *= s
        nbytes = n * (4 if dt == F32 else 2)
        nw = (nbytes + 3) // 4
        nw = (nw + 7) // 8 * 8
        assert self.off + nw <= self.words, f"arena overflow {name}: {self.off}+{nw}>{self.words}"
        v = self.ap[:, self.off:self.off + nw]
        self.off += nw
        if dt != F32:
            v = v.bitcast(dt)
        v = v[0:shape[0], 0:n]
        if len(shape) == 3:
            v = v.rearrange("p (a b) -> p a b", a=shape[1])
        elif len(shape) == 4:
            v = v.rearrange("p (a b c) -> p a b c", a=shape[1], b=shape[2])
        self.uid += 1
        return v


ARENA_WORDS = 19 * 1024
GDN_PHASES = 3
DEBUG_DUMP = []
DBG = {}
GDN_CUT = 99


def build(n_sub=8, with_mixers=True, only=None):
    from contextlib import ExitStack
    nc = bass.Bass("TRN2", target_bir_lowering=False)
    dr = {}

    SHAPES = {
        "x": [S, D], "norm_mix": [4, D], "norm_ffn": [4, D], "ffn_w_in": [4, D, 2 * FF], "ffn_w_out": [4, FF, D],
        "pool_w": [1, 4, 256, 256], "pool_scale": [1, D],
        "sb_w_qkv": [1, D, 3072], "sb_q_gain": [1, 128], "sb_k_gain": [1, 128], "sb_w_o": [1, D, D],
        "ret_w_in": [1, D, 6144], "ret_gn_gain": [1, 512], "ret_w_o": [1, 2048, D],
        "gdn_w_in": [1, D, 6160], "gdn_conv_w": [1, 4, 4096], "gdn_a_log": [1, 8], "gdn_dt_bias": [1, 8],
        "gdn_norm_gain": [1, 256], "gdn_w_o": [1, 2048, D],
    }
    for k_, v_ in CONST_SHAPES.items():
        SHAPES[k_] = list(v_)

    def G(name):
        if name not in dr:
            dr[name] = nc.dram_tensor(name, SHAPES[name], F32, kind="ExternalInput").ap()
        return dr[name]

    x_d = G("x")
    norm_mix = G("norm_mix")
    norm_ffn = G("norm_ffn")
    c_ident = G("c_ident")
    c_poolinv = G("c_poolinv")
    y_d = nc.dram_tensor("y", [S, D], F32, kind="ExternalOutput").ap()

    P = Prog()
    st = ExitStack()
    with st:
        xT = st.enter_context(nc.sbuf_tensor("xT", [128, NCH, S], F32))
        arena_t = st.enter_context(nc.sbuf_tensor("arena", [128, ARENA_WORDS], F32))
        cst_t = st.enter_context(nc.sbuf_tensor("cst", [128, 896], F32))
        ps = [st.enter_context(nc.psum_tensor(f"ps{i}", [128, 512], F32)) for i in range(8)]
        A = Arena(arena_t, ARENA_WORDS)
        C = Arena(cst_t, 896)

        ident = C.alloc([128, 128], F32, "ident")
        ones_bf = C.alloc([128, 128], BF16, "ones")
        gmix = C.alloc([128, 4, NCH], F32, "gmix")
        gffn = C.alloc([128, 4, NCH], F32, "gffn")
        pscale = C.alloc([128, NCH], F32, "pscale")
        poolinv = C.alloc([128, 4, 16], F32, "poolinv")
        P.op("sp", lambda e: e.dma_start(out=ident, in_=c_ident[:, :]), w=["ident"], dma=True)
        P.op("sp", lambda e: e.dma_start(out=poolinv, in_=c_poolinv[:, :, :]), w=["poolinv"], dma=True)
        P.op("dve", lambda e: e.memset(ones_bf, 1.0), w=["ones"])
        epsD = C.alloc([128, 1], F32, "epsD")
        P.op("dve", lambda e: e.memset(epsD, EPS * D), w=["epsD"])
        vstage = C.alloc([128, 128], F32, "vstage")
        nvec = [0]

        def load_vecT(src2d, nrows, dst, key, scale=1.0):
            pb = nvec[0] % 2
            nvec[0] += 1
            P.op("sp", lambda e: e.dma_start(out=vstage[0:nrows, :], in_=src2d), w=["vstage"], dma=True)
            P.op("pe", lambda e: e.transpose(ps[pb][:, 0:nrows], vstage[0:nrows, :], ident[0:nrows, 0:nrows]),
                 r=["vstage", "ident"], w=[f"ps{pb}"])
            P.op("dve", lambda e: e.tensor_scalar_mul(dst, ps[pb][:, 0:nrows], scale), r=[f"ps{pb}"], w=[key])

        rowh = C.alloc([1, 128], BF16, "rowh")
        rowl = C.alloc([1, 128], BF16, "rowl")

        def bcast_row(src_row, n, dst, key):
            pb = nvec[0] % 2
            nvec[0] += 1
            P.op("sp", lambda e: e.dma_start(out=vstage[0:1, 0:n], in_=src_row), w=["vstage"], dma=True)
            P.op("dve", lambda e: e.tensor_copy(rowh[0:1, 0:n], vstage[0:1, 0:n]), r=["vstage"], w=["rowh"])
            P.op("dve", lambda e: e.tensor_tensor(out=rowl[0:1, 0:n], in0=vstage[0:1, 0:n], in1=rowh[0:1, 0:n], op=ALU.subtract),
                 r=["vstage", "rowh"], w=["rowl"])
            P.op("pe", lambda e: e.matmul(ps[pb][:, 0:n], lhsT=ones_bf[0:1, :], rhs=rowh[0:1, 0:n], start=True, stop=False),
                 r=["rowh", "ones"], w=[f"ps{pb}"])
            P.op("pe", lambda e: e.matmul(ps[pb][:, 0:n], lhsT=ones_bf[0:1, :], rhs=rowl[0:1, 0:n], start=False, stop=True),
                 r=["rowl", "ones"], w=[f"ps{pb}"])
            P.op("dve", lambda e: e.tensor_copy(dst, ps[pb][:, 0:n]), r=[f"ps{pb}"], w=[key])

        SQD = float(np.sqrt(D))
        load_vecT(norm_mix.rearrange("l (c p) -> (l c) p", p=128), 32, gmix.rearrange("p l c -> p (l c)"), "gmix", SQD)
        load_vecT(norm_ffn.rearrange("l (c p) -> (l c) p", p=128), 32, gffn.rearrange("p l c -> p (l c)"), "gffn", SQD)
        load_vecT(G("pool_scale").rearrange("l (c p) -> (l c) p", p=128), 8, pscale, "pscale")

        def stage_load():
            A.reset()
            xin = [A.alloc([128, D], F32, f"xin{b}") for b in range(2)]
            for i in range(S // 128):
                b = i % 2
                P.op("sp", lambda e, i=i, b=b: e.dma_start(out=xin[b], in_=x_d[i * 128:(i + 1) * 128, :]),
                     w=[f"xin{b}"], dma=True)
                for h in range(2):
                    pb = ps[(2 * i + h) % 8]
                    for q in range(4):
                        c = 4 * h + q
                        P.op("pe", lambda e, pb=pb, q=q, c=c, b=b: e.transpose(
                            pb[:, q * 128:(q + 1) * 128], xin[b][:, c * 128:(c + 1) * 128], ident),
                            r=[f"xin{b}", "ident"], w=[f"ps{(2 * i + h) % 8}"])
                    eng = "act" if h == 0 else "dve"
                    dst = xT[:, 4 * h:4 * h + 4, i * 128:(i + 1) * 128]
                    src = pb[:, :].rearrange("p (a b) -> p a b", a=4)
                    if eng == "act":
                        P.op("act", lambda e, dst=dst, src=src: e.activation(out=dst, in_=src, func=AF.Copy),
                             r=[f"ps{(2 * i + h) % 8}"], w=[f"xTl{i}_{h}"])
                    else:
                        P.op("dve", lambda e, dst=dst, src=src: e.tensor_copy(dst, src),
                             r=[f"ps{(2 * i + h) % 8}"], w=[f"xTl{i}_{h}"])

        outs = []

        def stage_store():
            A.reset()
            xo = [A.alloc([128, D], F32, f"xo{b}") for b in range(2)]
            for i in range(S // 128):
                b = i % 2
                for h in range(2):
                    pb = ps[(2 * i + h) % 8]
                    for q in range(4):
                        c = 4 * h + q
                        P.op("pe", lambda e, pb=pb, q=q, c=c, i=i: e.transpose(
                            pb[:, q * 128:(q + 1) * 128], xT[:, c, i * 128:(i + 1) * 128], ident),
                            r=["ident"], w=[f"ps{(2 * i + h) % 8}"])
                    dst = xo[b][:, h * 512:(h + 1) * 512]
                    if h == 0:
                        P.op("act", lambda e, dst=dst, pb=pb: e.activation(out=dst, in_=pb[:, :], func=AF.Copy),
                             r=[f"ps{(2 * i + h) % 8}"], w=[f"xo{b}_{h}"])
                    else:
                        P.op("dve", lambda e, dst=dst, pb=pb: e.tensor_copy(dst, pb[:, :]),
                             r=[f"ps{(2 * i + h) % 8}"], w=[f"xo{b}_{h}"])
                outs.append(P.op("sp", lambda e, i=i, b=b: e.dma_start(out=y_d[i * 128:(i + 1) * 128, :], in_=xo[b]),
                                 r=[f"xo{b}_0", f"xo{b}_1"], dma=True))

        def stage_ffn(l):
            A.reset()
            rstd = A.alloc([128, TT], F32, "rstd")
            hT = A.alloc([128, NCH, TT], BF16, "hT")
            sil = [A.alloc([128, TT], F32, f"sil{b}") for b in range(2)]
            aT = A.alloc([128, NJ, TT], BF16, "aT")
            sqb = aT[:, 0:NCH, :]
            NWB = 2
            win = [A.alloc([128, 2, NCH, 256], BF16, f"win{b}") for b in range(NWB)]
            wout = [A.alloc([128, NJ, 256], BF16, f"wout{b}") for b in range(2)]
            w_in_v = G("ffn_w_in")[l].rearrange("(kc p) (gu f) -> p gu kc f", p=128, gu=2)
            w_out_v = G("ffn_w_out")[l].rearrange("(j p) d -> p j d", p=128)
            nwin = 0
            nwout = 0
            for T in range(NTT):
                t0 = T * TT
                rmsnorm_tile_ffn(T, l, sqb, rstd, hT)
                for jg in range(NJ // 2):
                    wb = nwin % NWB
                    nwin += 1
                    P.op("pool", lambda e, wb=wb, jg=jg: e.dma_start(
                        out=win[wb], in_=w_in_v[:, :, :, jg * 256:(jg + 1) * 256]),
                        w=[f"win{wb}"], dma=True)
                    for jj in range(2):
                        j = jg * 2 + jj
                        pg, pu = (0, 1) if j % 2 == 0 else (2, 3)
                        for k in range(NCH):
                            P.op("pe", lambda e, k=k, wb=wb, jj=jj, pg=pg: e.matmul(
                                ps[pg][:, :], lhsT=win[wb][:, 0, k, jj * 128:(jj + 1) * 128], rhs=hT[:, k, :],
                                start=(k == 0), stop=(k == NCH - 1)),
                                r=[f"win{wb}", f"fh{k}"], w=[f"ps{pg}"])
                        for k in range(NCH):
                            P.op("pe", lambda e, k=k, wb=wb, jj=jj, pu=pu: e.matmul(
                                ps[pu][:, :], lhsT=win[wb][:, 1, k, jj * 128:(jj + 1) * 128], rhs=hT[:, k, :],
                                start=(k == 0), stop=(k == NCH - 1)),
                                r=[f"win{wb}", f"fh{k}"], w=[f"ps{pu}"])
                        sb_ = j % 2
                        P.op("act", lambda e, sb_=sb_, pg=pg: e.activation(out=sil[sb_], in_=ps[pg][:, :], func=AF.Silu),
                             r=[f"ps{pg}"], w=[f"sil{sb_}"])
                        P.op("dve", lambda e, sb_=sb_, pu=pu, j=j: e.tensor_tensor(
                            out=aT[:, j, :], in0=sil[sb_], in1=ps[pu][:, :], op=ALU.mult),
                            r=[f"sil{sb_}", f"ps{pu}"], w=[f"aT{j}"])
                for dg in range(NCH // 2):
                    ob = nwout % 2
                    nwout += 1
                    P.op("pool", lambda e, ob=ob, dg=dg: e.dma_start(
                        out=wout[ob], in_=w_out_v[:, :, dg * 256:(dg + 1) * 256]),
                        w=[f"wout{ob}"], dma=True)
                    for dd in range(2):
                        d = dg * 2 + dd
                        py = 4 + (d % 4)
                        for j in range(NJ):
                            P.op("pe", lambda e, j=j, ob=ob, dd=dd, py=py: e.matmul(
                                ps[py][:, :], lhsT=wout[ob][:, j, dd * 128:(dd + 1) * 128], rhs=aT[:, j, :],
                                start=(j == 0), stop=(j == NJ - 1)),
                                r=[f"wout{ob}", f"aT{j}"], w=[f"ps{py}"])
                        P.op("dve", lambda e, d=d, py=py, t0=t0: e.tensor_tensor(
                            out=xT[:, d, t0:t0 + TT], in0=xT[:, d, t0:t0 + TT], in1=ps[py][:, :], op=ALU.add),
                            r=[f"ps{py}", f"xT{T}"], w=[f"xT{T}"])

        def norm_tile(T, gain, sq, sqk, rstd, rstdk, dst, dstk, pn=7, extra_r=()):
            t0 = T * TT
            for c in range(NCH):
                P.op("act", lambda e, c=c: e.activation(out=sq[c], in_=xT[:, c, t0:t0 + TT], func=AF.Square),
                     r=[f"xT{T}"], w=[sqk[c]])
            for c in range(NCH):
                P.op("pe", lambda e, c=c: e.matmul(ps[pn][:, :], lhsT=ones_bf, rhs=sq[c],
                                                   start=(c == 0), stop=(c == NCH - 1)),
                     r=[sqk[c], "ones"], w=[f"ps{pn}"])
            P.op("act", lambda e: e.activation(out=rstd, in_=ps[pn][:, :], func=AF.Sqrt, bias=epsD, scale=1.0),
                 r=[f"ps{pn}", "epsD"], w=[rstdk])
            P.op("dve", lambda e: e.reciprocal(rstd, rstd), r=[rstdk], w=[rstdk])
            for c in range(NCH):
                P.op("dve", lambda e, c=c: e.scalar_tensor_tensor(
                    out=dst[c], in0=xT[:, c, t0:t0 + TT], scalar=gain[:, c:c + 1], in1=rstd,
                    op0=ALU.mult, op1=ALU.mult),
                    r=[f"xT{T}", rstdk, "gmix", "gffn"] + list(extra_r), w=[dstk[c]])

        def rmsnorm_tile_ffn(T, l, sqb, rstd, hT):
            norm_tile(T, gffn[:, l, :], [sqb[:, c, :] for c in range(NCH)], [f"aT{c}" for c in range(NCH)],
                      rstd, "rstd", [hT[:, c, :] for c in range(NCH)], [f"fh{c}" for c in range(NCH)])

        def stage_pool(l):
            A.reset()
            HW = 16
            rstd = A.alloc([128, TT], F32, "rstd")
            sqb = A.alloc([128, NCH, TT], BF16, "sqb")
            hp = A.alloc([128, NCH, HW + TT], F32, "hp")
            wk = [[A.alloc([128, HW + TT], F32, f"wk{b}{q}") for q in range(2)] for b in range(2)]
            pooled = A.alloc([128, NCH, TT], BF16, "pooled")
            tmp16 = A.alloc([128, 16], F32, "tmp16")
            pw = A.alloc([128, 4, 2, 256], BF16, "pw")
            P.op("pool", lambda e: e.dma_start(out=pw, in_=G("pool_w")[0].rearrange("g (cc p) d -> p g cc d", p=128)),
                 w=["pw"], dma=True)
            for c in range(NCH):
                P.op("dve", lambda e, c=c: e.memset(hp[:, c, 0:HW], 0.0), w=[f"hp{c}"])
            for T in range(NTT):
                t0 = T * TT
                if T > 0:
                    for c in range(NCH):
                        P.op("act", lambda e, c=c: e.activation(out=hp[:, c, 0:HW], in_=hp[:, c, TT:TT + HW], func=AF.Copy),
                             r=[f"hp{c}"], w=[f"hp{c}"])
                norm_tile(T, gmix[:, l, :], [sqb[:, c, :] for c in range(NCH)], [f"sq{c}" for c in range(NCH)],
                          rstd, "rstd", [hp[:, c, HW:HW + TT] for c in range(NCH)], [f"hp{c}" for c in range(NCH)])
                for c in range(NCH):
                    g = c // 2
                    eng = "dve" if c % 2 == 0 else "pool"
                    b = c % 2
                    src = hp[:, c, :]
                    srck = f"hp{c}"
                    for lvl in range(g + 1):
                        sh = 1 << lvl
                        lo = (1 << (lvl + 1)) - 1
                        dstb = wk[b][lvl % 2]
                        dk = f"wk{b}{lvl % 2}"
                        P.op(eng, lambda e, dstb=dstb, src=src, lo=lo, sh=sh: e.tensor_tensor(
                            out=dstb[:, lo:HW + TT], in0=src[:, lo:HW + TT], in1=src[:, lo - sh:HW + TT - sh], op=ALU.add),
                            r=[srck], w=[dk])
                        src = dstb
                        srck = dk
                    w_ = 1 << (g + 1)
                    P.op("dve", lambda e, src=src, c=c, w_=w_: e.scalar_tensor_tensor(
                        out=pooled[:, c, :], in0=src[:, HW:HW + TT], scalar=1.0 / w_, in1=hp[:, c, HW:HW + TT],
                        op0=ALU.mult, op1=ALU.subtract),
                        r=[srck, f"hp{c}"], w=[f"pooled{c}"])
                    if T == 0:
                        P.op("dve", lambda e, src=src, g=g: e.tensor_tensor(
                            out=tmp16, in0=src[:, HW:HW + 16], in1=poolinv[:, g, :], op=ALU.mult),
                            r=[srck, "poolinv"], w=["tmp16"])
                        P.op("dve", lambda e, c=c: e.tensor_tensor(
                            out=pooled[:, c, 0:16], in0=tmp16, in1=hp[:, c, HW:HW + 16], op=ALU.subtract),
                            r=["tmp16", f"hp{c}", f"pooled{c}"], w=[f"pooled{c}"])
                for d in range(NCH):
                    g = d // 2
                    dd = d % 2
                    py = d % 4
                    for cc in range(2):
                        P.op("pe", lambda e, g=g, cc=cc, dd=dd, py=py: e.matmul(
                            ps[py][:, :], lhsT=pw[:, g, cc, dd * 128:(dd + 1) * 128], rhs=pooled[:, 2 * g + cc, :],
                            start=(cc == 0), stop=(cc == 1)),
                            r=["pw", f"pooled{2 * g + cc}"], w=[f"ps{py}"])
                    P.op("dve", lambda e, d=d, py=py, t0=t0: e.scalar_tensor_tensor(
                        out=xT[:, d, t0:t0 + TT], in0=ps[py][:, :], scalar=pscale[:, d:d + 1], in1=xT[:, d, t0:t0 + TT],
                        op0=ALU.mult, op1=ALU.add),
                        r=[f"ps{py}", f"xT{T}", "pscale"], w=[f"xT{T}"])

        def stage_sb(l):
            SBH = 8
            qT_s = nc.dram_tensor("qT_s", [SBH, 128, S], BF16, kind="Internal").ap()
            kT_s = nc.dram_tensor("kT_s", [SBH, 128, S], BF16, kind="Internal").ap()
            v_s = nc.dram_tensor("v_s", [S, D], BF16, kind="Internal").ap()
            o_s = nc.dram_tensor("o_s", [SBH, 128, S], BF16, kind="Internal").ap()
            w_qkv = G("sb_w_qkv")[0].rearrange("(kc p) n -> p kc n", p=128)
            w_o = G("sb_w_o")[0].rearrange("(kc p) n -> p kc n", p=128)
            A.reset()
            qg = A.alloc([128, 2], F32, "qg")
            load_vecT(G("sb_q_gain"), 1, qg[:, 0:1], "qg0")
            load_vecT(G("sb_k_gain"), 1, qg[:, 1:2], "qg1", float(np.sqrt(128.0)))
            eps128 = A.alloc([128, 1], F32, "eps128")
            P.op("dve", lambda e: e.memset(eps128, EPS * 128), w=["eps128"])
            rstd = A.alloc([128, TT], F32, "rstd")
            sqb = A.alloc([128, NCH, TT], BF16, "sqb")
            hT = A.alloc([128, NCH, TT], BF16, "hT")
            wq = [A.alloc([128, NCH, 512], BF16, f"wq{b}") for b in range(2)]
            sqh = [A.alloc([128, TT], BF16, f"sqh{b}") for b in range(2)]
            r2 = [A.alloc([128, TT], F32, f"r2{b}") for b in range(2)]
            qn = [A.alloc([128, TT], BF16, f"qn{b}") for b in range(2)]
            vsb = [A.alloc([128, 512], BF16, f"vsb{b}") for b in range(2)]
            nw = 0
            nq = 0
            nv = 0
            for T in range(NTT):
                t0 = T * TT
                norm_tile(T, gmix[:, l, :], [sqb[:, c, :] for c in range(NCH)], [f"sq{c}" for c in range(NCH)],
                          rstd, "rstd", [hT[:, c, :] for c in range(NCH)], [f"h{c}" for c in range(NCH)])
                hkeys = [f"h{c}" for c in range(NCH)]
                for piece in range(6):
                    wb = nw % 2
                    nw += 1
                    P.op("pool", lambda e, wb=wb, piece=piece: e.dma_start(
                        out=wq[wb], in_=w_qkv[:, :, piece * 512:(piece + 1) * 512]), w=[f"wq{wb}"], dma=True)
                    if piece < 4:
                        isk = piece // 2
                        for cc in range(4):
                            hd = (piece % 2) * 4 + cc
                            pq = cc % 2
                            pss = 2 + cc % 2
                            b = nq % 2
                            nq += 1
                            for kc in range(NCH):
                                P.op("pe", lambda e, kc=kc, wb=wb, cc=cc, pq=pq: e.matmul(
                                    ps[pq][:, :], lhsT=wq[wb][:, kc, cc * 128:(cc + 1) * 128], rhs=hT[:, kc, :],
                                    start=(kc == 0), stop=(kc == NCH - 1)),
                                    r=[f"wq{wb}", f"h{kc}"], w=[f"ps{pq}"])
                            P.op("act", lambda e, b=b, pq=pq: e.activation(out=sqh[b], in_=ps[pq][:, :], func=AF.Square),
                                 r=[f"ps{pq}"], w=[f"sqh{b}"])
                            P.op("pe", lambda e, b=b, pss=pss: e.matmul(ps[pss][:, :], lhsT=ones_bf, rhs=sqh[b],
                                                                       start=True, stop=True),
                                 r=[f"sqh{b}", "ones"], w=[f"ps{pss}"])
                            P.op("act", lambda e, b=b, pss=pss: e.activation(out=r2[b], in_=ps[pss][:, :], func=AF.Sqrt,
                                                                            bias=eps128, scale=1.0),
                                 r=[f"ps{pss}", "eps128"], w=[f"r2{b}"])
                            P.op("dve", lambda e, b=b: e.reciprocal(r2[b], r2[b]), r=[f"r2{b}"], w=[f"r2{b}"])
                            P.op("dve", lambda e, b=b, pq=pq, isk=isk: e.scalar_tensor_tensor(
                                out=qn[b], in0=ps[pq][:, :], scalar=qg[:, isk:isk + 1], in1=r2[b],
                                op0=ALU.mult, op1=ALU.mult),
                                r=[f"ps{pq}", f"r2{b}", "qg0", "qg1"], w=[f"qn{b}"])
                            dst = (kT_s if isk else qT_s)[hd, :, t0:t0 + TT]
                            P.op("sp", lambda e, b=b, dst=dst: e.dma_start(out=dst, in_=qn[b]),
                                 r=[f"qn{b}"], w=["kT_s" if isk else "qT_s"], dma=True)
                    else:
                        for s_ in range(4):
                            pv = 4 + s_ % 2
                            b = nv % 2
                            nv += 1
                            for kc in range(NCH):
                                P.op("pe", lambda e, kc=kc, wb=wb, s_=s_, pv=pv: e.matmul(
                                    ps[pv][:, :], lhsT=hT[:, kc, s_ * 128:(s_ + 1) * 128], rhs=wq[wb][:, kc, :],
                                    start=(kc == 0), stop=(kc == NCH - 1)),
                                    r=[f"wq{wb}", f"h{kc}"], w=[f"ps{pv}"])
                            P.op("act", lambda e, b=b, pv=pv: e.activation(out=vsb[b], in_=ps[pv][:, :], func=AF.Copy),
                                 r=[f"ps{pv}"], w=[f"vsb{b}"])
                            r0 = t0 + s_ * 128
                            c0 = (piece - 4) * 512
                            P.op("sp", lambda e, b=b, r0=r0, c0=c0: e.dma_start(
                                out=v_s[r0:r0 + 128, c0:c0 + 512], in_=vsb[b]),
                                r=[f"vsb{b}"], w=["v_s"], dma=True)
            P.barrier()
            A.reset()
            c_sbmask = G("c_sbmask")
            c_ustrict = G("c_ustrict")
            mask = A.alloc([128, 4, 512], BF16, "mask")
            ustr = A.alloc([128, 128], BF16, "ustr")
            P.op("pool", lambda e: e.dma_start(out=mask, in_=c_sbmask[:, :, :]), w=["mask"], dma=True)
            P.op("pool", lambda e: e.dma_start(out=ustr, in_=c_ustrict[:, :]), w=["ustr"], dma=True)
            G("c_sbmask"); G("c_ustrict")
            qT = A.alloc([128, S], BF16, "qT")
            kT = A.alloc([128, S], BF16, "kT")
            vv = A.alloc([128, S // 128, 128], BF16, "vv")
            ebuf = [A.alloc([128, 512], F32, f"eb{b}") for b in range(2)]
            spf = [A.alloc([128, 512], F32, f"spf{b}") for b in range(2)]
            spb = [A.alloc([128, 512], BF16, f"spb{b}") for b in range(2)]
            R = A.alloc([128, 512], F32, "R")
            Rb = [A.alloc([128, 512], BF16, f"Rb{b}") for b in range(2)]
            tb = [A.alloc([128, 512], F32, f"tb{b}") for b in range(2)]
            wT = [A.alloc([128, 512], BF16, f"wT{b}") for b in range(2)]
            osb = [A.alloc([128, 512], BF16, f"osb{b}") for b in range(2)]
            v_view = v_s.rearrange("(n p) (h d) -> p h n d", p=128, h=SBH)
            it = 0
            for hd in range(SBH):
                P.op("sp", lambda e, hd=hd: e.dma_start(out=qT, in_=qT_s[hd]), r=["qT_s"], w=["qT"], dma=True)
                P.op("sp", lambda e, hd=hd: e.dma_start(out=kT, in_=kT_s[hd]), r=["kT_s"], w=["kT"], dma=True)
                P.op("sp", lambda e, hd=hd: e.dma_start(out=vv, in_=v_view[:, hd]), r=["v_s"], w=["vv"], dma=True)
                for sq in range(S // 512):
                    q0 = sq * 512
                    po = 4 + sq % 2
                    P.op("pool", lambda e: e.memset(R, 0.0), w=["R"])
                    nkb = 4 * sq + 4
                    for kb in reversed(range(nkb)):
                        diag = kb - 4 * sq
                        b = it % 2
                        it += 1
                        pz = b
                        pa = 2 + b
                        P.op("pe", lambda e, kb=kb, q0=q0, pz=pz: e.matmul(
                            ps[pz][:, :], lhsT=kT[:, kb * 128:(kb + 1) * 128], rhs=qT[:, q0:q0 + 512],
                            start=True, stop=True), r=["kT", "qT"], w=[f"ps{pz}"])
                        P.op("act", lambda e, b=b, pz=pz: e.activation(out=ebuf[b], in_=ps[pz][:, :], func=AF.Exp),
                             r=[f"ps{pz}"], w=[f"eb{b}"])
                        P.op("act", lambda e, b=b: e.activation(out=spf[b], in_=ebuf[b], func=AF.Ln, bias=1.0, scale=1.0),
                             r=[f"eb{b}"], w=[f"spf{b}"])
                        if diag >= 0:
                            P.op("dve", lambda e, b=b, diag=diag: e.tensor_tensor(
                                out=spf[b], in0=spf[b], in1=mask[:, diag, :], op=ALU.mult),
                                r=[f"spf{b}", "mask"], w=[f"spf{b}"])
                        P.op("pool", lambda e, b=b: e.tensor_copy(spb[b], spf[b]), r=[f"spf{b}"], w=[f"spb{b}"])
                        P.op("pool", lambda e, b=b: e.tensor_copy(Rb[b], R), r=["R"], w=[f"Rb{b}"])
                        P.op("pe", lambda e, b=b, pa=pa: e.matmul(ps[pa][:, :], lhsT=ustr, rhs=spb[b], start=True, stop=False),
                             r=["ustr", f"spb{b}"], w=[f"ps{pa}"])
                        P.op("pe", lambda e, b=b, pa=pa: e.matmul(ps[pa][:, :], lhsT=ones_bf, rhs=Rb[b], start=False, stop=True),
                             r=["ones", f"Rb{b}"], w=[f"ps{pa}"])
                        P.op("dve", lambda e, b=b, pz=pz: e.tensor_tensor(
                            out=tb[b], in0=ps[pz][:, :], in1=spf[b], op=ALU.subtract),
                            r=[f"ps{pz}", f"spf{b}"], w=[f"tb{b}"])
                        P.op("dve", lambda e, b=b, pa=pa: e.tensor_tensor(
                            out=tb[b], in0=tb[b], in1=ps[pa][:, :], op=ALU.subtract),
                            r=[f"ps{pa}", f"tb{b}"], w=[f"tb{b}"])
                        P.op("act", lambda e, b=b: e.activation(out=wT[b], in_=tb[b], func=AF.Exp),
                             r=[f"tb{b}"], w=[f"wT{b}"])
                        if diag >= 0:
                            P.op("dve", lambda e, b=b, diag=diag: e.tensor_tensor(
                                out=wT[b], in0=wT[b], in1=mask[:, diag, :], op=ALU.mult),
                                r=[f"wT{b}", "mask"], w=[f"wT{b}"])
                        P.op("pool", lambda e, b=b: e.tensor_tensor(out=R, in0=R, in1=spf[b], op=ALU.add),
                             r=["R", f"spf{b}"], w=["R"])
                        P.op("pe", lambda e, b=b, kb=kb, po=po, nkb=nkb: e.matmul(
                            ps[po][:, :], lhsT=vv[:, kb, :], rhs=wT[b], start=(kb == nkb - 1), stop=(kb == 0)),
                            r=["vv", f"wT{b}"], w=[f"ps{po}"])
                    ob = sq % 2
                    P.op("act", lambda e, ob=ob, po=po: e.activation(out=osb[ob], in_=ps[po][:, :], func=AF.Copy),
                         r=[f"ps{po}"], w=[f"osb{ob}"])
                    P.op("sp", lambda e, ob=ob, hd=hd, q0=q0: e.dma_start(out=o_s[hd, :, q0:q0 + 512], in_=osb[ob]),
                         r=[f"osb{ob}"], w=["o_s"], dma=True)
            P.barrier()
            A.reset()
            wo = A.alloc([128, NCH, D], BF16, "wo")
            for half in range(2):
                P.op("pool", lambda e, half=half: e.dma_start(
                    out=wo[:, half * 4:(half + 1) * 4, :], in_=w_o[:, half * 4:(half + 1) * 4, :]),
                    w=[f"wo{half}"], dma=True)
            oall = [A.alloc([128, SBH, TT], BF16, f"oall{b}") for b in range(2)]
            o_view = o_s.rearrange("h p t -> p h t")
            for T in range(NTT):
                t0 = T * TT
                b = T % 2
                P.op("sp", lambda e, b=b, t0=t0: e.dma_start(out=oall[b], in_=o_view[:, :, t0:t0 + TT]),
                     r=["o_s"], w=[f"oall{b}"], dma=True)
                for d in range(NCH):
                    py = d % 4
                    for h in range(SBH):
                        P.op("pe", lambda e, h=h, d=d, b=b, py=py: e.matmul(
                            ps[py][:, :], lhsT=wo[:, h, d * 128:(d + 1) * 128], rhs=oall[b][:, h, :],
                            start=(h == 0), stop=(h == SBH - 1)),
                            r=["wo0", "wo1", f"oall{b}"], w=[f"ps{py}"])
                    P.op("dve", lambda e, d=d, py=py, t0=t0: e.tensor_tensor(
                        out=xT[:, d, t0:t0 + TT], in0=xT[:, d, t0:t0 + TT], in1=ps[py][:, :], op=ALU.add),
                        r=[f"ps{py}", f"xT{T}"], w=[f"xT{T}"])

        def stage_ret(l):
            RH = 4
            qr_s = nc.dram_tensor("qr_s", [RH, 2, 128, S], BF16, kind="Internal").ap()
            kr_s = nc.dram_tensor("kr_s", [RH, 2, 128, S], BF16, kind="Internal").ap()
            rv_s = nc.dram_tensor("rv_s", [S, 2048], BF16, kind="Internal").ap()
            sg_s = nc.dram_tensor("sg_s", [16, 128, S], F32, kind="Internal").ap()
            ry_s = nc.dram_tensor("ry_s", [16, 128, S], BF16, kind="Internal").ap()
            w_in = G("ret_w_in")[0].rearrange("(kc p) n -> p kc n", p=128)
            w_o = G("ret_w_o")[0].rearrange("(kc p) n -> p kc n", p=128)
            c_cos = G("c_cos")
            c_sin = G("c_sin")
            c_intraT = G("c_ret_intraT")
            c_qdec = G("c_ret_qdec")
            c_kdec = G("c_ret_kdec")
            A.reset()
            rstd = A.alloc([128, TT], F32, "rstd")
            sqb = A.alloc([128, NCH, TT], BF16, "sqb")
            hT = A.alloc([128, NCH, TT], BF16, "hT")
            wq = [A.alloc([128, NCH, 512], BF16, f"wq{b}") for b in range(2)]
            cs = [[A.alloc([128, TT], F32, f"cs{b}{q}") for q in range(2)] for b in range(2)]
            xs = [A.alloc([128, TT], F32, f"xs{q}") for q in range(2)]
            tt_ = [A.alloc([128, TT], F32, f"tt{q}") for q in range(4)]
            rr = [[A.alloc([128, TT], BF16, f"rr{b}{q}") for q in range(2)] for b in range(2)]
            vsb = [A.alloc([128, 512], BF16, f"vsb{b}") for b in range(2)]
            sgb = [A.alloc([128, TT], F32, f"sgb{b}") for b in range(2)]
            nw = 0
            nr = 0
            nv = 0
            ng = 0
            for T in range(NTT):
                t0 = T * TT
                cb = T % 2
                P.op("sp", lambda e, cb=cb, t0=t0: e.dma_start(out=cs[cb][0], in_=c_cos[:, t0:t0 + TT]), w=[f"cs{cb}0"], dma=True)
                P.op("sp", lambda e, cb=cb, t0=t0: e.dma_start(out=cs[cb][1], in_=c_sin[:, t0:t0 + TT]), w=[f"cs{cb}1"], dma=True)
                norm_tile(T, gmix[:, l, :], [sqb[:, c, :] for c in range(NCH)], [f"sq{c}" for c in range(NCH)],
                          rstd, "rstd", [hT[:, c, :] for c in range(NCH)], [f"h{c}" for c in range(NCH)])
                for piece in range(12):
                    wb = nw % 2
                    nw += 1
                    P.op("pool", lambda e, wb=wb, piece=piece: e.dma_start(
                        out=wq[wb], in_=w_in[:, :, piece * 512:(piece + 1) * 512]), w=[f"wq{wb}"], dma=True)
                    if piece < 4:
                        isk = piece // 2
                        for pr in range(2):
                            hd = (piece % 2) * 2 + pr
                            for half in range(2):
                                cc = pr * 2 + half
                                pq = half
                                for kc in range(NCH):
                                    P.op("pe", lambda e, kc=kc, wb=wb, cc=cc, pq=pq: e.matmul(
                                        ps[pq][:, :], lhsT=wq[wb][:, kc, cc * 128:(cc + 1) * 128], rhs=hT[:, kc, :],
                                        start=(kc == 0), stop=(kc == NCH - 1)),
                                        r=[f"wq{wb}", f"h{kc}"], w=[f"ps{pq}"])
                                sc = (1.0 / 16.0) if isk else 1.0
                                P.op("act", lambda e, half=half, pq=pq, sc=sc: e.activation(
                                    out=xs[half], in_=ps[pq][:, :], func=AF.Copy, scale=sc),
                                    r=[f"ps{pq}"], w=[f"xs{half}"])
                            b = nr % 2
                            nr += 1
                            cosT, sinT = cs[cb][0], cs[cb][1]
                            P.op("dve", lambda e, cosT=cosT: e.tensor_tensor(out=tt_[0], in0=xs[0], in1=cosT, op=ALU.mult),
                                 r=["xs0", f"cs{cb}0"], w=["tt0"])
                            P.op("pool", lambda e, sinT=sinT: e.tensor_tensor(out=tt_[1], in0=xs[1], in1=sinT, op=ALU.mult),
                                 r=["xs1", f"cs{cb}1"], w=["tt1"])
                            P.op("dve", lambda e, sinT=sinT: e.tensor_tensor(out=tt_[2], in0=xs[0], in1=sinT, op=ALU.mult),
                                 r=["xs0", f"cs{cb}1"], w=["tt2"])
                            P.op("pool", lambda e, cosT=cosT: e.tensor_tensor(out=tt_[3], in0=xs[1], in1=cosT, op=ALU.mult),
                                 r=["xs1", f"cs{cb}0"], w=["tt3"])
                            P.op("dve", lambda e, b=b: e.tensor_tensor(out=rr[b][0], in0=tt_[0], in1=tt_[1], op=ALU.subtract),
                                 r=["tt0", "tt1"], w=[f"rr{b}0"])
                            P.op("dve", lambda e, b=b: e.tensor_tensor(out=rr[b][1], in0=tt_[2], in1=tt_[3], op=ALU.add),
                                 r=["tt2", "tt3"], w=[f"rr{b}1"])
                            dsts = (kr_s if isk else qr_s)
                            for half in range(2):
                                P.op("sp", lambda e, b=b, half=half, hd=hd, dsts=dsts, t0=t0: e.dma_start(
                                    out=dsts[hd, half, :, t0:t0 + TT], in_=rr[b][half]),
                                    r=[f"rr{b}{half}"], w=["kr_s" if isk else "qr_s"], dma=True)
                    elif piece < 8:
                        hd = piece - 4
                        for s_ in range(4):
                            pv = 4 + s_ % 2
                            b = nv % 2
                            nv += 1
                            for kc in range(NCH):
                                P.op("pe", lambda e, kc=kc, wb=wb, s_=s_, pv=pv: e.matmul(
                                    ps[pv][:, :], lhsT=hT[:, kc, s_ * 128:(s_ + 1) * 128], rhs=wq[wb][:, kc, :],
                                    start=(kc == 0), stop=(kc == NCH - 1)),
                                    r=[f"wq{wb}", f"h{kc}"], w=[f"ps{pv}"])
                            P.op("act", lambda e, b=b, pv=pv: e.activation(out=vsb[b], in_=ps[pv][:, :], func=AF.Copy),
                                 r=[f"ps{pv}"], w=[f"vsb{b}"])
                            r0 = t0 + s_ * 128
                            P.op("sp", lambda e, b=b, r0=r0, hd=hd: e.dma_start(
                                out=rv_s[r0:r0 + 128, hd * 512:(hd + 1) * 512], in_=vsb[b]),
                                r=[f"vsb{b}"], w=["rv_s"], dma=True)
                    else:
                        for cc in range(4):
                            gc = (piece - 8) * 4 + cc
                            pg = 2 + cc % 2
                            b = ng % 2
                            ng += 1
                            for kc in range(NCH):
                                P.op("pe", lambda e, kc=kc, wb=wb, cc=cc, pg=pg: e.matmul(
                                    ps[pg][:, :], lhsT=wq[wb][:, kc, cc * 128:(cc + 1) * 128], rhs=hT[:, kc, :],
                                    start=(kc == 0), stop=(kc == NCH - 1)),
                                    r=[f"wq{wb}", f"h{kc}"], w=[f"ps{pg}"])
                            P.op("act", lambda e, b=b, pg=pg: e.activation(out=sgb[b], in_=ps[pg][:, :], func=AF.Silu),
                                 r=[f"ps{pg}"], w=[f"sgb{b}"])
                            P.op("sp", lambda e, b=b, gc=gc, t0=t0: e.dma_start(out=sg_s[gc, :, t0:t0 + TT], in_=sgb[b]),
                                 r=[f"sgb{b}"], w=["sg_s"], dma=True)
            P.barrier()
            A.reset()
            identb = A.alloc([128, 128], BF16, "identb")
            P.op("dve", lambda e: e.tensor_copy(identb, ident), r=["ident"], w=["identb"])
            gng = A.alloc([128, 4], F32, "gng")
            load_vecT(G("ret_gn_gain").rearrange("l (c p) -> (l c) p", p=128), 4, gng, "gng", float(np.sqrt(512.0)))
            eps512 = A.alloc([128, 1], F32, "eps512")
            P.op("dve", lambda e: e.memset(eps512, EPS * 512), w=["eps512"])
            intraT = A.alloc([128, 128], F32, "intraT")
            qdec = A.alloc([128, 128], F32, "qdec")
            kdec = A.alloc([128, 4], F32, "kdec")
            P.op("sp", lambda e: e.dma_start(out=kdec, in_=c_kdec[:, :]), w=["kdec"], dma=True)
            Sst = A.alloc([128, 2, 512], F32, "Sst")
            Sbf = A.alloc([128, 2, 512], BF16, "Sbf")
            qg_ = [A.alloc([128, 2, 512], BF16, f"qg{b}") for b in range(2)]
            kg_ = [A.alloc([128, 2, 512], BF16, f"kg{b}") for b in range(2)]
            vg_ = [A.alloc([128, 4, 512], BF16, f"vg{b}") for b in range(2)]
            sgg = [A.alloc([128, 4, 512], F32, f"sgg{b}") for b in range(2)]
            ybuf = [A.alloc([128, 4, 512], BF16, f"ybuf{b}") for b in range(2)]
            qd = [A.alloc([128, 2, 128], BF16, f"qd{b}") for b in range(2)]
            ktd = [A.alloc([128, 256], BF16, f"ktd{b}") for b in range(2)]
            scb = [A.alloc([128, 128], BF16, f"scb{b}") for b in range(2)]
            osq = [A.alloc([128, 4, 128], BF16, f"osq{b}") for b in range(2)]
            rr2 = [A.alloc([128, 128], F32, f"rr2{b}") for b in range(2)]
            ytmp = [A.alloc([128, 128], F32, f"ytmp{b}") for b in range(2)]
            rv_view = rv_s.rearrange("(n p) (h e) -> p h n e", p=128, h=RH)
            sg_view = sg_s.rearrange("k p t -> p k t")
            ry_view = ry_s.rearrange("k p t -> p k t")
            logg = [float(np.log1p(-2.0 ** (-5.0 - h))) for h in range(RH)]
            ps_bf = [ps[i][:, :].bitcast(BF16) for i in range(8)]
            ci = 0
            gi = 0
            for hd in range(RH):
                cd = float(np.exp(128.0 * logg[hd]))
                P.op("sp", lambda e, hd=hd: e.dma_start(out=intraT, in_=c_intraT[hd]), w=["intraT"], dma=True)
                P.op("sp", lambda e, hd=hd: e.dma_start(out=qdec, in_=c_qdec[hd]), w=["qdec"], dma=True)
                P.op("dve", lambda e: e.memset(Sst, 0.0), w=["Sst0", "Sst1"])
                P.op("dve", lambda e: e.memset(Sbf, 0.0), w=["Sbf0", "Sbf1"])
                for grp in range(NTT):
                    t0 = grp * TT
                    gb = gi % 2
                    gi += 1
                    P.op("sp", lambda e, gb=gb, hd=hd, t0=t0: e.dma_start(
                        out=qg_[gb], in_=qr_s[hd].rearrange("h p t -> p h t")[:, :, t0:t0 + TT]),
                        r=["qr_s"], w=[f"qg{gb}"], dma=True)
                    P.op("sp", lambda e, gb=gb, hd=hd, t0=t0: e.dma_start(
                        out=kg_[gb], in_=kr_s[hd].rearrange("h p t -> p h t")[:, :, t0:t0 + TT]),
                        r=["kr_s"], w=[f"kg{gb}"], dma=True)
                    P.op("sp", lambda e, gb=gb, hd=hd, grp=grp: e.dma_start(
                        out=vg_[gb], in_=rv_view[:, hd, grp * 4:(grp + 1) * 4, :]),
                        r=["rv_s"], w=[f"vg{gb}"], dma=True)
                    P.op("sp", lambda e, gb=gb, hd=hd, t0=t0: e.dma_start(
                        out=sgg[gb], in_=sg_view[:, hd * 4:(hd + 1) * 4, t0:t0 + TT]),
                        r=["sg_s"], w=[f"sgg{gb}"], dma=True)
                    for n in range(4):
                        b = ci % 2
                        ci += 1
                        c0 = n * 128
                        for dh in range(2):
                            P.op("dve", lambda e, b=b, dh=dh, gb=gb, c0=c0: e.tensor_tensor(
                                out=qd[b][:, dh, :], in0=qg_[gb][:, dh, c0:c0 + 128], in1=qdec, op=ALU.mult),
                                r=[f"qg{gb}", "qdec"], w=[f"qd{b}"])
                        for dh in range(2):
                            P.op("pe", lambda e, dh=dh, gb=gb, c0=c0: e.transpose(
                                ps_bf[6][:, dh * 128:(dh + 1) * 128], kg_[gb][:, dh, c0:c0 + 128], identb),
                                r=[f"kg{gb}", "identb"], w=["ps6"])
                        P.op("act", lambda e, b=b, hd=hd: e.activation(
                            out=ktd[b], in_=ps_bf[6][:, 0:256], func=AF.Copy, scale=kdec[:, hd:hd + 1]),
                            r=["ps6", "kdec"], w=[f"ktd{b}"])
                        for dh in range(2):
                            P.op("pe", lambda e, dh=dh, gb=gb, c0=c0: e.matmul(
                                ps[7][:, 0:128], lhsT=kg_[gb][:, dh, c0:c0 + 128], rhs=qg_[gb][:, dh, c0:c0 + 128],
                                start=(dh == 0), stop=(dh == 1)),
                                r=[f"kg{gb}", f"qg{gb}"], w=["ps7"])
                        P.op("dve", lambda e, b=b: e.tensor_tensor(out=scb[b], in0=ps[7][:, 0:128], in1=intraT, op=ALU.mult),
                             r=["ps7", "intraT"], w=[f"scb{b}"])
                        po = b
                        for ec in range(4):
                            P.op("pe", lambda e, ec=ec, gb=gb, n=n, b=b, po=po: e.matmul(
                                ps[po][:, ec * 128:(ec + 1) * 128], lhsT=vg_[gb][:, n, ec * 128:(ec + 1) * 128], rhs=scb[b],
                                start=True, stop=False),
                                r=[f"vg{gb}", f"scb{b}"], w=[f"ps{po}"])
                            for dh in range(2):
                                P.op("pe", lambda e, ec=ec, dh=dh, b=b, po=po: e.matmul(
                                    ps[po][:, ec * 128:(ec + 1) * 128], lhsT=Sbf[:, dh, ec * 128:(ec + 1) * 128],
                                    rhs=qd[b][:, dh, :], start=False, stop=(dh == 1)),
                                    r=[f"Sbf{dh}", f"qd{b}"], w=[f"ps{po}"])
                        for dh in range(2):
                            pS = 2 + dh
                            P.op("pe", lambda e, dh=dh, b=b, gb=gb, n=n, pS=pS: e.matmul(
                                ps[pS][:, :], lhsT=ktd[b][:, dh * 128:(dh + 1) * 128], rhs=vg_[gb][:, n, :],
                                start=True, stop=True),
                                r=[f"ktd{b}", f"vg{gb}"], w=[f"ps{pS}"])
                            P.op("dve", lambda e, dh=dh, pS=pS, cd=cd: e.scalar_tensor_tensor(
                                out=Sst[:, dh, :], in0=Sst[:, dh, :], scalar=cd, in1=ps[pS][:, :],
                                op0=ALU.mult, op1=ALU.add),
                                r=[f"Sst{dh}", f"ps{pS}"], w=[f"Sst{dh}"])
                            P.op("act", lambda e, dh=dh: e.activation(out=Sbf[:, dh, :], in_=Sst[:, dh, :], func=AF.Copy),
                                 r=[f"Sst{dh}"], w=[f"Sbf{dh}"])
                        P.op("act", lambda e, b=b, po=po: e.activation(
                            out=osq[b].rearrange("p a b -> p (a b)"), in_=ps[po][:, :], func=AF.Square),
                            r=[f"ps{po}"], w=[f"osq{b}"])
                        pn = 4 + b
                        for ec in range(4):
                            P.op("pe", lambda e, ec=ec, b=b, pn=pn: e.matmul(
                                ps[pn][:, 0:128], lhsT=ones_bf, rhs=osq[b][:, ec, :], start=(ec == 0), stop=(ec == 3)),
                                r=[f"osq{b}", "ones"], w=[f"ps{pn}"])
                        P.op("act", lambda e, b=b, pn=pn: e.activation(out=rr2[b], in_=ps[pn][:, 0:128], func=AF.Sqrt,
                                                                      bias=eps512, scale=1.0),
                             r=[f"ps{pn}", "eps512"], w=[f"rr2{b}"])
                        P.op("dve", lambda e, b=b: e.reciprocal(rr2[b], rr2[b]), r=[f"rr2{b}"], w=[f"rr2{b}"])
                        for ec in range(4):
                            P.op("dve", lambda e, ec=ec, b=b, po=po: e.scalar_tensor_tensor(
                                out=ytmp[b], in0=ps[po][:, ec * 128:(ec + 1) * 128], scalar=gng[:, ec:ec + 1], in1=rr2[b],
                                op0=ALU.mult, op1=ALU.mult),
                                r=[f"ps{po}", f"rr2{b}", "gng"], w=[f"ytmp{b}"])
                            P.op("dve", lambda e, ec=ec, b=b, gb=gb, c0=c0: e.tensor_tensor(
                                out=ybuf[gb][:, ec, c0:c0 + 128], in0=ytmp[b], in1=sgg[gb][:, ec, c0:c0 + 128], op=ALU.mult),
                                r=[f"ytmp{b}", f"sgg{gb}"], w=[f"ybuf{gb}"])
                    P.op("sp", lambda e, gb=gb, hd=hd, t0=t0: e.dma_start(
                        out=ry_view[:, hd * 4:(hd + 1) * 4, t0:t0 + TT], in_=ybuf[gb]),
                        r=[f"ybuf{gb}"], w=["ry_s"], dma=True)
            P.barrier()
            A.reset()
            wo = A.alloc([128, 16, D], BF16, "wo")
            for q in range(4):
                P.op("pool", lambda e, q=q: e.dma_start(
                    out=wo[:, q * 4:(q + 1) * 4, :], in_=w_o[:, q * 4:(q + 1) * 4, :]), w=[f"wo{q}"], dma=True)
            yall = [A.alloc([128, 16, TT], BF16, f"yall{b}") for b in range(2)]
            for T in range(NTT):
                t0 = T * TT
                b = T % 2
                P.op("sp", lambda e, b=b, t0=t0: e.dma_start(out=yall[b], in_=ry_view[:, :, t0:t0 + TT]),
                     r=["ry_s"], w=[f"yall{b}"], dma=True)
                for d in range(NCH):
                    py = d % 4
                    for k in range(16):
                        P.op("pe", lambda e, k=k, d=d, b=b, py=py: e.matmul(
                            ps[py][:, :], lhsT=wo[:, k, d * 128:(d + 1) * 128], rhs=yall[b][:, k, :],
                            start=(k == 0), stop=(k == 15)),
                            r=["wo0", "wo1", "wo2", "wo3", f"yall{b}"], w=[f"ps{py}"])
                    P.op("dve", lambda e, d=d, py=py, t0=t0: e.tensor_tensor(
                        out=xT[:, d, t0:t0 + TT], in0=xT[:, d, t0:t0 + TT], in1=ps[py][:, :], op=ALU.add),
                        r=[f"ps{py}", f"xT{T}"], w=[f"xT{T}"])

        def stage_gdn(l):
            GH = 8
            gq_s = nc.dram_tensor("gq_s", [GH, 128, S], BF16, kind="Internal").ap()
            gk_s = nc.dram_tensor("gk_s", [GH, 128, S], BF16, kind="Internal").ap()
            gv_s = nc.dram_tensor("gv_s", [16, 128, S], BF16, kind="Internal").ap()
            gz_s = nc.dram_tensor("gz_s", [S, 2048], F32, kind="Internal").ap()
            gg_s = nc.dram_tensor("gg_s", [S, 8], F32, kind="Internal").ap()
            gb_s = nc.dram_tensor("gb_s", [S, 8], F32, kind="Internal").ap()
            gy_s = nc.dram_tensor("gy_s", [16, 128, S], BF16, kind="Internal").ap()
            DBG["gg_s"] = gg_s
            DBG["gb_s"] = gb_s
            DBG["gz_s"] = gz_s
            w_in = G("gdn_w_in")[0].rearrange("(kc p) n -> p kc n", p=128)
            w_o = G("gdn_w_o")[0].rearrange("(kc p) n -> p kc n", p=128)
            c_mS = G("c_gdn_maskS")
            c_mIT = G("c_gdn_maskIT")
            c_half = G("c_gdn_half")
            c_sel = G("c_gdn_sel")
            a_log_d = G("gdn_a_log")
            dtb_d = G("gdn_dt_bias")
            ng_d = G("gdn_norm_gain")
            A.reset()
            cw = A.alloc([128, 128], F32, "cw")
            load_vecT(G("gdn_conv_w")[0].rearrange("k (c p) -> (k c) p", p=128), 128, cw, "cw")
            nea = A.alloc([128, 8], F32, "nea")
            dtb = A.alloc([128, 8], F32, "dtb")
            bcast_row(a_log_d[0:1, :], 8, nea, "nea")
            bcast_row(dtb_d[0:1, :], 8, dtb, "dtb")
            P.op("act", lambda e: e.activation(out=nea, in_=nea, func=AF.Exp), r=["nea"], w=["nea"])
            P.op("dve", lambda e: e.tensor_scalar_mul(nea, nea, -1.0), r=["nea"], w=["nea"])
            epsc = A.alloc([128, 1], F32, "epsc")
            P.op("dve", lambda e: e.memset(epsc, EPS), w=["epsc"])
            rstd = A.alloc([128, TT], F32, "rstd")
            sqb = A.alloc([128, NCH, TT], BF16, "sqb")
            hT = A.alloc([128, NCH, TT], BF16, "hT")
            wq = [A.alloc([128, NCH, 512], BF16, f"wq{b}") for b in range(2)]
            wab = A.alloc([128, NCH, 16], BF16, "wab")
            P.op("pool", lambda e: e.dma_start(out=wab, in_=w_in[:, :, 6144:6160]), w=["wab"], dma=True)
            halo = A.alloc([128, 32, 3], F32, "halo")
            P.op("dve", lambda e: e.memset(halo, 0.0), w=["halo"])
            xc = [A.alloc([128, 3 + TT], F32, f"xc{b}") for b in range(2)]
            acc = [A.alloc([128, TT], F32, f"acc{b}") for b in range(2)]
            sl = [A.alloc([128, TT], F32, f"sl{b}") for b in range(2)]
            sqh = [A.alloc([128, TT], BF16, f"sqh{b}") for b in range(2)]
            r2 = [A.alloc([128, TT], F32, f"r2{b}") for b in range(2)]
            qn = [A.alloc([128, TT], BF16, f"qn{b}") for b in range(2)]
            zsb = [A.alloc([128, 512], F32, f"zsb{b}") for b in range(2)]
            ab = [A.alloc([128, 16], F32, f"ab{b}") for b in range(2)]
            gt = [A.alloc([128, 8], F32, f"gt{b}") for b in range(2)]
            bt_ = [A.alloc([128, 8], F32, f"bt{b}") for b in range(2)]
            nw = 0
            nch_ = 0
            nz = 0
            nab = 0
            for T in range(NTT):
                t0 = T * TT
                norm_tile(T, gmix[:, l, :], [sqb[:, c, :] for c in range(NCH)], [f"sq{c}" for c in range(NCH)],
                          rstd, "rstd", [hT[:, c, :] for c in range(NCH)], [f"h{c}" for c in range(NCH)])
                for s_ in range(4):
                    b = nab % 2
                    nab += 1
                    for kc in range(NCH):
                        P.op("pe", lambda e, kc=kc, s_=s_: e.matmul(
                            ps[6][:, 0:16], lhsT=hT[:, kc, s_ * 128:(s_ + 1) * 128], rhs=wab[:, kc, :],
                            start=(kc == 0), stop=(kc == NCH - 1)), r=["wab", f"h{kc}"], w=["ps6"])
                    P.op("dve", lambda e, b=b: e.tensor_tensor(out=ab[b][:, 0:8], in0=ps[6][:, 0:8], in1=dtb, op=ALU.add),
                         r=["ps6", "dtb"], w=[f"ab{b}"])
                    P.op("act", lambda e, b=b: e.activation(out=ab[b][:, 8:16], in_=ps[6][:, 8:16], func=AF.Exp, scale=-1.0),
                         r=["ps6", f"ab{b}"], w=[f"ab{b}"])
                    P.op("act", lambda e, b=b: e.activation(out=ab[b][:, 0:8], in_=ab[b][:, 0:8], func=AF.Exp),
                         r=[f"ab{b}"], w=[f"ab{b}"])
                    P.op("act", lambda e, b=b: e.activation(out=ab[b][:, 0:8], in_=ab[b][:, 0:8], func=AF.Ln, bias=1.0, scale=1.0),
                         r=[f"ab{b}"], w=[f"ab{b}"])
                    P.op("dve", lambda e, b=b: e.tensor_tensor(out=gt[b], in0=ab[b][:, 0:8], in1=nea, op=ALU.mult),
                         r=[f"ab{b}", "nea"], w=[f"gt{b}"])
                    P.op("dve", lambda e, b=b: e.tensor_scalar_add(bt_[b], ab[b][:, 8:16], 1.0), r=[f"ab{b}"], w=[f"bt{b}"])
                    P.op("dve", lambda e, b=b: e.reciprocal(bt_[b], bt_[b]), r=[f"bt{b}"], w=[f"bt{b}"])
                    r0 = t0 + s_ * 128
                    P.op("sp", lambda e, b=b, r0=r0: e.dma_start(out=gg_s[r0:r0 + 128, :], in_=gt[b]), r=[f"gt{b}"], w=["gg_s"], dma=True)
                    P.op("sp", lambda e, b=b, r0=r0: e.dma_start(out=gb_s[r0:r0 + 128, :], in_=bt_[b]), r=[f"bt{b}"], w=["gb_s"], dma=True)
                for piece in range(12):
                    wb = nw % 2
                    nw += 1
                    P.op("pool", lambda e, wb=wb, piece=piece: e.dma_start(
                        out=wq[wb], in_=w_in[:, :, piece * 512:(piece + 1) * 512]), w=[f"wq{wb}"], dma=True)
                    if piece < 8:
                        for cc in range(4):
                            ch = piece * 4 + cc
                            b = nch_ % 2
                            nch_ += 1
                            pq = cc % 2
                            for kc in range(NCH):
                                P.op("pe", lambda e, kc=kc, wb=wb, cc=cc, pq=pq: e.matmul(
                                    ps[pq][:, :], lhsT=wq[wb][:, kc, cc * 128:(cc + 1) * 128], rhs=hT[:, kc, :],
                                    start=(kc == 0), stop=(kc == NCH - 1)),
                                    r=[f"wq{wb}", f"h{kc}"], w=[f"ps{pq}"])
                            P.op("act", lambda e, b=b, pq=pq: e.activation(out=xc[b][:, 3:3 + TT], in_=ps[pq][:, :], func=AF.Copy),
                                 r=[f"ps{pq}"], w=[f"xc{b}"])
                            P.op("pool", lambda e, b=b, ch=ch: e.tensor_copy(xc[b][:, 0:3], halo[:, ch, :]),
                                 r=["halo", f"xc{b}"], w=[f"xc{b}"])
                            P.op("pool", lambda e, b=b, ch=ch: e.tensor_copy(halo[:, ch, :], xc[b][:, TT:TT + 3]),
                                 r=[f"xc{b}"], w=["halo"])
                            P.op("dve", lambda e, b=b, ch=ch: e.tensor_scalar(
                                acc[b], xc[b][:, 3:3 + TT], cw[:, 96 + ch:97 + ch], None, ALU.mult),
                                r=[f"xc{b}", "cw"], w=[f"acc{b}"])
                            for tap in (2, 1, 0):
                                sh = 3 - tap
                                P.op("dve", lambda e, b=b, ch=ch, tap=tap, sh=sh: e.scalar_tensor_tensor(
                                    out=acc[b], in0=xc[b][:, 3 - sh:3 - sh + TT], scalar=cw[:, tap * 32 + ch:tap * 32 + ch + 1],
                                    in1=acc[b], op0=ALU.mult, op1=ALU.add),
                                    r=[f"xc{b}", "cw", f"acc{b}"], w=[f"acc{b}"])
                            if ch < 16:
                                P.op("act", lambda e, b=b: e.activation(out=sl[b], in_=acc[b], func=AF.Silu),
                                     r=[f"acc{b}"], w=[f"sl{b}"])
                                P.op("act", lambda e, b=b: e.activation(out=sqh[b], in_=sl[b], func=AF.Square),
                                     r=[f"sl{b}"], w=[f"sqh{b}"])
                                pss = 2 + cc % 2
                                P.op("pe", lambda e, b=b, pss=pss: e.matmul(ps[pss][:, :], lhsT=ones_bf, rhs=sqh[b],
                                                                           start=True, stop=True),
                                     r=[f"sqh{b}", "ones"], w=[f"ps{pss}"])
                                P.op("act", lambda e, b=b, pss=pss: e.activation(out=r2[b], in_=ps[pss][:, :], func=AF.Sqrt,
                                                                                bias=epsc, scale=1.0),
                                     r=[f"ps{pss}", "epsc"], w=[f"r2{b}"])
                                P.op("dve", lambda e, b=b: e.reciprocal(r2[b], r2[b]), r=[f"r2{b}"], w=[f"r2{b}"])
                                sc = (128.0 ** -0.5) if ch < 8 else 1.0
                                P.op("dve", lambda e, b=b, sc=sc: e.scalar_tensor_tensor(
                                    out=qn[b], in0=sl[b], scalar=sc, in1=r2[b], op0=ALU.mult, op1=ALU.mult),
                                    r=[f"sl{b}", f"r2{b}"], w=[f"qn{b}"])
                                dst = (gq_s if ch < 8 else gk_s)[ch % 8, :, t0:t0 + TT]
                                P.op("sp", lambda e, b=b, dst=dst: e.dma_start(out=dst, in_=qn[b]),
                                     r=[f"qn{b}"], w=["gq_s" if ch < 8 else "gk_s"], dma=True)
                            else:
                                P.op("act", lambda e, b=b: e.activation(out=qn[b], in_=acc[b], func=AF.Silu),
                                     r=[f"acc{b}"], w=[f"qn{b}"])
                                P.op("sp", lambda e, b=b, ch=ch, t0=t0: e.dma_start(out=gv_s[ch - 16, :, t0:t0 + TT], in_=qn[b]),
                                     r=[f"qn{b}"], w=["gv_s"], dma=True)
                    else:
                        for s_ in range(4):
                            pv = 4 + s_ % 2
                            b = nz % 2
                            nz += 1
                            for kc in range(NCH):
                                P.op("pe", lambda e, kc=kc, wb=wb, s_=s_, pv=pv: e.matmul(
                                    ps[pv][:, :], lhsT=hT[:, kc, s_ * 128:(s_ + 1) * 128], rhs=wq[wb][:, kc, :],
                                    start=(kc == 0), stop=(kc == NCH - 1)),
                                    r=[f"wq{wb}", f"h{kc}"], w=[f"ps{pv}"])
                            P.op("act", lambda e, b=b, pv=pv: e.activation(out=zsb[b], in_=ps[pv][:, :], func=AF.Silu),
                                 r=[f"ps{pv}"], w=[f"zsb{b}"])
                            r0 = t0 + s_ * 128
                            c0 = (piece - 8) * 512
                            P.op("sp", lambda e, b=b, r0=r0, c0=c0: e.dma_start(out=gz_s[r0:r0 + 128, c0:c0 + 512], in_=zsb[b]),
                                 r=[f"zsb{b}"], w=["gz_s"], dma=True)
            if GDN_PHASES < 2:
                return
            P.barrier()
            A.reset()
            identb = A.alloc([128, 128], BF16, "identb")
            P.op("dve", lambda e: e.tensor_copy(identb, ident), r=["ident"], w=["identb"])
            halfm = A.alloc([128, 2], F32, "halfm")
            P.op("sp", lambda e: e.dma_start(out=halfm, in_=c_half[:, :]), w=["halfm"], dma=True)
            mS = A.alloc([128, 128], F32, "mS")
            mIT = A.alloc([128, 128], F32, "mIT")
            P.op("sp", lambda e: e.dma_start(out=mS, in_=c_mS[:, :]), w=["mS"], dma=True)
            P.op("sp", lambda e: e.dma_start(out=mIT, in_=c_mIT[:, :]), w=["mIT"], dma=True)
            mITb = A.alloc([128, 128], BF16, "mITb")
            P.op("dve", lambda e: e.tensor_copy(mITb, mIT), r=["mIT"], w=["mITb"])
            ngb = A.alloc([128, 256], F32, "ngb")
            bcast_row(ng_d[0:1, 0:128], 128, ngb[:, 0:128], "ngb")
            bcast_row(ng_d[0:1, 128:256], 128, ngb[:, 128:256], "ngb")
            P.op("dve", lambda e: e.tensor_scalar_mul(ngb, ngb, 16.0), r=["ngb"], w=["ngb"])
            eps256 = A.alloc([128, 1], F32, "eps256")
            P.op("dve", lambda e: e.memset(eps256, EPS * 256), w=["eps256"])
            Sst = A.alloc([128, GH, 256], F32, "Sst")
            Sbf = A.alloc([128, GH, 256], BF16, "Sbf")
            P.op("dve", lambda e: e.memset(Sst, 0.0), w=[f"Sst{h}" for h in range(GH)])
            P.op("dve", lambda e: e.memset(Sbf, 0.0), w=[f"Sbf{h}" for h in range(GH)])
            qa = [A.alloc([128, GH, 128], BF16, f"qa{b}") for b in range(2)]
            ka = [A.alloc([128, GH, 128], BF16, f"ka{b}") for b in range(2)]
            va = [A.alloc([128, 16, 128], BF16, f"va{b}") for b in range(2)]
            za = [A.alloc([128, 2048], F32, f"za{b}") for b in range(2)]
            gtl = [A.alloc([128, 8], F32, f"gtl{b}") for b in range(2)]
            btl = [A.alloc([128, 8], F32, f"btl{b}") for b in range(2)]
            ya = [A.alloc([128, 16, 128], BF16, f"ya{b}") for b in range(2)]
            Gcol = A.alloc([128, 8], F32, "Gcol")
            negG = A.alloc([128, 8], F32, "negG")
            eG = A.alloc([128, 8], F32, "eG")
            beG = A.alloc([128, 8], F32, "beG")
            negb = A.alloc([128, 8], F32, "negb")
            ghb = A.alloc([128, 8], BF16, "ghb")
            glb = A.alloc([128, 8], BF16, "glb")
            Ghb = A.alloc([128, 8], BF16, "Ghb")
            Ghf = A.alloc([128, 8], F32, "Ghf")
            Glf = A.alloc([128, 8], F32, "Glf")
            diagGh = A.alloc([128, 128], BF16, "diagGh")
            diagGl = A.alloc([128, 128], BF16, "diagGl")
            Ptb = A.alloc([128, 128], BF16, "Ptb")
            dd = A.alloc([128, 128], F32, "dd")
            GrowS = A.alloc([128, 128], F32, "GrowS")
            eGh = A.alloc([128, 8], BF16, "eGh")
            eGlo = A.alloc([128, 8], BF16, "eGlo")
            eGL = A.alloc([128, 16], F32, "eGL")
            sel = A.alloc([128, 2, 128], BF16, "sel")
            P.op("pool", lambda e: e.dma_start(out=sel, in_=c_sel[:, :, :]), w=["sel"], dma=True)
            EG = A.alloc([128, 128], F32, "EG")
            osbA = A.alloc([128, 256], F32, "osbA")
            osbB = A.alloc([128, 256], F32, "osbB")
            ssq8 = A.alloc([128, 8], F32, "ssq8")
            r8 = A.alloc([128, 8], F32, "r8")
            P.op("dve", lambda e: e.memset(ssq8, 1.0), w=["ssq8"])
            Ds = A.alloc([128, 128], F32, "Ds")
            DiT = A.alloc([128, 128], F32, "DiT")
            eGl = A.alloc([128, 2], F32, "eGl")
            kd2 = A.alloc([128, 2], F32, "kd2")
            eGm = A.alloc([128, 2], F32, "eGm")
            kdec = [A.alloc([128, 128], BF16, f"kdec{j}") for j in range(2)]
            vnj = [A.alloc([128, 256], BF16, f"vnj{j}") for j in range(2)]
            kbg = A.alloc([128, 128], BF16, "kbg")
            vb = A.alloc([128, 256], BF16, "vb")
            Xm = [A.alloc([128, 128], BF16, f"Xm{b}") for b in range(2)]
            Xt = [A.alloc([128, 128], BF16, f"Xt{b}") for b in range(2)]
            Pt = A.alloc([128, 128], F32, "Pt")
            usb = A.alloc([128, 256], F32, "usb")
            wT = A.alloc([128, 128], BF16, "wT")
            attnT = A.alloc([128, 128], BF16, "attnT")
            vnew = A.alloc([128, 256], BF16, "vnew")
            osb = A.alloc([128, 256], F32, "osb")
            osq = A.alloc([128, 256], F32, "osq")
            ssc = A.alloc([128, 1], F32, "ssc")
            ysb = A.alloc([128, 256], BF16, "ysb")
            DBG["sb:GrowS"] = GrowS
            DBG["sb:Gcol"] = Gcol
            DBG["sb:dd"] = dd
            DBG["sb:DiT"] = DiT
            DBG["sb:Ds"] = Ds
            DBG["sb:gtl"] = gtl[1]
            P.op("dve", lambda e: e.memset(vnew, 0.0), w=["vnew"])
            ps_bf = [ps[i][:, :].bitcast(BF16) for i in range(8)]
            gq_v = gq_s.rearrange("h p t -> p h t")
            gk_v = gk_s.rearrange("h p t -> p h t")
            gv_v = gv_s.rearrange("k p t -> p k t")
            gy_v = gy_s.rearrange("k p t -> p k t")
            for ti in range(S // 128):
                c0 = ti * 128
                lb = ti % 2
                P.op("sp", lambda e, lb=lb, c0=c0: e.dma_start(out=qa[lb], in_=gq_v[:, :, c0:c0 + 128]), r=["gq_s"], w=[f"qa{lb}"], dma=True)
                P.op("sp", lambda e, lb=lb, c0=c0: e.dma_start(out=ka[lb], in_=gk_v[:, :, c0:c0 + 128]), r=["gk_s"], w=[f"ka{lb}"], dma=True)
                P.op("sp", lambda e, lb=lb, c0=c0: e.dma_start(out=va[lb], in_=gv_v[:, :, c0:c0 + 128]), r=["gv_s"], w=[f"va{lb}"], dma=True)
                P.op("sp", lambda e, lb=lb, c0=c0: e.dma_start(out=za[lb], in_=gz_s[c0:c0 + 128, :]), r=["gz_s"], w=[f"za{lb}"], dma=True)
                P.op("sp", lambda e, lb=lb, c0=c0: e.dma_start(out=gtl[lb], in_=gg_s[c0:c0 + 128, :]), r=["gg_s"], w=[f"gtl{lb}"], dma=True)
                P.op("sp", lambda e, lb=lb, c0=c0: e.dma_start(out=btl[lb], in_=gb_s[c0:c0 + 128, :]), r=["gb_s"], w=[f"btl{lb}"], dma=True)
                P.op("dve", lambda e, lb=lb: e.tensor_copy(ghb, gtl[lb]), r=[f"gtl{lb}"], w=["ghb"])
                P.op("dve", lambda e, lb=lb: e.tensor_tensor(out=glb, in0=gtl[lb], in1=ghb, op=ALU.subtract),
                     r=[f"gtl{lb}", "ghb"], w=["glb"])
                P.op("pe", lambda e: e.matmul(ps[7][:, 0:8], lhsT=mITb, rhs=ghb, start=True, stop=False),
                     r=["mITb", "ghb"], w=["ps7"])
                P.op("pe", lambda e: e.matmul(ps[7][:, 0:8], lhsT=mITb, rhs=glb, start=False, stop=True),
                     r=["mITb", "glb"], w=["ps7"])
                P.op("dve", lambda e: e.tensor_copy(Gcol, ps[7][:, 0:8]), r=["ps7"], w=["Gcol"])
                P.op("dve", lambda e: e.tensor_copy(Ghb, Gcol), r=["Gcol"], w=["Ghb"])
                P.op("dve", lambda e: e.tensor_copy(Ghf, Ghb), r=["Ghb"], w=["Ghf"])
                P.op("dve", lambda e: e.tensor_tensor(out=Glf, in0=Gcol, in1=Ghf, op=ALU.subtract), r=["Gcol", "Ghf"], w=["Glf"])
                P.op("dve", lambda e: e.tensor_scalar_mul(negG, Gcol, -1.0), r=["Gcol"], w=["negG"])
                P.op("act", lambda e: e.activation(out=eG, in_=Gcol, func=AF.Exp), r=["Gcol"], w=["eG"])
                P.op("dve", lambda e, lb=lb: e.tensor_tensor(out=beG, in0=eG, in1=btl[lb], op=ALU.mult), r=["eG", f"btl{lb}"], w=["beG"])
                P.op("dve", lambda e: e.tensor_copy(eGh, eG), r=["eG"], w=["eGh"])
                P.op("dve", lambda e: e.tensor_tensor(out=eGlo, in0=eG, in1=eGh, op=ALU.subtract), r=["eG", "eGh"], w=["eGlo"])
                for j in range(2):
                    P.op("pe", lambda e, j=j: e.matmul(ps[7][:, 16 + 8 * j:24 + 8 * j], lhsT=sel[:, j, :], rhs=eGh, start=True, stop=False),
                         r=["sel", "eGh"], w=["ps7"])
                    P.op("pe", lambda e, j=j: e.matmul(ps[7][:, 16 + 8 * j:24 + 8 * j], lhsT=sel[:, j, :], rhs=eGlo, start=False, stop=True),
                         r=["sel", "eGlo"], w=["ps7"])
                P.op("dve", lambda e: e.tensor_copy(eGL, ps[7][:, 16:32]), r=["ps7"], w=["eGL"])
                P.op("dve", lambda e, lb=lb: e.tensor_scalar_mul(negb, btl[lb], -1.0), r=[f"btl{lb}"], w=["negb"])
                for hd in range(GH):
                    if GDN_CUT < 0.5:
                        continue
                    qT_ = qa[lb][:, hd, :]
                    kT_ = ka[lb][:, hd, :]
                    P.op("dve", lambda e, hd=hd: e.tensor_scalar(diagGh, ident, Ghf[:, hd:hd + 1], None, ALU.mult),
                         r=["ident", "Ghf"], w=["diagGh"])
                    P.op("dve", lambda e, hd=hd: e.tensor_scalar(diagGl, ident, Glf[:, hd:hd + 1], None, ALU.mult),
                         r=["ident", "Glf"], w=["diagGl"])
                    P.op("pe", lambda e: e.matmul(ps[0][:, 0:128], lhsT=ones_bf, rhs=diagGh, start=True, stop=False),
                         r=["ones", "diagGh"], w=["ps0"])
                    P.op("pe", lambda e: e.matmul(ps[0][:, 0:128], lhsT=ones_bf, rhs=diagGl, start=False, stop=True),
                         r=["ones", "diagGl"], w=["ps0"])
                    P.op("act", lambda e: e.activation(out=GrowS, in_=ps[0][:, 0:128], func=AF.Copy), r=["ps0"], w=["GrowS"])
                    if GDN_CUT < 0.6:
                        continue
                    P.op("act", lambda e, hd=hd: e.activation(out=dd, in_=GrowS, func=AF.Relu, bias=negG[:, hd:hd + 1], scale=1.0),
                         r=["GrowS", "negG"], w=["dd"])
                    P.op("act", lambda e: e.activation(out=dd, in_=dd, func=AF.Exp, scale=-1.0), r=["dd"], w=["dd"])
                    P.op("dve", lambda e: e.tensor_tensor(out=Ds, in0=dd, in1=mS, op=ALU.mult), r=["dd", "mS"], w=["Ds"])
                    if GDN_CUT < 0.7:
                        continue
                    P.op("act", lambda e, hd=hd: e.activation(out=dd, in_=GrowS, func=AF.Relu, bias=Gcol[:, hd:hd + 1], scale=-1.0),
                         r=["GrowS", "Gcol", "Ds"], w=["dd"])
                    P.op("act", lambda e: e.activation(out=dd, in_=dd, func=AF.Exp, scale=-1.0), r=["dd"], w=["dd"])
                    P.op("dve", lambda e: e.tensor_tensor(out=DiT, in0=dd, in1=mIT, op=ALU.mult), r=["dd", "mIT"], w=["DiT"])
                    if GDN_CUT < 1:
                        continue
                    P.op("pe", lambda e, kT_=kT_: e.transpose(ps_bf[1][:, 0:128], kT_, identb), r=[f"ka{lb}", "identb"], w=["ps1"])
                    for hf in range(2):
                        P.op("pe", lambda e, hf=hf, hd=hd, lb=lb: e.transpose(
                            ps_bf[1][:, 128 + hf * 128:256 + hf * 128], va[lb][:, hd * 2 + hf, :], identb),
                            r=[f"va{lb}", "identb"], w=["ps1"])
                    for j in range(2):
                        P.op("act", lambda e, j=j: e.activation(out=kdec[j], in_=ps_bf[1][:, 0:128], func=AF.Copy,
                                                               scale=DiT[:, 63 + 64 * j:64 + 64 * j]),
                             r=["ps1", "DiT"], w=[f"kdec{j}"])
                    P.op("act", lambda e, hd=hd: e.activation(out=kbg, in_=ps_bf[1][:, 0:128], func=AF.Copy, scale=beG[:, hd:hd + 1]),
                         r=["ps1", "beG"], w=["kbg"])
                    P.op("act", lambda e, hd=hd, lb=lb: e.activation(out=vb, in_=ps_bf[1][:, 128:384], func=AF.Copy,
                                                                    scale=btl[lb][:, hd:hd + 1]),
                         r=["ps1", f"btl{lb}"], w=["vb"])
                    if GDN_CUT < 2:
                        continue
                    P.op("pe", lambda e, kT_=kT_: e.matmul(ps[2][:, 0:128], lhsT=kT_, rhs=kT_, start=True, stop=True),
                         r=[f"ka{lb}"], w=["ps2"])
                    P.op("dve", lambda e, hd=hd: e.scalar_tensor_tensor(
                        out=Xm[0], in0=ps[2][:, 0:128], scalar=negb[:, hd:hd + 1], in1=Ds, op0=ALU.mult, op1=ALU.mult),
                        r=["ps2", "negb", "Ds"], w=["Xm0"])
                    P.op("pe", lambda e: e.transpose(ps_bf[1][:, 384:512], Xm[0], identb), r=["Xm0", "identb"], w=["ps1"])
                    P.op("act", lambda e: e.activation(out=Xt[0], in_=ps_bf[1][:, 384:512], func=AF.Copy), r=["ps1"], w=["Xt0"])
                    P.op("dve", lambda e: e.tensor_tensor(out=Pt, in0=Xt[0], in1=ident, op=ALU.add),
                         r=["Xt0", "ident"], w=["Pt"])
                    P.op("act", lambda e: e.activation(out=Ptb, in_=Pt, func=AF.Copy), r=["Pt"], w=["Ptb"])
                    cur = 0
                    for it_ in range(1, 6):
                        nx = 1 - cur
                        P.op("pe", lambda e, cur=cur: e.matmul(ps[3][:, 0:128], lhsT=Xt[cur], rhs=Xm[cur], start=True, stop=True),
                             r=[f"Xt{cur}", f"Xm{cur}"], w=["ps3"])
                        P.op("act", lambda e, nx=nx: e.activation(out=Xm[nx], in_=ps[3][:, 0:128], func=AF.Copy),
                             r=["ps3"], w=[f"Xm{nx}"])
                        if it_ < 5:
                            P.op("pe", lambda e, cur=cur: e.matmul(ps[3][:, 128:256], lhsT=Xm[cur], rhs=Xt[cur], start=True, stop=True),
                                 r=[f"Xt{cur}", f"Xm{cur}"], w=["ps3"])
                            P.op("act", lambda e, nx=nx: e.activation(out=Xt[nx], in_=ps[3][:, 128:256], func=AF.Copy),
                                 r=["ps3"], w=[f"Xt{nx}"])
                        P.op("pe", lambda e, nx=nx: e.matmul(ps[3][:, 256:384], lhsT=Xm[nx], rhs=Ptb, start=True, stop=True),
                             r=[f"Xm{nx}", "Ptb"], w=["ps3"])
                        P.op("dve", lambda e: e.tensor_tensor(out=Pt, in0=Pt, in1=ps[3][:, 256:384], op=ALU.add),
                             r=["ps3", "Pt"], w=["Pt"])
                        P.op("act", lambda e: e.activation(out=Ptb, in_=Pt, func=AF.Copy), r=["Pt"], w=["Ptb"])
                        cur = nx
                    if GDN_CUT < 3:
                        continue
                    P.op("pe", lambda e: e.matmul(ps[4][:, 0:256], lhsT=Ptb, rhs=vb, start=True, stop=True), r=["Ptb", "vb"], w=["ps4"])
                    P.op("pe", lambda e: e.matmul(ps[4][:, 256:384], lhsT=kbg, rhs=Ptb, start=True, stop=True), r=["Ptb", "kbg"], w=["ps4"])
                    if GDN_CUT < 3.1:
                        continue
                    P.op("act", lambda e: e.activation(out=usb, in_=ps[4][:, 0:256], func=AF.Copy), r=["ps4"], w=["usb"])
                    P.op("act", lambda e: e.activation(out=wT, in_=ps[4][:, 256:384], func=AF.Copy), r=["ps4"], w=["wT"])
                    if GDN_CUT < 3.2:
                        continue
                    P.op("pe", lambda e, kT_=kT_, qT_=qT_: e.matmul(ps[2][:, 256:384], lhsT=kT_, rhs=qT_, start=True, stop=True),
                         r=[f"ka{lb}", f"qa{lb}"], w=["ps2"])
                    if GDN_CUT < 3.3:
                        continue
                    P.op("dve", lambda e: e.tensor_tensor(out=attnT, in0=ps[2][:, 256:384], in1=DiT, op=ALU.mult),
                         r=["ps2", "DiT"], w=["attnT"])
                    if GDN_CUT < 4:
                        continue
                    for j in range(2):
                        P.op("pe", lambda e, qT_=qT_, hd=hd: e.matmul(ps[5][:, 0:256], lhsT=qT_, rhs=Sbf[:, hd, :], start=True, stop=True),
                             r=[f"qa{lb}", f"Sbf{hd}"], w=["ps5"])
                        P.op("pe", lambda e, hd=hd: e.matmul(ps[5][:, 256:512], lhsT=wT, rhs=Sbf[:, hd, :], start=True, stop=True),
                             r=["wT", f"Sbf{hd}"], w=["ps5"])
                        if GDN_CUT >= 4.04:
                            P.op("dve", lambda e, j=j: e.tensor_tensor(out=vnj[j], in0=usb, in1=ps[5][:, 256:512], op=ALU.subtract),
                                 r=["usb", "ps5"], w=[f"vnj{j}"])
                        oj = osbA if j == 0 else osbB
                        if GDN_CUT >= 4.06:
                            P.op("act", lambda e, hd=hd, oj=oj: e.activation(out=oj, in_=ps[5][:, 0:256], func=AF.Copy, scale=eG[:, hd:hd + 1]),
                                 r=["ps5", "eG"], w=["osbA" if j == 0 else "osbB"])
                        if GDN_CUT >= 4.08:
                            P.op("pe", lambda e, j=j: e.matmul(ps[6][:, 0:256], lhsT=kdec[j], rhs=vnj[j], start=True, stop=True),
                                 r=[f"kdec{j}", f"vnj{j}"], w=["ps6"])
                        if GDN_CUT >= 4.09:
                            P.op("dve", lambda e, hd=hd, j=j: e.scalar_tensor_tensor(
                                out=Sst[:, hd, :], in0=Sst[:, hd, :], scalar=eGL[:, 8 * j + hd:8 * j + hd + 1], in1=ps[6][:, 0:256],
                                op0=ALU.mult, op1=ALU.add),
                                r=[f"Sst{hd}", "eGL", "ps6"], w=[f"Sst{hd}"])
                        if GDN_CUT >= 4.095:
                            P.op("act", lambda e, hd=hd: e.activation(out=Sbf[:, hd, :], in_=Sst[:, hd, :], func=AF.Copy),
                                 r=[f"Sst{hd}"], w=[f"Sbf{hd}"])
                    if GDN_CUT < 4.3:
                        continue
                    P.op("dve", lambda e: e.tensor_scalar(osb, osbA, halfm[:, 0:1], None, ALU.mult),
                         r=["osbA", "halfm"], w=["osb"])
                    P.op("dve", lambda e: e.scalar_tensor_tensor(
                        out=osb, in0=osbB, scalar=halfm[:, 1:2], in1=osb, op0=ALU.mult, op1=ALU.add),
                        r=["osbB", "halfm", "osb"], w=["osb"])
                    if GDN_CUT < 4.6:
                        continue
                    P.op("dve", lambda e: e.tensor_scalar(vnew, vnj[0], halfm[:, 0:1], None, ALU.mult),
                         r=["vnj0", "halfm"], w=["vnew"])
                    P.op("dve", lambda e: e.scalar_tensor_tensor(
                        out=vnew, in0=vnj[1], scalar=halfm[:, 1:2], in1=vnew, op0=ALU.mult, op1=ALU.add),
                        r=["vnj1", "halfm", "vnew"], w=["vnew"])
                    if GDN_CUT < 5:
                        continue
                    P.op("pe", lambda e: e.matmul(ps[6][:, 256:512], lhsT=attnT, rhs=vnew, start=True, stop=True),
                         r=["attnT", "vnew"], w=["ps6"])
                    P.op("dve", lambda e: e.tensor_tensor(out=osb, in0=osb, in1=ps[6][:, 256:512], op=ALU.add),
                         r=["osb", "ps6"], w=["osb"])
                    P.op("act", lambda e: e.activation(out=osq, in_=osb, func=AF.Square), r=["osb"], w=["osq"])
                    P.op("dve", lambda e, hd=hd: e.reduce_sum(out=ssq8[:, hd:hd + 1], in_=osq, axis=AX.X), r=["osq", "ssq8"], w=["ssq8"])
                    P.op("act", lambda e: e.activation(out=r8, in_=ssq8, func=AF.Sqrt, bias=eps256, scale=1.0),
                         r=["ssq8", "eps256"], w=["r8"])
                    P.op("dve", lambda e: e.reciprocal(r8, r8), r=["r8"], w=["r8"])
                    P.op("dve", lambda e, hd=hd: e.scalar_tensor_tensor(
                        out=osq, in0=osb, scalar=r8[:, hd:hd + 1], in1=ngb, op0=ALU.mult, op1=ALU.mult),
                        r=["osb", "r8", "ngb", "osq"], w=["osq"])
                    P.op("dve", lambda e, hd=hd, lb=lb: e.tensor_tensor(
                        out=ysb, in0=osq, in1=za[lb][:, hd * 256:(hd + 1) * 256], op=ALU.mult),
                        r=["osq", f"za{lb}"], w=["ysb"])
                    if GDN_CUT < 6:
                        continue
                    for hf in range(2):
                        P.op("pe", lambda e, hf=hf: e.transpose(ps_bf[1][:, 512 + hf * 128:640 + hf * 128], ysb[:, hf * 128:(hf + 1) * 128], identb),
                             r=["ysb", "identb"], w=["ps1"])
                    P.op("act", lambda e, hd=hd, lb=lb: e.activation(
                        out=ya[lb][:, hd * 2:hd * 2 + 2, :], in_=ps_bf[1][:, 512:768].rearrange("p (a b) -> p a b", a=2), func=AF.Copy),
                        r=["ps1"], w=[f"ya{lb}"])
                P.op("sp", lambda e, lb=lb, c0=c0: e.dma_start(out=gy_v[:, :, c0:c0 + 128], in_=ya[lb]),
                     r=[f"ya{lb}"], w=["gy_s"], dma=True)
            if GDN_PHASES < 3:
                return
            P.barrier()
            A.reset()
            wo = A.alloc([128, 16, D], BF16, "wo")
            for q in range(4):
                P.op("pool", lambda e, q=q: e.dma_start(
                    out=wo[:, q * 4:(q + 1) * 4, :], in_=w_o[:, q * 4:(q + 1) * 4, :]), w=[f"wo{q}"], dma=True)
            yall = [A.alloc([128, 16, TT], BF16, f"yall{b}") for b in range(2)]
            for T in range(NTT):
                t0 = T * TT
                b = T % 2
                P.op("sp", lambda e, b=b, t0=t0: e.dma_start(out=yall[b], in_=gy_v[:, :, t0:t0 + TT]),
                     r=["gy_s"], w=[f"yall{b}"], dma=True)
                for d in range(NCH):
                    py = d % 4
                    for k in range(16):
                        P.op("pe", lambda e, k=k, d=d, b=b, py=py: e.matmul(
                            ps[py][:, :], lhsT=wo[:, k, d * 128:(d + 1) * 128], rhs=yall[b][:, k, :],
                            start=(k == 0), stop=(k == 15)),
                            r=["wo0", "wo1", "wo2", "wo3", f"yall{b}"], w=[f"ps{py}"])
                    P.op("dve", lambda e, d=d, py=py, t0=t0: e.tensor_tensor(
                        out=xT[:, d, t0:t0 + TT], in0=xT[:, d, t0:t0 + TT], in1=ps[py][:, :], op=ALU.add),
                        r=[f"ps{py}", f"xT{T}"], w=[f"xT{T}"])

        MIX = {0: stage_pool, 1: stage_sb, 2: stage_ret, 3: stage_gdn}
        stage_load()
        if only is not None:
            P.barrier(); P.new_epoch()
            if only.startswith("ffn"):
                stage_ffn(int(only[3:]))
            else:
                li = {"pool": 0, "sb": 1, "ret": 2, "gdn": 3}[only]
                MIX[li](li)
        else:
            sub = 0
            for l in range(4):
                if sub < n_sub:
                    if with_mixers:
                        P.barrier(); P.new_epoch()
                        MIX[l](l)
                    sub += 1
                if sub < n_sub:
                    P.barrier(); P.new_epoch()
                    stage_ffn(l)
                    sub += 1
        P.barrier(); P.new_epoch()
        stage_store()
        if DEBUG_DUMP:
            P.barrier()
            for (nm, c0, w_) in DEBUG_DUMP:
                src = DBG[nm]
                if nm.startswith("sb:"):
                    outs.append(P.op("sp", lambda e, src=src, c0=c0, w_=w_: e.dma_start(out=y_d[0:128, c0:c0 + w_], in_=src[:, 0:w_]),
                                     dma=True))
                else:
                    outs.append(P.op("sp", lambda e, src=src, c0=c0, w_=w_: e.dma_start(out=y_d[:, c0:c0 + w_], in_=src[:, 0:w_]),
                                     dma=True))

        P.finalize(nc, st)
        with nc.Block() as block:
            @block.tensor
            def _(e):
                P.emit("pe", e)

            @block.scalar
            def _(e):
                P.emit("act", e)

            @block.vector
            def _(e):
                P.emit("dve", e)

            @block.gpsimd
            def _(e):
                P.emit("pool", e)

            @block.sync
            def _(e):
                P.emit("sp", e)
                P.final_waits("sp", e, outs)
    return nc, list(dr.keys())


def host_consts():
    c = {}
    c["c_ident"] = np.eye(128, dtype=np.float32)
    inv = np.zeros((128, 4, 16), np.float32)
    for g, w in enumerate((2, 4, 8, 16)):
        for t in range(16):
            inv[:, g, t] = 1.0 / min(t + 1, w)
    c["c_poolinv"] = inv
    si = np.arange(128)[:, None, None]
    db = np.arange(4)[None, :, None]
    ti = np.arange(512)[None, None, :]
    c["c_sbmask"] = ((db * 128 + si) < ti).astype(np.float32)
    jj = np.arange(128)[:, None]
    ss = np.arange(128)[None, :]
    c["c_ustrict"] = (jj > ss).astype(np.float32)
    half = 128
    inv = (np.float32(10000.0) ** (-np.arange(half, dtype=np.float32) / np.float32(half))).astype(np.float32)
    ang = (np.arange(4096, dtype=np.float32)[None, :] * inv[:, None]).astype(np.float32)
    c["c_cos"] = np.cos(ang).astype(np.float32)
    c["c_sin"] = np.sin(ang).astype(np.float32)
    lg = np.log1p(-np.exp2(-5.0 - np.arange(4, dtype=np.float64)))
    idx = np.arange(128, dtype=np.float64)
    diff = idx[:, None] - idx[None, :]
    intra = np.where(diff >= 0, np.exp(np.maximum(diff, 0.0)[None] * lg[:, None, None]), 0.0)
    c["c_ret_intraT"] = np.ascontiguousarray(intra.transpose(0, 2, 1)).astype(np.float32)
    qdec = np.exp((idx + 1.0)[None] * lg[:, None])
    c["c_ret_qdec"] = np.ascontiguousarray(np.broadcast_to(qdec[:, None, :], (4, 128, 128))).astype(np.float32)
    kdec = np.exp((127.0 - idx)[None] * lg[:, None])
    c["c_ret_kdec"] = np.ascontiguousarray(kdec.T).astype(np.float32)
    a_ = np.arange(128)
    same = (a_[:, None] // 64) == (a_[None, :] // 64)
    c["c_gdn_maskS"] = (same & (a_[:, None] > a_[None, :])).astype(np.float32)
    c["c_gdn_maskIT"] = (same & (a_[None, :] >= a_[:, None])).astype(np.float32)
    c["c_gdn_half"] = np.stack([(a_ < 64), (a_ >= 64)], axis=1).astype(np.float32)
    sel = np.zeros((128, 2, 128), np.float32)
    sel[63, 0, :] = 1.0
    sel[127, 1, :] = 1.0
    c["c_gdn_sel"] = sel
    return c


CONST_SHAPES = {k: v.shape for k, v in host_consts().items()}


def run(inputs, n_sub=8, cores=8, with_mixers=True, only=None):
    nc, names = build(n_sub, with_mixers, only)
    consts = host_consts()
    x = np.ascontiguousarray(np.asarray(inputs["x"], dtype=np.float32))
    shared = {}
    for k in names:
        if k == "x":
            continue
        shared[k] = consts[k] if k in consts else np.ascontiguousarray(np.asarray(inputs[k], dtype=np.float32))
    in_maps = []
    for b in range(cores):
        m = dict(shared)
        m["x"] = x[b]
        in_maps.append(m)
    res = run_bass_kernel_spmd(nc, in_maps, core_ids=list(range(cores)))
    return np.stack([np.asarray(r["y"]) for r in res.results], axis=0)


def kernel(**inputs):
    return run(inputs, n_sub=8, cores=8).astype(np.float32)
```

```python
import numpy as np
import ml_dtypes
import concourse.bass as bass
import concourse.mybir as mybir
from concourse.bass_utils import run_bass_kernel_spmd

F32 = mybir.dt.float32
BF16 = mybir.dt.bfloat16
AF = mybir.ActivationFunctionType
ALU = mybir.AluOpType
AX = mybir.AxisListType

D = 1024
S = 4096
NCH = D // 128
FF = 2816
NJ = FF // 128
EPS = 1e-6
TT = 512
NTT = S // TT

ENGS = ("pe", "act", "dve", "pool", "sp")
NSLOT = 8


class Ins:
    __slots__ = ("fn", "eng", "idx", "waits", "need_inc", "epoch", "dma", "rank", "slotwait")

    def __init__(self, fn, eng, idx, epoch, dma):
        self.fn = fn
        self.eng = eng
        self.idx = idx
        self.waits = []
        self.need_inc = False
        self.epoch = epoch
        self.dma = dma
        self.rank = None
        self.slotwait = None


class Prog:
    def __init__(self):
        self.ins = {e: [] for e in ENGS}
        self.res = {}
        self.epoch = 0
        self.ndma = {e: 0 for e in ENGS}
        self.pending_barrier = {e: [] for e in ENGS}

    def new_epoch(self):
        self.epoch += 1

    def barrier(self):
        lasts = []
        for e in ENGS:
            if self.ins[e]:
                lasts.append(self.ins[e][-1])
            k = 0
            for ins in reversed(self.ins[e]):
                if ins.dma is not None:
                    lasts.append(ins)
                    k += 1
                    if k >= NSLOT:
                        break
        for e in ENGS:
            self.pending_barrier[e] = list(lasts)

    def op(self, eng, fn, r=(), w=(), dma=False):
        w = list(w) + [k for k in r if k.startswith("ps") and k[2:].isdigit() and k not in w]
        lst = self.ins[eng]
        ins = Ins(fn, eng, len(lst), self.epoch, None)
        if dma:
            ins.dma = self.ndma[eng]
            self.ndma[eng] += 1
        deps = []
        if self.pending_barrier[eng]:
            deps.extend(self.pending_barrier[eng])
            self.pending_barrier[eng] = []
        for k in r:
            st = self.res.get(k)
            if st is None:
                st = self.res[k] = [None, {}]
            if st[0] is not None:
                deps.append(st[0])
        for k in w:
            st = self.res.get(k)
            if st is None:
                st = self.res[k] = [None, {}]
            if st[0] is not None:
                deps.append(st[0])
            for rd in st[1].values():
                deps.append(rd)
        for d in deps:
            if d is ins:
                continue
            if d.eng == eng and d.dma is None and eng == "pe":
                continue
            if d.eng == eng and d.dma is None and ins.dma is not None and eng in ("sp",):
                continue
            ins.waits.append(d)
            d.need_inc = True
        for k in r:
            self.res[k][1][eng + ("D" if dma else "")] = ins
        for k in w:
            st = self.res[k]
            st[0] = ins
            st[1] = {}
        lst.append(ins)
        return ins

    def finalize(self, nc, stack):
        nep = self.epoch + 1
        self.csem = {}
        for e in ("pe", "act", "dve", "pool"):
            for ep in range(nep):
                used = any(i.need_inc and i.dma is None and i.epoch == ep for i in self.ins[e])
                if used:
                    self.csem[(e, ep)] = stack.enter_context(nc.semaphore(f"c_{e}_{ep}"))
        self.dsem = {}
        for e in ENGS:
            if self.ndma[e]:
                for s in range(NSLOT):
                    self.dsem[(e, s)] = stack.enter_context(nc.semaphore(f"d_{e}_{s}"))
        for e in ENGS:
            cnt = {}
            for i in self.ins[e]:
                if i.dma is None:
                    if i.need_inc:
                        cnt[i.epoch] = cnt.get(i.epoch, 0) + 1
                        i.rank = cnt[i.epoch]
                        assert i.rank < 60000, "semaphore overflow; add epochs"
                else:
                    i.need_inc = True

    def target(self, d):
        if d.dma is None:
            return self.csem[(d.eng, d.epoch)], d.rank
        return self.dsem[(d.eng, d.dma % NSLOT)], 16 * (d.dma // NSLOT + 1)

    def emit(self, eng, engobj):
        waited = {}
        for i in self.ins[eng]:
            ws = []
            if i.dma is not None and i.dma >= NSLOT:
                ws.append((self.dsem[(eng, i.dma % NSLOT)], 16 * (i.dma // NSLOT)))
            for d in i.waits:
                ws.append(self.target(d))
            for sem, val in ws:
                if waited.get(sem.num, 0) >= val:
                    continue
                waited[sem.num] = val
                engobj.wait_ge(sem, val)
            bi = i.fn(engobj)
            if i.dma is not None:
                bi.then_inc(self.dsem[(eng, i.dma % NSLOT)], 16)
            elif i.need_inc:
                bi.then_inc(self.csem[(eng, i.epoch)], 1)

    def final_waits(self, eng, engobj, outs):
        for d in outs:
            sem, val = self.target(d)
            engobj.wait_ge(sem, val)


class Arena:
    def __init__(self, ap, words):
        self.ap = ap
        self.words = words
        self.off = 0
        self.uid = 0

    def reset(self):
        self.off = 0

    def alloc(self, shape, dt, name):
        n = 1
        for s in shape[1:]:
            n *= s
        nbytes = n * (4 if dt == F32 else 2)
        nw = (nbytes + 3) // 4
        nw = (nw + 7) // 8 * 8
        assert self.off + nw <= self.words, f"arena overflow {name}: {self.off}+{nw}>{self.words}"
        v = self.ap[:, self.off:self.off + nw]
        self.off += nw
        if dt != F32:
            v = v.bitcast(dt)
        v = v[0:shape[0], 0:n]
        if len(shape) == 3:
            v = v.rearrange("p (a b) -> p a b", a=shape[1])
        elif len(shape) == 4:
            v = v.rearrange("p (a b c) -> p a b c", a=shape[1], b=shape[2])
        self.uid += 1
        return v


ARENA_WORDS = 19 * 1024
GDN_PHASES = 3
DEBUG_DUMP = []
DBG = {}
GDN_CUT = 99


def build(n_sub=8, with_mixers=True, only=None):
    from contextlib import ExitStack
    nc = bass.Bass("TRN2", target_bir_lowering=False)
    dr = {}

    SHAPES = {
        "x": [S, D], "norm_mix": [4, D], "norm_ffn": [4, D], "ffn_w_in": [4, D, 2 * FF], "ffn_w_out": [4, FF, D],
        "pool_w": [1, 4, 256, 256], "pool_scale": [1, D],
        "sb_w_qkv": [1, D, 3072], "sb_q_gain": [1, 128], "sb_k_gain": [1, 128], "sb_w_o": [1, D, D],
        "ret_w_in": [1, D, 6144], "ret_gn_gain": [1, 512], "ret_w_o": [1, 2048, D],
        "gdn_w_in": [1, D, 6160], "gdn_conv_w": [1, 4, 4096], "gdn_a_log": [1, 8], "gdn_dt_bias": [1, 8],
        "gdn_norm_gain": [1, 256], "gdn_w_o": [1, 2048, D],
    }
    for k_, v_ in CONST_SHAPES.items():
        SHAPES[k_] = list(v_)

    def G(name):
        if name not in dr:
            dr[name] = nc.dram_tensor(name, SHAPES[name], F32, kind="ExternalInput").ap()
        return dr[name]

    x_d = G("x")
    norm_mix = G("norm_mix")
    norm_ffn = G("norm_ffn")
    c_ident = G("c_ident")
    c_poolinv = G("c_poolinv")
    y_d = nc.dram_tensor("y", [S, D], F32, kind="ExternalOutput").ap()

    P = Prog()
    st = ExitStack()
    with st:
        xT = st.enter_context(nc.sbuf_tensor("xT", [128, NCH, S], F32))
        arena_t = st.enter_context(nc.sbuf_tensor("arena", [128, ARENA_WORDS], F32))
        cst_t = st.enter_context(nc.sbuf_tensor("cst", [128, 896], F32))
        ps = [st.enter_context(nc.psum_tensor(f"ps{i}", [128, 512], F32)) for i in range(8)]
        A = Arena(arena_t, ARENA_WORDS)
        C = Arena(cst_t, 896)

        ident = C.alloc([128, 128], F32, "ident")
        ones_bf = C.alloc([128, 128], BF16, "ones")
        gmix = C.alloc([128, 4, NCH], F32, "gmix")
        gffn = C.alloc([128, 4, NCH], F32, "gffn")
        pscale = C.alloc([128, NCH], F32, "pscale")
        poolinv = C.alloc([128, 4, 16], F32, "poolinv")
        P.op("sp", lambda e: e.dma_start(out=ident, in_=c_ident[:, :]), w=["ident"], dma=True)
        P.op("sp", lambda e: e.dma_start(out=poolinv, in_=c_poolinv[:, :, :]), w=["poolinv"], dma=True)
        P.op("dve", lambda e: e.memset(ones_bf, 1.0), w=["ones"])
        epsD = C.alloc([128, 1], F32, "epsD")
        P.op("dve", lambda e: e.memset(epsD, EPS * D), w=["epsD"])
        vstage = C.alloc([128, 128], F32, "vstage")
        nvec = [0]

        def load_vecT(src2d, nrows, dst, key, scale=1.0):
            pb = nvec[0] % 2
            nvec[0] += 1
            P.op("sp", lambda e: e.dma_start(out=vstage[0:nrows, :], in_=src2d), w=["vstage"], dma=True)
            P.op("pe", lambda e: e.transpose(ps[pb][:, 0:nrows], vstage[0:nrows, :], ident[0:nrows, 0:nrows]),
                 r=["vstage", "ident"], w=[f"ps{pb}"])
            P.op("dve", lambda e: e.tensor_scalar_mul(dst, ps[pb][:, 0:nrows], scale), r=[f"ps{pb}"], w=[key])

        rowh = C.alloc([1, 128], BF16, "rowh")
        rowl = C.alloc([1, 128], BF16, "rowl")

        def bcast_row(src_row, n, dst, key):
            pb = nvec[0] % 2
            nvec[0] += 1
            P.op("sp", lambda e: e.dma_start(out=vstage[0:1, 0:n], in_=src_row), w=["vstage"], dma=True)
            P.op("dve", lambda e: e.tensor_copy(rowh[0:1, 0:n], vstage[0:1, 0:n]), r=["vstage"], w=["rowh"])
            P.op("dve", lambda e: e.tensor_tensor(out=rowl[0:1, 0:n], in0=vstage[0:1, 0:n], in1=rowh[0:1, 0:n], op=ALU.subtract),
                 r=["vstage", "rowh"], w=["rowl"])
            P.op("pe", lambda e: e.matmul(ps[pb][:, 0:n], lhsT=ones_bf[0:1, :], rhs=rowh[0:1, 0:n], start=True, stop=False),
                 r=["rowh", "ones"], w=[f"ps{pb}"])
            P.op("pe", lambda e: e.matmul(ps[pb][:, 0:n], lhsT=ones_bf[0:1, :], rhs=rowl[0:1, 0:n], start=False, stop=True),
                 r=["rowl", "ones"], w=[f"ps{pb}"])
            P.op("dve", lambda e: e.tensor_copy(dst, ps[pb][:, 0:n]), r=[f"ps{pb}"], w=[key])

        SQD = float(np.sqrt(D))
        load_vecT(norm_mix.rearrange("l (c p) -> (l c) p", p=128), 32, gmix.rearrange("p l c -> p (l c)"), "gmix", SQD)
        load_vecT(norm_ffn.rearrange("l (c p) -> (l c) p", p=128), 32, gffn.rearrange("p l c -> p (l c)"), "gffn", SQD)
        load_vecT(G("pool_scale").rearrange("l (c p) -> (l c) p", p=128), 8, pscale, "pscale")

        def stage_load():
            A.reset()
            xin = [A.alloc([128, D], F32, f"xin{b}") for b in range(2)]
            for i in range(S // 128):
                b = i % 2
                P.op("sp", lambda e, i=i, b=b: e.dma_start(out=xin[b], in_=x_d[i * 128:(i + 1) * 128, :]),
                     w=[f"xin{b}"], dma=True)
                for h in range(2):
                    pb = ps[(2 * i + h) % 8]
                    for q in range(4):
                        c = 4 * h + q
                        P.op("pe", lambda e, pb=pb, q=q, c=c, b=b: e.transpose(
                            pb[:, q * 128:(q + 1) * 128], xin[b][:, c * 128:(c + 1) * 128], ident),
                            r=[f"xin{b}", "ident"], w=[f"ps{(2 * i + h) % 8}"])
                    eng = "act" if h == 0 else "dve"
                    dst = xT[:, 4 * h:4 * h + 4, i * 128:(i + 1) * 128]
                    src = pb[:, :].rearrange("p (a b) -> p a b", a=4)
                    if eng == "act":
                        P.op("act", lambda e, dst=dst, src=src: e.activation(out=dst, in_=src, func=AF.Copy),
                             r=[f"ps{(2 * i + h) % 8}"], w=[f"xTl{i}_{h}"])
                    else:
                        P.op("dve", lambda e, dst=dst, src=src: e.tensor_copy(dst, src),
                             r=[f"ps{(2 * i + h) % 8}"], w=[f"xTl{i}_{h}"])

        outs = []

        def stage_store():
            A.reset()
            xo = [A.alloc([128, D], F32, f"xo{b}") for b in range(2)]
            for i in range(S // 128):
                b = i % 2
                for h in range(2):
                    pb = ps[(2 * i + h) % 8]
                    for q in range(4):
                        c = 4 * h + q
                        P.op("pe", lambda e, pb=pb, q=q, c=c, i=i: e.transpose(
                            pb[:, q * 128:(q + 1) * 128], xT[:, c, i * 128:(i + 1) * 128], ident),
                            r=["ident"], w=[f"ps{(2 * i + h) % 8}"])
                    dst = xo[b][:, h * 512:(h + 1) * 512]
                    if h == 0:
                        P.op("act", lambda e, dst=dst, pb=pb: e.activation(out=dst, in_=pb[:, :], func=AF.Copy),
                             r=[f"ps{(2 * i + h) % 8}"], w=[f"xo{b}_{h}"])
                    else:
                        P.op("dve", lambda e, dst=dst, pb=pb: e.tensor_copy(dst, pb[:, :]),
                             r=[f"ps{(2 * i + h) % 8}"], w=[f"xo{b}_{h}"])
                outs.append(P.op("sp", lambda e, i=i, b=b: e.dma_start(out=y_d[i * 128:(i + 1) * 128, :], in_=xo[b]),
                                 r=[f"xo{b}_0", f"xo{b}_1"], dma=True))

        def stage_ffn(l):
            A.reset()
            rstd = A.alloc([128, TT], F32, "rstd")
            hT = A.alloc([128, NCH, TT], BF16, "hT")
            sil = [A.alloc([128, TT], F32, f"sil{b}") for b in range(2)]
            aT = A.alloc([128, NJ, TT], BF16, "aT")
            sqb = aT[:, 0:NCH, :]
            NWB = 2
            win = [A.alloc([128, 2, NCH, 256], BF16, f"win{b}") for b in range(NWB)]
            wout = [A.alloc([128, NJ, 256], BF16, f"wout{b}") for b in range(2)]
            w_in_v = G("ffn_w_in")[l].rearrange("(kc p) (gu f) -> p gu kc f", p=128, gu=2)
            w_out_v = G("ffn_w_out")[l].rearrange("(j p) d -> p j d", p=128)
            nwin = 0
            nwout = 0
            for T in range(NTT):
                t0 = T * TT
                rmsnorm_tile_ffn(T, l, sqb, rstd, hT)
                for jg in range(NJ // 2):
                    wb = nwin % NWB
                    nwin += 1
                    P.op("pool", lambda e, wb=wb, jg=jg: e.dma_start(
                        out=win[wb], in_=w_in_v[:, :, :, jg * 256:(jg + 1) * 256]),
                        w=[f"win{wb}"], dma=True)
                    for jj in range(2):
                        j = jg * 2 + jj
                        pg, pu = (0, 1) if j % 2 == 0 else (2, 3)
                        for k in range(NCH):
                            P.op("pe", lambda e, k=k, wb=wb, jj=jj, pg=pg: e.matmul(
                                ps[pg][:, :], lhsT=win[wb][:, 0, k, jj * 128:(jj + 1) * 128], rhs=hT[:, k, :],
                                start=(k == 0), stop=(k == NCH - 1)),
                                r=[f"win{wb}", f"fh{k}"], w=[f"ps{pg}"])
                        for k in range(NCH):
                            P.op("pe", lambda e, k=k, wb=wb, jj=jj, pu=pu: e.matmul(
                                ps[pu][:, :], lhsT=win[wb][:, 1, k, jj * 128:(jj + 1) * 128], rhs=hT[:, k, :],
                                start=(k == 0), stop=(k == NCH - 1)),
                                r=[f"win{wb}", f"fh{k}"], w=[f"ps{pu}"])
                        sb_ = j % 2
                        P.op("act", lambda e, sb_=sb_, pg=pg: e.activation(out=sil[sb_], in_=ps[pg][:, :], func=AF.Silu),
                             r=[f"ps{pg}"], w=[f"sil{sb_}"])
                        P.op("dve", lambda e, sb_=sb_, pu=pu, j=j: e.tensor_tensor(
                            out=aT[:, j, :], in0=sil[sb_], in1=ps[pu][:, :], op=ALU.mult),
                            r=[f"sil{sb_}", f"ps{pu}"], w=[f"aT{j}"])
                for dg in range(NCH // 2):
                    ob = nwout % 2
                    nwout += 1
                    P.op("pool", lambda e, ob=ob, dg=dg: e.dma_start(
                        out=wout[ob], in_=w_out_v[:, :, dg * 256:(dg + 1) * 256]),
                        w=[f"wout{ob}"], dma=True)
                    for dd in range(2):
                        d = dg * 2 + dd
                        py = 4 + (d % 4)
                        for j in range(NJ):
                            P.op("pe", lambda e, j=j, ob=ob, dd=dd, py=py: e.matmul(
                                ps[py][:, :], lhsT=wout[ob][:, j, dd * 128:(dd + 1) * 128], rhs=aT[:, j, :],
                                start=(j == 0), stop=(j == NJ - 1)),
                                r=[f"wout{ob}", f"aT{j}"], w=[f"ps{py}"])
                        P.op("dve", lambda e, d=d, py=py, t0=t0: e.tensor_tensor(
                            out=xT[:, d, t0:t0 + TT], in0=xT[:, d, t0:t0 + TT], in1=ps[py][:, :], op=ALU.add),
                            r=[f"ps{py}", f"xT{T}"], w=[f"xT{T}"])

        def norm_tile(T, gain, sq, sqk, rstd, rstdk, dst, dstk, pn=7, extra_r=()):
            t0 = T * TT
            for c in range(NCH):
                P.op("act", lambda e, c=c: e.activation(out=sq[c], in_=xT[:, c, t0:t0 + TT], func=AF.Square),
                     r=[f"xT{T}"], w=[sqk[c]])
            for c in range(NCH):
                P.op("pe", lambda e, c=c: e.matmul(ps[pn][:, :], lhsT=ones_bf, rhs=sq[c],
                                                   start=(c == 0), stop=(c == NCH - 1)),
                     r=[sqk[c], "ones"], w=[f"ps{pn}"])
            P.op("act", lambda e: e.activation(out=rstd, in_=ps[pn][:, :], func=AF.Sqrt, bias=epsD, scale=1.0),
                 r=[f"ps{pn}", "epsD"], w=[rstdk])
            P.op("dve", lambda e: e.reciprocal(rstd, rstd), r=[rstdk], w=[rstdk])
            for c in range(NCH):
                P.op("dve", lambda e, c=c: e.scalar_tensor_tensor(
                    out=dst[c], in0=xT[:, c, t0:t0 + TT], scalar=gain[:, c:c + 1], in1=rstd,
                    op0=ALU.mult, op1=ALU.mult),
                    r=[f"xT{T}", rstdk, "gmix", "gffn"] + list(extra_r), w=[dstk[c]])

        def rmsnorm_tile_ffn(T, l, sqb, rstd, hT):
            norm_tile(T, gffn[:, l, :], [sqb[:, c, :] for c in range(NCH)], [f"aT{c}" for c in range(NCH)],
                      rstd, "rstd", [hT[:, c, :] for c in range(NCH)], [f"fh{c}" for c in range(NCH)])

        def stage_pool(l):
            A.reset()
            HW = 16
            rstd = A.alloc([128, TT], F32, "rstd")
            sqb = A.alloc([128, NCH, TT], BF16, "sqb")
            hp = A.alloc([128, NCH, HW + TT], F32, "hp")
            wk = [[A.alloc([128, HW + TT], F32, f"wk{b}{q}") for q in range(2)] for b in range(2)]
            pooled = A.alloc([128, NCH, TT], BF16, "pooled")
            tmp16 = A.alloc([128, 16], F32, "tmp16")
            pw = A.alloc([128, 4, 2, 256], BF16, "pw")
            P.op("pool", lambda e: e.dma_start(out=pw, in_=G("pool_w")[0].rearrange("g (cc p) d -> p g cc d", p=128)),
                 w=["pw"], dma=True)
            for c in range(NCH):
                P.op("dve", lambda e, c=c: e.memset(hp[:, c, 0:HW], 0.0), w=[f"hp{c}"])
            for T in range(NTT):
                t0 = T * TT
                if T > 0:
                    for c in range(NCH):
                        P.op("act", lambda e, c=c: e.activation(out=hp[:, c, 0:HW], in_=hp[:, c, TT:TT + HW], func=AF.Copy),
                             r=[f"hp{c}"], w=[f"hp{c}"])
                norm_tile(T, gmix[:, l, :], [sqb[:, c, :] for c in range(NCH)], [f"sq{c}" for c in range(NCH)],
                          rstd, "rstd", [hp[:, c, HW:HW + TT] for c in range(NCH)], [f"hp{c}" for c in range(NCH)])
                for c in range(NCH):
                    g = c // 2
                    eng = "dve" if c % 2 == 0 else "pool"
                    b = c % 2
                    src = hp[:, c, :]
                    srck = f"hp{c}"
                    for lvl in range(g + 1):
                        sh = 1 << lvl
                        lo = (1 << (lvl + 1)) - 1
                        dstb = wk[b][lvl % 2]
                        dk = f"wk{b}{lvl % 2}"
                        P.op(eng, lambda e, dstb=dstb, src=src, lo=lo, sh=sh: e.tensor_tensor(
                            out=dstb[:, lo:HW + TT], in0=src[:, lo:HW + TT], in1=src[:, lo - sh:HW + TT - sh], op=ALU.add),
                            r=[srck], w=[dk])
                        src = dstb
                        srck = dk
                    w_ = 1 << (g + 1)
                    P.op("dve", lambda e, src=src, c=c, w_=w_: e.scalar_tensor_tensor(
                        out=pooled[:, c, :], in0=src[:, HW:HW + TT], scalar=1.0 / w_, in1=hp[:, c, HW:HW + TT],
                        op0=ALU.mult, op1=ALU.subtract),
                        r=[srck, f"hp{c}"], w=[f"pooled{c}"])
                    if T == 0:
                        P.op("dve", lambda e, src=src, g=g: e.tensor_tensor(
                            out=tmp16, in0=src[:, HW:HW + 16], in1=poolinv[:, g, :], op=ALU.mult),
                            r=[srck, "poolinv"], w=["tmp16"])
                        P.op("dve", lambda e, c=c: e.tensor_tensor(
                            out=pooled[:, c, 0:16], in0=tmp16, in1=hp[:, c, HW:HW + 16], op=ALU.subtract),
                            r=["tmp16", f"hp{c}", f"pooled{c}"], w=[f"pooled{c}"])
                for d in range(NCH):
                    g = d // 2
                    dd = d % 2
                    py = d % 4
                    for cc in range(2):
                        P.op("pe", lambda e, g=g, cc=cc, dd=dd, py=py: e.matmul(
                            ps[py][:, :], lhsT=pw[:, g, cc, dd * 128:(dd + 1) * 128], rhs=pooled[:, 2 * g + cc, :],
                            start=(cc == 0), stop=(cc == 1)),
                            r=["pw", f"pooled{2 * g + cc}"], w=[f"ps{py}"])
                    P.op("dve", lambda e, d=d, py=py, t0=t0: e.scalar_tensor_tensor(
                        out=xT[:, d, t0:t0 + TT], in0=ps[py][:, :], scalar=pscale[:, d:d + 1], in1=xT[:, d, t0:t0 + TT],
                        op0=ALU.mult, op1=ALU.add),
                        r=[f"ps{py}", f"xT{T}", "pscale"], w=[f"xT{T}"])

        def stage_sb(l):
            SBH = 8
            qT_s = nc.dram_tensor("qT_s", [SBH, 128, S], BF16, kind="Internal").ap()
            kT_s = nc.dram_tensor("kT_s", [SBH, 128, S], BF16, kind="Internal").ap()
            v_s = nc.dram_tensor("v_s", [S, D], BF16, kind="Internal").ap()
            o_s = nc.dram_tensor("o_s", [SBH, 128, S], BF16, kind="Internal").ap()
            w_qkv = G("sb_w_qkv")[0].rearrange("(kc p) n -> p kc n", p=128)
            w_o = G("sb_w_o")[0].rearrange("(kc p) n -> p kc n", p=128)
            A.reset()
            qg = A.alloc([128, 2], F32, "qg")
            load_vecT(G("sb_q_gain"), 1, qg[:, 0:1], "qg0")
            load_vecT(G("sb_k_gain"), 1, qg[:, 1:2], "qg1", float(np.sqrt(128.0)))
            eps128 = A.alloc([128, 1], F32, "eps128")
            P.op("dve", lambda e: e.memset(eps128, EPS * 128), w=["eps128"])
            rstd = A.alloc([128, TT], F32, "rstd")
            sqb = A.alloc([128, NCH, TT], BF16, "sqb")
            hT = A.alloc([128, NCH, TT], BF16, "hT")
            wq = [A.alloc([128, NCH, 512], BF16, f"wq{b}") for b in range(2)]
            sqh = [A.alloc([128, TT], BF16, f"sqh{b}") for b in range(2)]
            r2 = [A.alloc([128, TT], F32, f"r2{b}") for b in range(2)]
            qn = [A.alloc([128, TT], BF16, f"qn{b}") for b in range(2)]
            vsb = [A.alloc([128, 512], BF16, f"vsb{b}") for b in range(2)]
            nw = 0
            nq = 0
            nv = 0
            for T in range(NTT):
                t0 = T * TT
                norm_tile(T, gmix[:, l, :], [sqb[:, c, :] for c in range(NCH)], [f"sq{c}" for c in range(NCH)],
                          rstd, "rstd", [hT[:, c, :] for c in range(NCH)], [f"h{c}" for c in range(NCH)])
                hkeys = [f"h{c}" for c in range(NCH)]
                for piece in range(6):
                    wb = nw % 2
                    nw += 1
                    P.op("pool", lambda e, wb=wb, piece=piece: e.dma_start(
                        out=wq[wb], in_=w_qkv[:, :, piece * 512:(piece + 1) * 512]), w=[f"wq{wb}"], dma=True)
                    if piece < 4:
                        isk = piece // 2
                        for cc in range(4):
                            hd = (piece % 2) * 4 + cc
                            pq = cc % 2
                            pss = 2 + cc % 2
                            b = nq % 2
                            nq += 1
                            for kc in range(NCH):
                                P.op("pe", lambda e, kc=kc, wb=wb, cc=cc, pq=pq: e.matmul(
                                    ps[pq][:, :], lhsT=wq[wb][:, kc, cc * 128:(cc + 1) * 128], rhs=hT[:, kc, :],
                                    start=(kc == 0), stop=(kc == NCH - 1)),
                                    r=[f"wq{wb}", f"h{kc}"], w=[f"ps{pq}"])
                            P.op("act", lambda e, b=b, pq=pq: e.activation(out=sqh[b], in_=ps[pq][:, :], func=AF.Square),
                                 r=[f"ps{pq}"], w=[f"sqh{b}"])
                            P.op("pe", lambda e, b=b, pss=pss: e.matmul(ps[pss][:, :], lhsT=ones_bf, rhs=sqh[b],
                                                                       start=True, stop=True),
                                 r=[f"sqh{b}", "ones"], w=[f"ps{pss}"])
                            P.op("act", lambda e, b=b, pss=pss: e.activation(out=r2[b], in_=ps[pss][:, :], func=AF.Sqrt,
                                                                            bias=eps128, scale=1.0),
                                 r=[f"ps{pss}", "eps128"], w=[f"r2{b}"])
                            P.op("dve", lambda e, b=b: e.reciprocal(r2[b], r2[b]), r=[f"r2{b}"], w=[f"r2{b}"])
                            P.op("dve", lambda e, b=b, pq=pq, isk=isk: e.scalar_tensor_tensor(
                                out=qn[b], in0=ps[pq][:, :], scalar=qg[:, isk:isk + 1], in1=r2[b],
                                op0=ALU.mult, op1=ALU.mult),
                                r=[f"ps{pq}", f"r2{b}", "qg0", "qg1"], w=[f"qn{b}"])
                            dst = (kT_s if isk else qT_s)[hd, :, t0:t0 + TT]
                            P.op("sp", lambda e, b=b, dst=dst: e.dma_start(out=dst, in_=qn[b]),
                                 r=[f"qn{b}"], w=["kT_s" if isk else "qT_s"], dma=True)
                    else:
                        for s_ in range(4):
                            pv = 4 + s_ % 2
                            b = nv % 2
                            nv += 1
                            for kc in range(NCH):
                                P.op("pe", lambda e, kc=kc, wb=wb, s_=s_, pv=pv: e.matmul(
                                    ps[pv][:, :], lhsT=hT[:, kc, s_ * 128:(s_ + 1) * 128], rhs=wq[wb][:, kc, :],
                                    start=(kc == 0), stop=(kc == NCH - 1)),
                                    r=[f"wq{wb}", f"h{kc}"], w=[f"ps{pv}"])
                            P.op("act", lambda e, b=b, pv=pv: e.activation(out=vsb[b], in_=ps[pv][:, :], func=AF.Copy),
                                 r=[f"ps{pv}"], w=[f"vsb{b}"])
                            r0 = t0 + s_ * 128
                            c0 = (piece - 4) * 512
                            P.op("sp", lambda e, b=b, r0=r0, c0=c0: e.dma_start(
                                out=v_s[r0:r0 + 128, c0:c0 + 512], in_=vsb[b]),
                                r=[f"vsb{b}"], w=["v_s"], dma=True)
            P.barrier()
            A.reset()
            c_sbmask = G("c_sbmask")
            c_ustrict = G("c_ustrict")
            mask = A.alloc([128, 4, 512], BF16, "mask")
            ustr = A.alloc([128, 128], BF16, "ustr")
            P.op("pool", lambda e: e.dma_start(out=mask, in_=c_sbmask[:, :, :]), w=["mask"], dma=True)
            P.op("pool", lambda e: e.dma_start(out=ustr, in_=c_ustrict[:, :]), w=["ustr"], dma=True)
            G("c_sbmask"); G("c_ustrict")
            qT = A.alloc([128, S], BF16, "qT")
            kT = A.alloc([128, S], BF16, "kT")
            vv = A.alloc([128, S // 128, 128], BF16, "vv")
            ebuf = [A.alloc([128, 512], F32, f"eb{b}") for b in range(2)]
            spf = [A.alloc([128, 512], F32, f"spf{b}") for b in range(2)]
            spb = [A.alloc([128, 512], BF16, f"spb{b}") for b in range(2)]
            R = A.alloc([128, 512], F32, "R")
            Rb = [A.alloc([128, 512], BF16, f"Rb{b}") for b in range(2)]
            tb = [A.alloc([128, 512], F32, f"tb{b}") for b in range(2)]
            wT = [A.alloc([128, 512], BF16, f"wT{b}") for b in range(2)]
            osb = [A.alloc([128, 512], BF16, f"osb{b}") for b in range(2)]
            v_view = v_s.rearrange("(n p) (h d) -> p h n d", p=128, h=SBH)
            it = 0
            for hd in range(SBH):
                P.op("sp", lambda e, hd=hd: e.dma_start(out=qT, in_=qT_s[hd]), r=["qT_s"], w=["qT"], dma=True)
                P.op("sp", lambda e, hd=hd: e.dma_start(out=kT, in_=kT_s[hd]), r=["kT_s"], w=["kT"], dma=True)
                P.op("sp", lambda e, hd=hd: e.dma_start(out=vv, in_=v_view[:, hd]), r=["v_s"], w=["vv"], dma=True)
                for sq in range(S // 512):
                    q0 = sq * 512
                    po = 4 + sq % 2
                    P.op("pool", lambda e: e.memset(R, 0.0), w=["R"])
                    nkb = 4 * sq + 4
                    for kb in reversed(range(nkb)):
                        diag = kb - 4 * sq
                        b = it % 2
                        it += 1
                        pz = b
                        pa = 2 + b
                        P.op("pe", lambda e, kb=kb, q0=q0, pz=pz: e.matmul(
                            ps[pz][:, :], lhsT=kT[:, kb * 128:(kb + 1) * 128], rhs=qT[:, q0:q0 + 512],
                            start=True, stop=True), r=["kT", "qT"], w=[f"ps{pz}"])
                        P.op("act", lambda e, b=b, pz=pz: e.activation(out=ebuf[b], in_=ps[pz][:, :], func=AF.Exp),
                             r=[f"ps{pz}"], w=[f"eb{b}"])
                        P.op("act", lambda e, b=b: e.activation(out=spf[b], in_=ebuf[b], func=AF.Ln, bias=1.0, scale=1.0),
                             r=[f"eb{b}"], w=[f"spf{b}"])
                        if diag >= 0:
                            P.op("dve", lambda e, b=b, diag=diag: e.tensor_tensor(
                                out=spf[b], in0=spf[b], in1=mask[:, diag, :], op=ALU.mult),
                                r=[f"spf{b}", "mask"], w=[f"spf{b}"])
                        P.op("dve", lambda e, b=b: e.tensor_copy(spb[b], spf[b]), r=[f"spf{b}"], w=[f"spb{b}"])
                        P.op("act", lambda e, b=b: e.activation(out=Rb[b], in_=R, func=AF.Copy), r=["R"], w=[f"Rb{b}"])
                        P.op("pe", lambda e, b=b, pa=pa: e.matmul(ps[pa][:, :], lhsT=ustr, rhs=spb[b], start=True, stop=False),
                             r=["ustr", f"spb{b}"], w=[f"ps{pa}"])
                        P.op("pe", lambda e, b=b, pa=pa: e.matmul(ps[pa][:, :], lhsT=ones_bf, rhs=Rb[b], start=False, stop=True),
                             r=["ones", f"Rb{b}"], w=[f"ps{pa}"])
                        P.op("dve", lambda e, b=b, pz=pz: e.tensor_tensor(
                            out=tb[b], in0=ps[pz][:, :], in1=spf[b], op=ALU.subtract),
                            r=[f"ps{pz}", f"spf{b}"], w=[f"tb{b}"])
                        P.op("dve", lambda e, b=b, pa=pa: e.tensor_tensor(
                            out=tb[b], in0=tb[b], in1=ps[pa][:, :], op=ALU.subtract),
                            r=[f"ps{pa}", f"tb{b}"], w=[f"tb{b}"])
                        P.op("act", lambda e, b=b: e.activation(out=wT[b], in_=tb[b], func=AF.Exp),
                             r=[f"tb{b}"], w=[f"wT{b}"])
                        if diag >= 0:
                            P.op("dve", lambda e, b=b, diag=diag: e.tensor_tensor(
                                out=wT[b], in0=wT[b], in1=mask[:, diag, :], op=ALU.mult),
                                r=[f"wT{b}", "mask"], w=[f"wT{b}"])
                        P.op("pool", lambda e, b=b: e.tensor_tensor(out=R, in0=R, in1=spf[b], op=ALU.add),
                             r=["R", f"spf{b}"], w=["R"])
                        P.op("pe", lambda e, b=b, kb=kb, po=po, nkb=nkb: e.matmul(
                            ps[po][:, :], lhsT=vv[:, kb, :], rhs=wT[b], start=(kb == nkb - 1), stop=(kb == 0)),
                            r=["vv", f"wT{b}"], w=[f"ps{po}"])
                    ob = sq % 2
                    P.op("act", lambda e, ob=ob, po=po: e.activation(out=osb[ob], in_=ps[po][:, :], func=AF.Copy),
                         r=[f"ps{po}"], w=[f"osb{ob}"])
                    P.op("sp", lambda e, ob=ob, hd=hd, q0=q0: e.dma_start(out=o_s[hd, :, q0:q0 + 512], in_=osb[ob]),
                         r=[f"osb{ob}"], w=["o_s"], dma=True)
            P.barrier()
            A.reset()
            wo = A.alloc([128, NCH, D], BF16, "wo")
            for half in range(2):
                P.op("pool", lambda e, half=half: e.dma_start(
                    out=wo[:, half * 4:(half + 1) * 4, :], in_=w_o[:, half * 4:(half + 1) * 4, :]),
                    w=[f"wo{half}"], dma=True)
            oall = [A.alloc([128, SBH, TT], BF16, f"oall{b}") for b in range(2)]
            o_view = o_s.rearrange("h p t -> p h t")
            for T in range(NTT):
                t0 = T * TT
                b = T % 2
                P.op("sp", lambda e, b=b, t0=t0: e.dma_start(out=oall[b], in_=o_view[:, :, t0:t0 + TT]),
                     r=["o_s"], w=[f"oall{b}"], dma=True)
                for d in range(NCH):
                    py = d % 4
                    for h in range(SBH):
                        P.op("pe", lambda e, h=h, d=d, b=b, py=py: e.matmul(
                            ps[py][:, :], lhsT=wo[:, h, d * 128:(d + 1) * 128], rhs=oall[b][:, h, :],
                            start=(h == 0), stop=(h == SBH - 1)),
                            r=["wo0", "wo1", f"oall{b}"], w=[f"ps{py}"])
                    P.op("dve", lambda e, d=d, py=py, t0=t0: e.tensor_tensor(
                        out=xT[:, d, t0:t0 + TT], in0=xT[:, d, t0:t0 + TT], in1=ps[py][:, :], op=ALU.add),
                        r=[f"ps{py}", f"xT{T}"], w=[f"xT{T}"])

        def stage_ret(l):
            RH = 4
            qr_s = nc.dram_tensor("qr_s", [RH, 2, 128, S], BF16, kind="Internal").ap()
            kr_s = nc.dram_tensor("kr_s", [RH, 2, 128, S], BF16, kind="Internal").ap()
            rv_s = nc.dram_tensor("rv_s", [S, 2048], BF16, kind="Internal").ap()
            sg_s = nc.dram_tensor("sg_s", [16, 128, S], F32, kind="Internal").ap()
            ry_s = nc.dram_tensor("ry_s", [16, 128, S], BF16, kind="Internal").ap()
            w_in = G("ret_w_in")[0].rearrange("(kc p) n -> p kc n", p=128)
            w_o = G("ret_w_o")[0].rearrange("(kc p) n -> p kc n", p=128)
            c_cos = G("c_cos")
            c_sin = G("c_sin")
            c_intraT = G("c_ret_intraT")
            c_qdec = G("c_ret_qdec")
            c_kdec = G("c_ret_kdec")
            A.reset()
            rstd = A.alloc([128, TT], F32, "rstd")
            sqb = A.alloc([128, NCH, TT], BF16, "sqb")
            hT = A.alloc([128, NCH, TT], BF16, "hT")
            wq = [A.alloc([128, NCH, 512], BF16, f"wq{b}") for b in range(2)]
            cs = [[A.alloc([128, TT], F32, f"cs{b}{q}") for q in range(2)] for b in range(2)]
            xs = [A.alloc([128, TT], F32, f"xs{q}") for q in range(2)]
            tt_ = [A.alloc([128, TT], F32, f"tt{q}") for q in range(4)]
            rr = [[A.alloc([128, TT], BF16, f"rr{b}{q}") for q in range(2)] for b in range(2)]
            vsb = [A.alloc([128, 512], BF16, f"vsb{b}") for b in range(2)]
            sgb = [A.alloc([128, TT], F32, f"sgb{b}") for b in range(2)]
            nw = 0
            nr = 0
            nv = 0
            ng = 0
            for T in range(NTT):
                t0 = T * TT
                cb = T % 2
                P.op("sp", lambda e, cb=cb, t0=t0: e.dma_start(out=cs[cb][0], in_=c_cos[:, t0:t0 + TT]), w=[f"cs{cb}0"], dma=True)
                P.op("sp", lambda e, cb=cb, t0=t0: e.dma_start(out=cs[cb][1], in_=c_sin[:, t0:t0 + TT]), w=[f"cs{cb}1"], dma=True)
                norm_tile(T, gmix[:, l, :], [sqb[:, c, :] for c in range(NCH)], [f"sq{c}" for c in range(NCH)],
                          rstd, "rstd", [hT[:, c, :] for c in range(NCH)], [f"h{c}" for c in range(NCH)])
                for piece in range(12):
                    wb = nw % 2
                    nw += 1
                    P.op("pool", lambda e, wb=wb, piece=piece: e.dma_start(
                        out=wq[wb], in_=w_in[:, :, piece * 512:(piece + 1) * 512]), w=[f"wq{wb}"], dma=True)
                    if piece < 4:
                        isk = piece // 2
                        for pr in range(2):
                            hd = (piece % 2) * 2 + pr
                            for half in range(2):
                                cc = pr * 2 + half
                                pq = half
                                for kc in range(NCH):
                                    P.op("pe", lambda e, kc=kc, wb=wb, cc=cc, pq=pq: e.matmul(
                                        ps[pq][:, :], lhsT=wq[wb][:, kc, cc * 128:(cc + 1) * 128], rhs=hT[:, kc, :],
                                        start=(kc == 0), stop=(kc == NCH - 1)),
                                        r=[f"wq{wb}", f"h{kc}"], w=[f"ps{pq}"])
                                sc = (1.0 / 16.0) if isk else 1.0
                                P.op("act", lambda e, half=half, pq=pq, sc=sc: e.activation(
                                    out=xs[half], in_=ps[pq][:, :], func=AF.Copy, scale=sc),
                                    r=[f"ps{pq}"], w=[f"xs{half}"])
                            b = nr % 2
                            nr += 1
                            cosT, sinT = cs[cb][0], cs[cb][1]
                            P.op("dve", lambda e, cosT=cosT: e.tensor_tensor(out=tt_[0], in0=xs[0], in1=cosT, op=ALU.mult),
                                 r=["xs0", f"cs{cb}0"], w=["tt0"])
                            P.op("pool", lambda e, sinT=sinT: e.tensor_tensor(out=tt_[1], in0=xs[1], in1=sinT, op=ALU.mult),
                                 r=["xs1", f"cs{cb}1"], w=["tt1"])
                            P.op("dve", lambda e, sinT=sinT: e.tensor_tensor(out=tt_[2], in0=xs[0], in1=sinT, op=ALU.mult),
                                 r=["xs0", f"cs{cb}1"], w=["tt2"])
                            P.op("pool", lambda e, cosT=cosT: e.tensor_tensor(out=tt_[3], in0=xs[1], in1=cosT, op=ALU.mult),
                                 r=["xs1", f"cs{cb}0"], w=["tt3"])
                            P.op("dve", lambda e, b=b: e.tensor_tensor(out=rr[b][0], in0=tt_[0], in1=tt_[1], op=ALU.subtract),
                                 r=["tt0", "tt1"], w=[f"rr{b}0"])
                            P.op("dve", lambda e, b=b: e.tensor_tensor(out=rr[b][1], in0=tt_[2], in1=tt_[3], op=ALU.add),
                                 r=["tt2", "tt3"], w=[f"rr{b}1"])
                            dsts = (kr_s if isk else qr_s)
                            for half in range(2):
                                P.op("sp", lambda e, b=b, half=half, hd=hd, dsts=dsts, t0=t0: e.dma_start(
                                    out=dsts[hd, half, :, t0:t0 + TT], in_=rr[b][half]),
                                    r=[f"rr{b}{half}"], w=["kr_s" if isk else "qr_s"], dma=True)
                    elif piece < 8:
                        hd = piece - 4
                        for s_ in range(4):
                            pv = 4 + s_ % 2
                            b = nv % 2
                            nv += 1
                            for kc in range(NCH):
                                P.op("pe", lambda e, kc=kc, wb=wb, s_=s_, pv=pv: e.matmul(
                                    ps[pv][:, :], lhsT=hT[:, kc, s_ * 128:(s_ + 1) * 128], rhs=wq[wb][:, kc, :],
                                    start=(kc == 0), stop=(kc == NCH - 1)),
                                    r=[f"wq{wb}", f"h{kc}"], w=[f"ps{pv}"])
                            P.op("act", lambda e, b=b, pv=pv: e.activation(out=vsb[b], in_=ps[pv][:, :], func=AF.Copy),
                                 r=[f"ps{pv}"], w=[f"vsb{b}"])
                            r0 = t0 + s_ * 128
                            P.op("sp", lambda e, b=b, r0=r0, hd=hd: e.dma_start(
                                out=rv_s[r0:r0 + 128, hd * 512:(hd + 1) * 512], in_=vsb[b]),
                                r=[f"vsb{b}"], w=["rv_s"], dma=True)
                    else:
                        for cc in range(4):
                            gc = (piece - 8) * 4 + cc
                            pg = 2 + cc % 2
                            b = ng % 2
                            ng += 1
                            for kc in range(NCH):
                                P.op("pe", lambda e, kc=kc, wb=wb, cc=cc, pg=pg: e.matmul(
                                    ps[pg][:, :], lhsT=wq[wb][:, kc, cc * 128:(cc + 1) * 128], rhs=hT[:, kc, :],
                                    start=(kc == 0), stop=(kc == NCH - 1)),
                                    r=[f"wq{wb}", f"h{kc}"], w=[f"ps{pg}"])
                            P.op("act", lambda e, b=b, pg=pg: e.activation(out=sgb[b], in_=ps[pg][:, :], func=AF.Silu),
                                 r=[f"ps{pg}"], w=[f"sgb{b}"])
                            P.op("sp", lambda e, b=b, gc=gc, t0=t0: e.dma_start(out=sg_s[gc, :, t0:t0 + TT], in_=sgb[b]),
                                 r=[f"sgb{b}"], w=["sg_s"], dma=True)
            P.barrier()
            A.reset()
            identb = A.alloc([128, 128], BF16, "identb")
            P.op("dve", lambda e: e.tensor_copy(identb, ident), r=["ident"], w=["identb"])
            gng = A.alloc([128, 4], F32, "gng")
            load_vecT(G("ret_gn_gain").rearrange("l (c p) -> (l c) p", p=128), 4, gng, "gng", float(np.sqrt(512.0)))
            eps512 = A.alloc([128, 1], F32, "eps512")
            P.op("dve", lambda e: e.memset(eps512, EPS * 512), w=["eps512"])
            intraT = A.alloc([128, 128], F32, "intraT")
            qdec = A.alloc([128, 128], F32, "qdec")
            kdec = A.alloc([128, 4], F32, "kdec")
            P.op("sp", lambda e: e.dma_start(out=kdec, in_=c_kdec[:, :]), w=["kdec"], dma=True)
            Sst = A.alloc([128, 2, 512], F32, "Sst")
            Sbf = A.alloc([128, 2, 512], BF16, "Sbf")
            qg_ = [A.alloc([128, 2, 512], BF16, f"qg{b}") for b in range(2)]
            kg_ = [A.alloc([128, 2, 512], BF16, f"kg{b}") for b in range(2)]
            vg_ = [A.alloc([128, 4, 512], BF16, f"vg{b}") for b in range(2)]
            sgg = [A.alloc([128, 4, 512], F32, f"sgg{b}") for b in range(2)]
            ybuf = [A.alloc([128, 4, 512], BF16, f"ybuf{b}") for b in range(2)]
            qd = [A.alloc([128, 2, 128], BF16, f"qd{b}") for b in range(2)]
            ktd = [A.alloc([128, 256], BF16, f"ktd{b}") for b in range(2)]
            scb = [A.alloc([128, 128], BF16, f"scb{b}") for b in range(2)]
            osq = [A.alloc([128, 4, 128], BF16, f"osq{b}") for b in range(2)]
            rr2 = [A.alloc([128, 128], F32, f"rr2{b}") for b in range(2)]
            ytmp = [A.alloc([128, 128], F32, f"ytmp{b}") for b in range(2)]
            rv_view = rv_s.rearrange("(n p) (h e) -> p h n e", p=128, h=RH)
            sg_view = sg_s.rearrange("k p t -> p k t")
            ry_view = ry_s.rearrange("k p t -> p k t")
            logg = [float(np.log1p(-2.0 ** (-5.0 - h))) for h in range(RH)]
            ps_bf = [ps[i][:, :].bitcast(BF16) for i in range(8)]
            ci = 0
            gi = 0
            for hd in range(RH):
                cd = float(np.exp(128.0 * logg[hd]))
                P.op("sp", lambda e, hd=hd: e.dma_start(out=intraT, in_=c_intraT[hd]), w=["intraT"], dma=True)
                P.op("sp", lambda e, hd=hd: e.dma_start(out=qdec, in_=c_qdec[hd]), w=["qdec"], dma=True)
                P.op("dve", lambda e: e.memset(Sst, 0.0), w=["Sst0", "Sst1"])
                P.op("dve", lambda e: e.memset(Sbf, 0.0), w=["Sbf0", "Sbf1"])
                for grp in range(NTT):
                    t0 = grp * TT
                    gb = gi % 2
                    gi += 1
                    P.op("sp", lambda e, gb=gb, hd=hd, t0=t0: e.dma_start(
                        out=qg_[gb], in_=qr_s[hd].rearrange("h p t -> p h t")[:, :, t0:t0 + TT]),
                        r=["qr_s"], w=[f"qg{gb}"], dma=True)
                    P.op("sp", lambda e, gb=gb, hd=hd, t0=t0: e.dma_start(
                        out=kg_[gb], in_=kr_s[hd].rearrange("h p t -> p h t")[:, :, t0:t0 + TT]),
                        r=["kr_s"], w=[f"kg{gb}"], dma=True)
                    P.op("sp", lambda e, gb=gb, hd=hd, grp=grp: e.dma_start(
                        out=vg_[gb], in_=rv_view[:, hd, grp * 4:(grp + 1) * 4, :]),
                        r=["rv_s"], w=[f"vg{gb}"], dma=True)
                    P.op("sp", lambda e, gb=gb, hd=hd, t0=t0: e.dma_start(
                        out=sgg[gb], in_=sg_view[:, hd * 4:(hd + 1) * 4, t0:t0 + TT]),
                        r=["sg_s"], w=[f"sgg{gb}"], dma=True)
                    for n in range(4):
                        b = ci % 2
                        ci += 1
                        c0 = n * 128
                        for dh in range(2):
                            P.op("dve", lambda e, b=b, dh=dh, gb=gb, c0=c0: e.tensor_tensor(
                                out=qd[b][:, dh, :], in0=qg_[gb][:, dh, c0:c0 + 128], in1=qdec, op=ALU.mult),
                                r=[f"qg{gb}", "qdec"], w=[f"qd{b}"])
                        for dh in range(2):
                            P.op("pe", lambda e, dh=dh, gb=gb, c0=c0: e.transpose(
                                ps_bf[6][:, dh * 128:(dh + 1) * 128], kg_[gb][:, dh, c0:c0 + 128], identb),
                                r=[f"kg{gb}", "identb"], w=["ps6"])
                        P.op("act", lambda e, b=b, hd=hd: e.activation(
                            out=ktd[b], in_=ps_bf[6][:, 0:256], func=AF.Copy, scale=kdec[:, hd:hd + 1]),
                            r=["ps6", "kdec"], w=[f"ktd{b}"])
                        for dh in range(2):
                            P.op("pe", lambda e, dh=dh, gb=gb, c0=c0: e.matmul(
                                ps[7][:, 0:128], lhsT=kg_[gb][:, dh, c0:c0 + 128], rhs=qg_[gb][:, dh, c0:c0 + 128],
                                start=(dh == 0), stop=(dh == 1)),
                                r=[f"kg{gb}", f"qg{gb}"], w=["ps7"])
                        P.op("dve", lambda e, b=b: e.tensor_tensor(out=scb[b], in0=ps[7][:, 0:128], in1=intraT, op=ALU.mult),
                             r=["ps7", "intraT"], w=[f"scb{b}"])
                        po = b
                        for ec in range(4):
                            P.op("pe", lambda e, ec=ec, gb=gb, n=n, b=b, po=po: e.matmul(
                                ps[po][:, ec * 128:(ec + 1) * 128], lhsT=vg_[gb][:, n, ec * 128:(ec + 1) * 128], rhs=scb[b],
                                start=True, stop=False),
                                r=[f"vg{gb}", f"scb{b}"], w=[f"ps{po}"])
                            for dh in range(2):
                                P.op("pe", lambda e, ec=ec, dh=dh, b=b, po=po: e.matmul(
                                    ps[po][:, ec * 128:(ec + 1) * 128], lhsT=Sbf[:, dh, ec * 128:(ec + 1) * 128],
                                    rhs=qd[b][:, dh, :], start=False, stop=(dh == 1)),
                                    r=[f"Sbf{dh}", f"qd{b}"], w=[f"ps{po}"])
                        for dh in range(2):
                            pS = 2 + dh
                            P.op("pe", lambda e, dh=dh, b=b, gb=gb, n=n, pS=pS: e.matmul(
                                ps[pS][:, :], lhsT=ktd[b][:, dh * 128:(dh + 1) * 128], rhs=vg_[gb][:, n, :],
                                start=True, stop=True),
                                r=[f"ktd{b}", f"vg{gb}"], w=[f"ps{pS}"])
                            P.op("dve", lambda e, dh=dh, pS=pS, cd=cd: e.scalar_tensor_tensor(
                                out=Sst[:, dh, :], in0=Sst[:, dh, :], scalar=cd, in1=ps[pS][:, :],
                                op0=ALU.mult, op1=ALU.add),
                                r=[f"Sst{dh}", f"ps{pS}"], w=[f"Sst{dh}"])
                            P.op("act", lambda e, dh=dh: e.activation(out=Sbf[:, dh, :], in_=Sst[:, dh, :], func=AF.Copy),
                                 r=[f"Sst{dh}"], w=[f"Sbf{dh}"])
                        P.op("act", lambda e, b=b, po=po: e.activation(
                            out=osq[b].rearrange("p a b -> p (a b)"), in_=ps[po][:, :], func=AF.Square),
                            r=[f"ps{po}"], w=[f"osq{b}"])
                        pn = 4 + b
                        for ec in range(4):
                            P.op("pe", lambda e, ec=ec, b=b, pn=pn: e.matmul(
                                ps[pn][:, 0:128], lhsT=ones_bf, rhs=osq[b][:, ec, :], start=(ec == 0), stop=(ec == 3)),
                                r=[f"osq{b}", "ones"], w=[f"ps{pn}"])
                        P.op("act", lambda e, b=b, pn=pn: e.activation(out=rr2[b], in_=ps[pn][:, 0:128], func=AF.Sqrt,
                                                                      bias=eps512, scale=1.0),
                             r=[f"ps{pn}", "eps512"], w=[f"rr2{b}"])
                        P.op("dve", lambda e, b=b: e.reciprocal(rr2[b], rr2[b]), r=[f"rr2{b}"], w=[f"rr2{b}"])
                        for ec in range(4):
                            P.op("dve", lambda e, ec=ec, b=b, po=po: e.scalar_tensor_tensor(
                                out=ytmp[b], in0=ps[po][:, ec * 128:(ec + 1) * 128], scalar=gng[:, ec:ec + 1], in1=rr2[b],
                                op0=ALU.mult, op1=ALU.mult),
                                r=[f"ps{po}", f"rr2{b}", "gng"], w=[f"ytmp{b}"])
                            P.op("dve", lambda e, ec=ec, b=b, gb=gb, c0=c0: e.tensor_tensor(
                                out=ybuf[gb][:, ec, c0:c0 + 128], in0=ytmp[b], in1=sgg[gb][:, ec, c0:c0 + 128], op=ALU.mult),
                                r=[f"ytmp{b}", f"sgg{gb}"], w=[f"ybuf{gb}"])
                    P.op("sp", lambda e, gb=gb, hd=hd, t0=t0: e.dma_start(
                        out=ry_view[:, hd * 4:(hd + 1) * 4, t0:t0 + TT], in_=ybuf[gb]),
                        r=[f"ybuf{gb}"], w=["ry_s"], dma=True)
            P.barrier()
            A.reset()
            wo = A.alloc([128, 16, D], BF16, "wo")
            for q in range(4):
                P.op("pool", lambda e, q=q: e.dma_start(
                    out=wo[:, q * 4:(q + 1) * 4, :], in_=w_o[:, q * 4:(q + 1) * 4, :]), w=[f"wo{q}"], dma=True)
            yall = [A.alloc([128, 16, TT], BF16, f"yall{b}") for b in range(2)]
            for T in range(NTT):
                t0 = T * TT
                b = T % 2
                P.op("sp", lambda e, b=b, t0=t0: e.dma_start(out=yall[b], in_=ry_view[:, :, t0:t0 + TT]),
                     r=["ry_s"], w=[f"yall{b}"], dma=True)
                for d in range(NCH):
                    py = d % 4
                    for k in range(16):
                        P.op("pe", lambda e, k=k, d=d, b=b, py=py: e.matmul(
                            ps[py][:, :], lhsT=wo[:, k, d * 128:(d + 1) * 128], rhs=yall[b][:, k, :],
                            start=(k == 0), stop=(k == 15)),
                            r=["wo0", "wo1", "wo2", "wo3", f"yall{b}"], w=[f"ps{py}"])
                    P.op("dve", lambda e, d=d, py=py, t0=t0: e.tensor_tensor(
                        out=xT[:, d, t0:t0 + TT], in0=xT[:, d, t0:t0 + TT], in1=ps[py][:, :], op=ALU.add),
                        r=[f"ps{py}", f"xT{T}"], w=[f"xT{T}"])

        def stage_gdn(l):
            GH = 8
            gq_s = nc.dram_tensor("gq_s", [GH, 128, S], BF16, kind="Internal").ap()
            gk_s = nc.dram_tensor("gk_s", [GH, 128, S], BF16, kind="Internal").ap()
            gv_s = nc.dram_tensor("gv_s", [16, 128, S], BF16, kind="Internal").ap()
            gz_s = nc.dram_tensor("gz_s", [S, 2048], F32, kind="Internal").ap()
            gg_s = nc.dram_tensor("gg_s", [S, 8], F32, kind="Internal").ap()
            gb_s = nc.dram_tensor("gb_s", [S, 8], F32, kind="Internal").ap()
            gy_s = nc.dram_tensor("gy_s", [16, 128, S], BF16, kind="Internal").ap()
            DBG["gg_s"] = gg_s
            DBG["gb_s"] = gb_s
            DBG["gz_s"] = gz_s
            w_in = G("gdn_w_in")[0].rearrange("(kc p) n -> p kc n", p=128)
            w_o = G("gdn_w_o")[0].rearrange("(kc p) n -> p kc n", p=128)
            c_mS = G("c_gdn_maskS")
            c_mIT = G("c_gdn_maskIT")
            c_half = G("c_gdn_half")
            c_sel = G("c_gdn_sel")
            a_log_d = G("gdn_a_log")
            dtb_d = G("gdn_dt_bias")
            ng_d = G("gdn_norm_gain")
            A.reset()
            cw = A.alloc([128, 128], F32, "cw")
            load_vecT(G("gdn_conv_w")[0].rearrange("k (c p) -> (k c) p", p=128), 128, cw, "cw")
            nea = A.alloc([128, 8], F32, "nea")
            dtb = A.alloc([128, 8], F32, "dtb")
            bcast_row(a_log_d[0:1, :], 8, nea, "nea")
            bcast_row(dtb_d[0:1, :], 8, dtb, "dtb")
            P.op("act", lambda e: e.activation(out=nea, in_=nea, func=AF.Exp), r=["nea"], w=["nea"])
            P.op("dve", lambda e: e.tensor_scalar_mul(nea, nea, -1.0), r=["nea"], w=["nea"])
            epsc = A.alloc([128, 1], F32, "epsc")
            P.op("dve", lambda e: e.memset(epsc, EPS), w=["epsc"])
            rstd = A.alloc([128, TT], F32, "rstd")
            sqb = A.alloc([128, NCH, TT], BF16, "sqb")
            hT = A.alloc([128, NCH, TT], BF16, "hT")
            wq = [A.alloc([128, NCH, 512], BF16, f"wq{b}") for b in range(2)]
            wab = A.alloc([128, NCH, 16], BF16, "wab")
            P.op("pool", lambda e: e.dma_start(out=wab, in_=w_in[:, :, 6144:6160]), w=["wab"], dma=True)
            halo = A.alloc([128, 32, 3], F32, "halo")
            P.op("dve", lambda e: e.memset(halo, 0.0), w=["halo"])
            xc = [A.alloc([128, 3 + TT], F32, f"xc{b}") for b in range(2)]
            acc = [A.alloc([128, TT], F32, f"acc{b}") for b in range(2)]
            sl = [A.alloc([128, TT], F32, f"sl{b}") for b in range(2)]
            sqh = [A.alloc([128, TT], BF16, f"sqh{b}") for b in range(2)]
            r2 = [A.alloc([128, TT], F32, f"r2{b}") for b in range(2)]
            qn = [A.alloc([128, TT], BF16, f"qn{b}") for b in range(2)]
            zsb = [A.alloc([128, 512], F32, f"zsb{b}") for b in range(2)]
            ab = [A.alloc([128, 16], F32, f"ab{b}") for b in range(2)]
            gt = [A.alloc([128, 8], F32, f"gt{b}") for b in range(2)]
            bt_ = [A.alloc([128, 8], F32, f"bt{b}") for b in range(2)]
            nw = 0
            nch_ = 0
            nz = 0
            nab = 0
            for T in range(NTT):
                t0 = T * TT
                norm_tile(T, gmix[:, l, :], [sqb[:, c, :] for c in range(NCH)], [f"sq{c}" for c in range(NCH)],
                          rstd, "rstd", [hT[:, c, :] for c in range(NCH)], [f"h{c}" for c in range(NCH)])
                for s_ in range(4):
                    b = nab % 2
                    nab += 1
                    for kc in range(NCH):
                        P.op("pe", lambda e, kc=kc, s_=s_: e.matmul(
                            ps[6][:, 0:16], lhsT=hT[:, kc, s_ * 128:(s_ + 1) * 128], rhs=wab[:, kc, :],
                            start=(kc == 0), stop=(kc == NCH - 1)), r=["wab", f"h{kc}"], w=["ps6"])
                    P.op("dve", lambda e, b=b: e.tensor_tensor(out=ab[b][:, 0:8], in0=ps[6][:, 0:8], in1=dtb, op=ALU.add),
                         r=["ps6", "dtb"], w=[f"ab{b}"])
                    P.op("act", lambda e, b=b: e.activation(out=ab[b][:, 8:16], in_=ps[6][:, 8:16], func=AF.Exp, scale=-1.0),
                         r=["ps6", f"ab{b}"], w=[f"ab{b}"])
                    P.op("act", lambda e, b=b: e.activation(out=ab[b][:, 0:8], in_=ab[b][:, 0:8], func=AF.Exp),
                         r=[f"ab{b}"], w=[f"ab{b}"])
                    P.op("act", lambda e, b=b: e.activation(out=ab[b][:, 0:8], in_=ab[b][:, 0:8], func=AF.Ln, bias=1.0, scale=1.0),
                         r=[f"ab{b}"], w=[f"ab{b}"])
                    P.op("dve", lambda e, b=b: e.tensor_tensor(out=gt[b], in0=ab[b][:, 0:8], in1=nea, op=ALU.mult),
                         r=[f"ab{b}", "nea"], w=[f"gt{b}"])
                    P.op("dve", lambda e, b=b: e.tensor_scalar_add(bt_[b], ab[b][:, 8:16], 1.0), r=[f"ab{b}"], w=[f"bt{b}"])
                    P.op("dve", lambda e, b=b: e.reciprocal(bt_[b], bt_[b]), r=[f"bt{b}"], w=[f"bt{b}"])
                    r0 = t0 + s_ * 128
                    P.op("sp", lambda e, b=b, r0=r0: e.dma_start(out=gg_s[r0:r0 + 128, :], in_=gt[b]), r=[f"gt{b}"], w=["gg_s"], dma=True)
                    P.op("sp", lambda e, b=b, r0=r0: e.dma_start(out=gb_s[r0:r0 + 128, :], in_=bt_[b]), r=[f"bt{b}"], w=["gb_s"], dma=True)
                for piece in range(12):
                    wb = nw % 2
                    nw += 1
                    P.op("pool", lambda e, wb=wb, piece=piece: e.dma_start(
                        out=wq[wb], in_=w_in[:, :, piece * 512:(piece + 1) * 512]), w=[f"wq{wb}"], dma=True)
                    if piece < 8:
                        for cc in range(4):
                            ch = piece * 4 + cc
                            b = nch_ % 2
                            nch_ += 1
                            pq = cc % 2
                            for kc in range(NCH):
                                P.op("pe", lambda e, kc=kc, wb=wb, cc=cc, pq=pq: e.matmul(
                                    ps[pq][:, :], lhsT=wq[wb][:, kc, cc * 128:(cc + 1) * 128], rhs=hT[:, kc, :],
                                    start=(kc == 0), stop=(kc == NCH - 1)),
                                    r=[f"wq{wb}", f"h{kc}"], w=[f"ps{pq}"])
                            P.op("act", lambda e, b=b, pq=pq: e.activation(out=xc[b][:, 3:3 + TT], in_=ps[pq][:, :], func=AF.Copy),
                                 r=[f"ps{pq}"], w=[f"xc{b}"])
                            P.op("pool", lambda e, b=b, ch=ch: e.tensor_copy(xc[b][:, 0:3], halo[:, ch, :]),
                                 r=["halo", f"xc{b}"], w=[f"xc{b}"])
                            P.op("pool", lambda e, b=b, ch=ch: e.tensor_copy(halo[:, ch, :], xc[b][:, TT:TT + 3]),
                                 r=[f"xc{b}"], w=["halo"])
                            P.op("dve", lambda e, b=b, ch=ch: e.tensor_scalar(
                                acc[b], xc[b][:, 3:3 + TT], cw[:, 96 + ch:97 + ch], None, ALU.mult),
                                r=[f"xc{b}", "cw"], w=[f"acc{b}"])
                            for tap in (2, 1, 0):
                                sh = 3 - tap
                                P.op("dve", lambda e, b=b, ch=ch, tap=tap, sh=sh: e.scalar_tensor_tensor(
                                    out=acc[b], in0=xc[b][:, 3 - sh:3 - sh + TT], scalar=cw[:, tap * 32 + ch:tap * 32 + ch + 1],
                                    in1=acc[b], op0=ALU.mult, op1=ALU.add),
                                    r=[f"xc{b}", "cw", f"acc{b}"], w=[f"acc{b}"])
                            if ch < 16:
                                P.op("act", lambda e, b=b: e.activation(out=sl[b], in_=acc[b], func=AF.Silu),
                                     r=[f"acc{b}"], w=[f"sl{b}"])
                                P.op("act", lambda e, b=b: e.activation(out=sqh[b], in_=sl[b], func=AF.Square),
                                     r=[f"sl{b}"], w=[f"sqh{b}"])
                                pss = 2 + cc % 2
                                P.op("pe", lambda e, b=b, pss=pss: e.matmul(ps[pss][:, :], lhsT=ones_bf, rhs=sqh[b],
                                                                           start=True, stop=True),
                                     r=[f"sqh{b}", "ones"], w=[f"ps{pss}"])
                                P.op("act", lambda e, b=b, pss=pss: e.activation(out=r2[b], in_=ps[pss][:, :], func=AF.Sqrt,
                                                                                bias=epsc, scale=1.0),
                                     r=[f"ps{pss}", "epsc"], w=[f"r2{b}"])
                                P.op("dve", lambda e, b=b: e.reciprocal(r2[b], r2[b]), r=[f"r2{b}"], w=[f"r2{b}"])
                                sc = (128.0 ** -0.5) if ch < 8 else 1.0
                                P.op("dve", lambda e, b=b, sc=sc: e.scalar_tensor_tensor(
                                    out=qn[b], in0=sl[b], scalar=sc, in1=r2[b], op0=ALU.mult, op1=ALU.mult),
                                    r=[f"sl{b}", f"r2{b}"], w=[f"qn{b}"])
                                dst = (gq_s if ch < 8 else gk_s)[ch % 8, :, t0:t0 + TT]
                                P.op("sp", lambda e, b=b, dst=dst: e.dma_start(out=dst, in_=qn[b]),
                                     r=[f"qn{b}"], w=["gq_s" if ch < 8 else "gk_s"], dma=True)
                            else:
                                P.op("act", lambda e, b=b: e.activation(out=qn[b], in_=acc[b], func=AF.Silu),
                                     r=[f"acc{b}"], w=[f"qn{b}"])
                                P.op("sp", lambda e, b=b, ch=ch, t0=t0: e.dma_start(out=gv_s[ch - 16, :, t0:t0 + TT], in_=qn[b]),
                                     r=[f"qn{b}"], w=["gv_s"], dma=True)
                    else:
                        for s_ in range(4):
                            pv = 4 + s_ % 2
                            b = nz % 2
                            nz += 1
                            for kc in range(NCH):
                                P.op("pe", lambda e, kc=kc, wb=wb, s_=s_, pv=pv: e.matmul(
                                    ps[pv][:, :], lhsT=hT[:, kc, s_ * 128:(s_ + 1) * 128], rhs=wq[wb][:, kc, :],
                                    start=(kc == 0), stop=(kc == NCH - 1)),
                                    r=[f"wq{wb}", f"h{kc}"], w=[f"ps{pv}"])
                            P.op("act", lambda e, b=b, pv=pv: e.activation(out=zsb[b], in_=ps[pv][:, :], func=AF.Silu),
                                 r=[f"ps{pv}"], w=[f"zsb{b}"])
                            r0 = t0 + s_ * 128
                            c0 = (piece - 8) * 512
                            P.op("sp", lambda e, b=b, r0=r0, c0=c0: e.dma_start(out=gz_s[r0:r0 + 128, c0:c0 + 512], in_=zsb[b]),
                                 r=[f"zsb{b}"], w=["gz_s"], dma=True)
            if GDN_PHASES < 2:
                return
            P.barrier()
            A.reset()
            identb = A.alloc([128, 128], BF16, "identb")
            P.op("dve", lambda e: e.tensor_copy(identb, ident), r=["ident"], w=["identb"])
            halfm = A.alloc([128, 2], F32, "halfm")
            P.op("sp", lambda e: e.dma_start(out=halfm, in_=c_half[:, :]), w=["halfm"], dma=True)
            mS = A.alloc([128, 128], F32, "mS")
            mIT = A.alloc([128, 128], F32, "mIT")
            P.op("sp", lambda e: e.dma_start(out=mS, in_=c_mS[:, :]), w=["mS"], dma=True)
            P.op("sp", lambda e: e.dma_start(out=mIT, in_=c_mIT[:, :]), w=["mIT"], dma=True)
            mITb = A.alloc([128, 128], BF16, "mITb")
            P.op("dve", lambda e: e.tensor_copy(mITb, mIT), r=["mIT"], w=["mITb"])
            ngb = A.alloc([128, 256], F32, "ngb")
            bcast_row(ng_d[0:1, 0:128], 128, ngb[:, 0:128], "ngb")
            bcast_row(ng_d[0:1, 128:256], 128, ngb[:, 128:256], "ngb")
            P.op("dve", lambda e: e.tensor_scalar_mul(ngb, ngb, 16.0), r=["ngb"], w=["ngb"])
            eps256 = A.alloc([128, 1], F32, "eps256")
            P.op("dve", lambda e: e.memset(eps256, EPS * 256), w=["eps256"])
            Sst = A.alloc([128, GH, 256], F32, "Sst")
            Sbf = A.alloc([128, GH, 256], BF16, "Sbf")
            P.op("dve", lambda e: e.memset(Sst, 0.0), w=[f"Sst{h}" for h in range(GH)])
            P.op("dve", lambda e: e.memset(Sbf, 0.0), w=[f"Sbf{h}" for h in range(GH)])
            qa = [A.alloc([128, GH, 128], BF16, f"qa{b}") for b in range(2)]
            ka = [A.alloc([128, GH, 128], BF16, f"ka{b}") for b in range(2)]
            va = [A.alloc([128, 16, 128], BF16, f"va{b}") for b in range(2)]
            za = [A.alloc([128, 2048], F32, f"za{b}") for b in range(2)]
            gtl = [A.alloc([128, 8], F32, f"gtl{b}") for b in range(2)]
            btl = [A.alloc([128, 8], F32, f"btl{b}") for b in range(2)]
            ya = [A.alloc([128, 16, 128], BF16, f"ya{b}") for b in range(2)]
            Gcol = A.alloc([128, 8], F32, "Gcol")
            negG = A.alloc([128, 8], F32, "negG")
            eG = A.alloc([128, 8], F32, "eG")
            beG = A.alloc([128, 8], F32, "beG")
            negb = A.alloc([128, 8], F32, "negb")
            ghb = A.alloc([128, 8], BF16, "ghb")
            glb = A.alloc([128, 8], BF16, "glb")
            Ghb = A.alloc([128, 8], BF16, "Ghb")
            Ghf = A.alloc([128, 8], F32, "Ghf")
            Glf = A.alloc([128, 8], F32, "Glf")
            diagGh = A.alloc([128, 128], BF16, "diagGh")
            diagGl = A.alloc([128, 128], BF16, "diagGl")
            Ptb = A.alloc([128, 128], BF16, "Ptb")
            dd = A.alloc([128, 128], F32, "dd")
            GrowS = A.alloc([128, 128], F32, "GrowS")
            eGh = A.alloc([128, 8], BF16, "eGh")
            eGlo = A.alloc([128, 8], BF16, "eGlo")
            eGL = A.alloc([128, 16], F32, "eGL")
            sel = A.alloc([128, 2, 128], BF16, "sel")
            P.op("pool", lambda e: e.dma_start(out=sel, in_=c_sel[:, :, :]), w=["sel"], dma=True)
            EG = A.alloc([128, 128], F32, "EG")
            osbA = A.alloc([128, 256], F32, "osbA")
            osbB = A.alloc([128, 256], F32, "osbB")
            ssq8 = A.alloc([128, 8], F32, "ssq8")
            r8 = A.alloc([128, 8], F32, "r8")
            P.op("dve", lambda e: e.memset(ssq8, 1.0), w=["ssq8"])
            Ds = A.alloc([128, 128], F32, "Ds")
            DiT = A.alloc([128, 128], F32, "DiT")
            eGl = A.alloc([128, 2], F32, "eGl")
            kd2 = A.alloc([128, 2], F32, "kd2")
            eGm = A.alloc([128, 2], F32, "eGm")
            kdec = [A.alloc([128, 128], BF16, f"kdec{j}") for j in range(2)]
            vnj = [A.alloc([128, 256], BF16, f"vnj{j}") for j in range(2)]
            kbg = A.alloc([128, 128], BF16, "kbg")
            vb = A.alloc([128, 256], BF16, "vb")
            Xm = [A.alloc([128, 128], BF16, f"Xm{b}") for b in range(2)]
            Xt = [A.alloc([128, 128], BF16, f"Xt{b}") for b in range(2)]
            Pt = A.alloc([128, 128], F32, "Pt")
            usb = A.alloc([128, 256], F32, "usb")
            wT = A.alloc([128, 128], BF16, "wT")
            attnT = A.alloc([128, 128], BF16, "attnT")
            vnew = A.alloc([128, 256], BF16, "vnew")
            osb = A.alloc([128, 256], F32, "osb")
            osq = A.alloc([128, 256], F32, "osq")
            ssc = A.alloc([128, 1], F32, "ssc")
            ysb = A.alloc([128, 256], BF16, "ysb")
            DBG["sb:GrowS"] = GrowS
            DBG["sb:Gcol"] = Gcol
            DBG["sb:dd"] = dd
            DBG["sb:DiT"] = DiT
            DBG["sb:Ds"] = Ds
            DBG["sb:gtl"] = gtl[1]
            P.op("dve", lambda e: e.memset(vnew, 0.0), w=["vnew"])
            ps_bf = [ps[i][:, :].bitcast(BF16) for i in range(8)]
            gq_v = gq_s.rearrange("h p t -> p h t")
            gk_v = gk_s.rearrange("h p t -> p h t")
            gv_v = gv_s.rearrange("k p t -> p k t")
            gy_v = gy_s.rearrange("k p t -> p k t")
            for ti in range(S // 128):
                c0 = ti * 128
                lb = ti % 2
                P.op("sp", lambda e, lb=lb, c0=c0: e.dma_start(out=qa[lb], in_=gq_v[:, :, c0:c0 + 128]), r=["gq_s"], w=[f"qa{lb}"], dma=True)
                P.op("sp", lambda e, lb=lb, c0=c0: e.dma_start(out=ka[lb], in_=gk_v[:, :, c0:c0 + 128]), r=["gk_s"], w=[f"ka{lb}"], dma=True)
                P.op("sp", lambda e, lb=lb, c0=c0: e.dma_start(out=va[lb], in_=gv_v[:, :, c0:c0 + 128]), r=["gv_s"], w=[f"va{lb}"], dma=True)
                P.op("sp", lambda e, lb=lb, c0=c0: e.dma_start(out=za[lb], in_=gz_s[c0:c0 + 128, :]), r=["gz_s"], w=[f"za{lb}"], dma=True)
                P.op("sp", lambda e, lb=lb, c0=c0: e.dma_start(out=gtl[lb], in_=gg_s[c0:c0 + 128, :]), r=["gg_s"], w=[f"gtl{lb}"], dma=True)
                P.op("sp", lambda e, lb=lb, c0=c0: e.dma_start(out=btl[lb], in_=gb_s[c0:c0 + 128, :]), r=["gb_s"], w=[f"btl{lb}"], dma=True)
                P.op("dve", lambda e, lb=lb: e.tensor_copy(ghb, gtl[lb]), r=[f"gtl{lb}"], w=["ghb"])
                P.op("dve", lambda e, lb=lb: e.tensor_tensor(out=glb, in0=gtl[lb], in1=ghb, op=ALU.subtract),
                     r=[f"gtl{lb}", "ghb"], w=["glb"])
                P.op("pe", lambda e: e.matmul(ps[7][:, 0:8], lhsT=mITb, rhs=ghb, start=True, stop=False),
                     r=["mITb", "ghb"], w=["ps7"])
                P.op("pe", lambda e: e.matmul(ps[7][:, 0:8], lhsT=mITb, rhs=glb, start=False, stop=True),
                     r=["mITb", "glb"], w=["ps7"])
                P.op("dve", lambda e: e.tensor_copy(Gcol, ps[7][:, 0:8]), r=["ps7"], w=["Gcol"])
                P.op("dve", lambda e: e.tensor_copy(Ghb, Gcol), r=["Gcol"], w=["Ghb"])
                P.op("dve", lambda e: e.tensor_copy(Ghf, Ghb), r=["Ghb"], w=["Ghf"])
                P.op("dve", lambda e: e.tensor_tensor(out=Glf, in0=Gcol, in1=Ghf, op=ALU.subtract), r=["Gcol", "Ghf"], w=["Glf"])
                P.op("dve", lambda e: e.tensor_scalar_mul(negG, Gcol, -1.0), r=["Gcol"], w=["negG"])
                P.op("act", lambda e: e.activation(out=eG, in_=Gcol, func=AF.Exp), r=["Gcol"], w=["eG"])
                P.op("dve", lambda e, lb=lb: e.tensor_tensor(out=beG, in0=eG, in1=btl[lb], op=ALU.mult), r=["eG", f"btl{lb}"], w=["beG"])
                P.op("dve", lambda e: e.tensor_copy(eGh, eG), r=["eG"], w=["eGh"])
                P.op("dve", lambda e: e.tensor_tensor(out=eGlo, in0=eG, in1=eGh, op=ALU.subtract), r=["eG", "eGh"], w=["eGlo"])
                for j in range(2):
                    P.op("pe", lambda e, j=j: e.matmul(ps[7][:, 16 + 8 * j:24 + 8 * j], lhsT=sel[:, j, :], rhs=eGh, start=True, stop=False),
                         r=["sel", "eGh"], w=["ps7"])
                    P.op("pe", lambda e, j=j: e.matmul(ps[7][:, 16 + 8 * j:24 + 8 * j], lhsT=sel[:, j, :], rhs=eGlo, start=False, stop=True),
                         r=["sel", "eGlo"], w=["ps7"])
                P.op("dve", lambda e: e.tensor_copy(eGL, ps[7][:, 16:32]), r=["ps7"], w=["eGL"])
                P.op("dve", lambda e, lb=lb: e.tensor_scalar_mul(negb, btl[lb], -1.0), r=[f"btl{lb}"], w=["negb"])
                for hd in range(GH):
                    if GDN_CUT < 0.5:
                        continue
                    qT_ = qa[lb][:, hd, :]
                    kT_ = ka[lb][:, hd, :]
                    P.op("dve", lambda e, hd=hd: e.tensor_scalar(diagGh, ident, Ghf[:, hd:hd + 1], None, ALU.mult),
                         r=["ident", "Ghf"], w=["diagGh"])
                    P.op("dve", lambda e, hd=hd: e.tensor_scalar(diagGl, ident, Glf[:, hd:hd + 1], None, ALU.mult),
                         r=["ident", "Glf"], w=["diagGl"])
                    P.op("pe", lambda e: e.matmul(ps[0][:, 0:128], lhsT=ones_bf, rhs=diagGh, start=True, stop=False),
                         r=["ones", "diagGh"], w=["ps0"])
                    P.op("pe", lambda e: e.matmul(ps[0][:, 0:128], lhsT=ones_bf, rhs=diagGl, start=False, stop=True),
                         r=["ones", "diagGl"], w=["ps0"])
                    P.op("act", lambda e: e.activation(out=GrowS, in_=ps[0][:, 0:128], func=AF.Copy), r=["ps0"], w=["GrowS"])
                    if GDN_CUT < 0.6:
                        continue
                    P.op("act", lambda e, hd=hd: e.activation(out=dd, in_=GrowS, func=AF.Relu, bias=negG[:, hd:hd + 1], scale=1.0),
                         r=["GrowS", "negG"], w=["dd"])
                    P.op("act", lambda e: e.activation(out=dd, in_=dd, func=AF.Exp, scale=-1.0), r=["dd"], w=["dd"])
                    P.op("dve", lambda e: e.tensor_tensor(out=Ds, in0=dd, in1=mS, op=ALU.mult), r=["dd", "mS"], w=["Ds"])
                    if GDN_CUT < 0.7:
                        continue
                    P.op("act", lambda e, hd=hd: e.activation(out=dd, in_=GrowS, func=AF.Relu, bias=Gcol[:, hd:hd + 1], scale=-1.0),
                         r=["GrowS", "Gcol", "Ds"], w=["dd"])
                    P.op("act", lambda e: e.activation(out=dd, in_=dd, func=AF.Exp, scale=-1.0), r=["dd"], w=["dd"])
                    P.op("dve", lambda e: e.tensor_tensor(out=DiT, in0=dd, in1=mIT, op=ALU.mult), r=["dd", "mIT"], w=["DiT"])
                    if GDN_CUT < 1:
                        continue
                    P.op("pe", lambda e, kT_=kT_: e.transpose(ps_bf[1][:, 0:128], kT_, identb), r=[f"ka{lb}", "identb"], w=["ps1"])
                    for hf in range(2):
                        P.op("pe", lambda e, hf=hf, hd=hd, lb=lb: e.transpose(
                            ps_bf[1][:, 128 + hf * 128:256 + hf * 128], va[lb][:, hd * 2 + hf, :], identb),
                            r=[f"va{lb}", "identb"], w=["ps1"])
                    for j in range(2):
                        P.op("act", lambda e, j=j: e.activation(out=kdec[j], in_=ps_bf[1][:, 0:128], func=AF.Copy,
                                                               scale=DiT[:, 63 + 64 * j:64 + 64 * j]),
                             r=["ps1", "DiT"], w=[f"kdec{j}"])
                    P.op("act", lambda e, hd=hd: e.activation(out=kbg, in_=ps_bf[1][:, 0:128], func=AF.Copy, scale=beG[:, hd:hd + 1]),
                         r=["ps1", "beG"], w=["kbg"])
                    P.op("act", lambda e, hd=hd, lb=lb: e.activation(out=vb, in_=ps_bf[1][:, 128:384], func=AF.Copy,
                                                                    scale=btl[lb][:, hd:hd + 1]),
                         r=["ps1", f"btl{lb}"], w=["vb"])
                    if GDN_CUT < 2:
                        continue
                    P.op("pe", lambda e, kT_=kT_: e.matmul(ps[2][:, 0:128], lhsT=kT_, rhs=kT_, start=True, stop=True),
                         r=[f"ka{lb}"], w=["ps2"])
                    P.op("dve", lambda e, hd=hd: e.scalar_tensor_tensor(
                        out=Xm[0], in0=ps[2][:, 0:128], scalar=negb[:, hd:hd + 1], in1=Ds, op0=ALU.mult, op1=ALU.mult),
                        r=["ps2", "negb", "Ds"], w=["Xm0"])
                    P.op("pe", lambda e: e.transpose(ps_bf[1][:, 384:512], Xm[0], identb), r=["Xm0", "identb"], w=["ps1"])
                    P.op("act", lambda e: e.activation(out=Xt[0], in_=ps_bf[1][:, 384:512], func=AF.Copy), r=["ps1"], w=["Xt0"])
                    P.op("dve", lambda e: e.tensor_tensor(out=Pt, in0=Xt[0], in1=ident, op=ALU.add),
                         r=["Xt0", "ident"], w=["Pt"])
                    P.op("act", lambda e: e.activation(out=Ptb, in_=Pt, func=AF.Copy), r=["Pt"], w=["Ptb"])
                    cur = 0
                    for it_ in range(1, 6):
                        nx = 1 - cur
                        P.op("pe", lambda e, cur=cur: e.matmul(ps[3][:, 0:128], lhsT=Xt[cur], rhs=Xm[cur], start=True, stop=True),
                             r=[f"Xt{cur}", f"Xm{cur}"], w=["ps3"])
                        P.op("act", lambda e, nx=nx: e.activation(out=Xm[nx], in_=ps[3][:, 0:128], func=AF.Copy),
                             r=["ps3"], w=[f"Xm{nx}"])
                        if it_ < 5:
                            P.op("pe", lambda e, cur=cur: e.matmul(ps[3][:, 128:256], lhsT=Xm[cur], rhs=Xt[cur], start=True, stop=True),
                                 r=[f"Xt{cur}", f"Xm{cur}"], w=["ps3"])
                            P.op("act", lambda e, nx=nx: e.activation(out=Xt[nx], in_=ps[3][:, 128:256], func=AF.Copy),
                                 r=["ps3"], w=[f"Xt{nx}"])
                        P.op("pe", lambda e, nx=nx: e.matmul(ps[3][:, 256:384], lhsT=Xm[nx], rhs=Ptb, start=True, stop=True),
                             r=[f"Xm{nx}", "Ptb"], w=["ps3"])
                        P.op("dve", lambda e: e.tensor_tensor(out=Pt, in0=Pt, in1=ps[3][:, 256:384], op=ALU.add),
                             r=["ps3", "Pt"], w=["Pt"])
                        P.op("act", lambda e: e.activation(out=Ptb, in_=Pt, func=AF.Copy), r=["Pt"], w=["Ptb"])
                        cur = nx
                    if GDN_CUT < 3:
                        continue
                    P.op("pe", lambda e: e.matmul(ps[4][:, 0:256], lhsT=Ptb, rhs=vb, start=True, stop=True), r=["Ptb", "vb"], w=["ps4"])
                    P.op("pe", lambda e: e.matmul(ps[4][:, 256:384], lhsT=kbg, rhs=Ptb, start=True, stop=True), r=["Ptb", "kbg"], w=["ps4"])
                    if GDN_CUT < 3.1:
                        continue
                    P.op("act", lambda e: e.activation(out=usb, in_=ps[4][:, 0:256], func=AF.Copy), r=["ps4"], w=["usb"])
                    P.op("act", lambda e: e.activation(out=wT, in_=ps[4][:, 256:384], func=AF.Copy), r=["ps4"], w=["wT"])
                    if GDN_CUT < 3.2:
                        continue
                    P.op("pe", lambda e, kT_=kT_, qT_=qT_: e.matmul(ps[2][:, 256:384], lhsT=kT_, rhs=qT_, start=True, stop=True),
                         r=[f"ka{lb}", f"qa{lb}"], w=["ps2"])
                    if GDN_CUT < 3.3:
                        continue
                    P.op("dve", lambda e: e.tensor_tensor(out=attnT, in0=ps[2][:, 256:384], in1=DiT, op=ALU.mult),
                         r=["ps2", "DiT"], w=["attnT"])
                    if GDN_CUT < 4:
                        continue
                    for j in range(2):
                        P.op("pe", lambda e, qT_=qT_, hd=hd: e.matmul(ps[5][:, 0:256], lhsT=qT_, rhs=Sbf[:, hd, :], start=True, stop=True),
                             r=[f"qa{lb}", f"Sbf{hd}"], w=["ps5"])
                        P.op("pe", lambda e, hd=hd: e.matmul(ps[5][:, 256:512], lhsT=wT, rhs=Sbf[:, hd, :], start=True, stop=True),
                             r=["wT", f"Sbf{hd}"], w=["ps5"])
                        if GDN_CUT >= 4.04:
                            P.op("dve", lambda e, j=j: e.tensor_tensor(out=vnj[j], in0=usb, in1=ps[5][:, 256:512], op=ALU.subtract),
                                 r=["usb", "ps5"], w=[f"vnj{j}"])
                        oj = osbA if j == 0 else osbB
                        if GDN_CUT >= 4.06:
                            P.op("act", lambda e, hd=hd, oj=oj: e.activation(out=oj, in_=ps[5][:, 0:256], func=AF.Copy, scale=eG[:, hd:hd + 1]),
                                 r=["ps5", "eG"], w=["osbA" if j == 0 else "osbB"])
                        if GDN_CUT >= 4.08:
                            P.op("pe", lambda e, j=j: e.matmul(ps[6][:, 0:256], lhsT=kdec[j], rhs=vnj[j], start=True, stop=True),
                                 r=[f"kdec{j}", f"vnj{j}"], w=["ps6"])
                        if GDN_CUT >= 4.09:
                            P.op("dve", lambda e, hd=hd, j=j: e.scalar_tensor_tensor(
                                out=Sst[:, hd, :], in0=Sst[:, hd, :], scalar=eGL[:, 8 * j + hd:8 * j + hd + 1], in1=ps[6][:, 0:256],
                                op0=ALU.mult, op1=ALU.add),
                                r=[f"Sst{hd}", "eGL", "ps6"], w=[f"Sst{hd}"])
                        if GDN_CUT >= 4.095:
                            P.op("act", lambda e, hd=hd: e.activation(out=Sbf[:, hd, :], in_=Sst[:, hd, :], func=AF.Copy),
                                 r=[f"Sst{hd}"], w=[f"Sbf{hd}"])
                    if GDN_CUT < 4.3:
                        continue
                    P.op("dve", lambda e: e.tensor_scalar(osb, osbA, halfm[:, 0:1], None, ALU.mult),
                         r=["osbA", "halfm"], w=["osb"])
                    P.op("dve", lambda e: e.scalar_tensor_tensor(
                        out=osb, in0=osbB, scalar=halfm[:, 1:2], in1=osb, op0=ALU.mult, op1=ALU.add),
                        r=["osbB", "halfm", "osb"], w=["osb"])
                    if GDN_CUT < 4.6:
                        continue
                    P.op("dve", lambda e: e.tensor_scalar(vnew, vnj[0], halfm[:, 0:1], None, ALU.mult),
                         r=["vnj0", "halfm"], w=["vnew"])
                    P.op("dve", lambda e: e.scalar_tensor_tensor(
                        out=vnew, in0=vnj[1], scalar=halfm[:, 1:2], in1=vnew, op0=ALU.mult, op1=ALU.add),
                        r=["vnj1", "halfm", "vnew"], w=["vnew"])
                    if GDN_CUT < 5:
                        continue
                    P.op("pe", lambda e: e.matmul(ps[6][:, 256:512], lhsT=attnT, rhs=vnew, start=True, stop=True),
                         r=["attnT", "vnew"], w=["ps6"])
                    P.op("dve", lambda e: e.tensor_tensor(out=osb, in0=osb, in1=ps[6][:, 256:512], op=ALU.add),
                         r=["osb", "ps6"], w=["osb"])
                    P.op("act", lambda e: e.activation(out=osq, in_=osb, func=AF.Square), r=["osb"], w=["osq"])
                    P.op("dve", lambda e, hd=hd: e.reduce_sum(out=ssq8[:, hd:hd + 1], in_=osq, axis=AX.X), r=["osq", "ssq8"], w=["ssq8"])
                    P.op("act", lambda e: e.activation(out=r8, in_=ssq8, func=AF.Sqrt, bias=eps256, scale=1.0),
                         r=["ssq8", "eps256"], w=["r8"])
                    P.op("dve", lambda e: e.reciprocal(r8, r8), r=["r8"], w=["r8"])
                    P.op("dve", lambda e, hd=hd: e.scalar_tensor_tensor(
                        out=osq, in0=osb, scalar=r8[:, hd:hd + 1], in1=ngb, op0=ALU.mult, op1=ALU.mult),
                        r=["osb", "r8", "ngb", "osq"], w=["osq"])
                    P.op("dve", lambda e, hd=hd, lb=lb: e.tensor_tensor(
                        out=ysb, in0=osq, in1=za[lb][:, hd * 256:(hd + 1) * 256], op=ALU.mult),
                        r=["osq", f"za{lb}"], w=["ysb"])
                    if GDN_CUT < 6:
                        continue
                    for hf in range(2):
                        P.op("pe", lambda e, hf=hf: e.transpose(ps_bf[1][:, 512 + hf * 128:640 + hf * 128], ysb[:, hf * 128:(hf + 1) * 128], identb),
                             r=["ysb", "identb"], w=["ps1"])
                    P.op("act", lambda e, hd=hd, lb=lb: e.activation(
                        out=ya[lb][:, hd * 2:hd * 2 + 2, :], in_=ps_bf[1][:, 512:768].rearrange("p (a b) -> p a b", a=2), func=AF.Copy),
                        r=["ps1"], w=[f"ya{lb}"])
                P.op("sp", lambda e, lb=lb, c0=c0: e.dma_start(out=gy_v[:, :, c0:c0 + 128], in_=ya[lb]),
                     r=[f"ya{lb}"], w=["gy_s"], dma=True)
            if GDN_PHASES < 3:
                return
            P.barrier()
            A.reset()
            wo = A.alloc([128, 16, D], BF16, "wo")
            for q in range(4):
                P.op("pool", lambda e, q=q: e.dma_start(
                    out=wo[:, q * 4:(q + 1) * 4, :], in_=w_o[:, q * 4:(q + 1) * 4, :]), w=[f"wo{q}"], dma=True)
            yall = [A.alloc([128, 16, TT], BF16, f"yall{b}") for b in range(2)]
            for T in range(NTT):
                t0 = T * TT
                b = T % 2
                P.op("sp", lambda e, b=b, t0=t0: e.dma_start(out=yall[b], in_=gy_v[:, :, t0:t0 + TT]),
                     r=["gy_s"], w=[f"yall{b}"], dma=True)
                for d in range(NCH):
                    py = d % 4
                    for k in range(16):
                        P.op("pe", lambda e, k=k, d=d, b=b, py=py: e.matmul(
                            ps[py][:, :], lhsT=wo[:, k, d * 128:(d + 1) * 128], rhs=yall[b][:, k, :],
                            start=(k == 0), stop=(k == 15)),
                            r=["wo0", "wo1", "wo2", "wo3", f"yall{b}"], w=[f"ps{py}"])
                    P.op("dve", lambda e, d=d, py=py, t0=t0: e.tensor_tensor(
                        out=xT[:, d, t0:t0 + TT], in0=xT[:, d, t0:t0 + TT], in1=ps[py][:, :], op=ALU.add),
                        r=[f"ps{py}", f"xT{T}"], w=[f"xT{T}"])

        MIX = {0: stage_pool, 1: stage_sb, 2: stage_ret, 3: stage_gdn}
        stage_load()
        if only is not None:
            P.barrier(); P.new_epoch()
            if only.startswith("ffn"):
                stage_ffn(int(only[3:]))
            else:
                li = {"pool": 0, "sb": 1, "ret": 2, "gdn": 3}[only]
                MIX[li](li)
        else:
            sub = 0
            for l in range(4):
                if sub < n_sub:
                    if with_mixers:
                        P.barrier(); P.new_epoch()
                        MIX[l](l)
                    sub += 1
                if sub < n_sub:
                    P.barrier(); P.new_epoch()
                    stage_ffn(l)
                    sub += 1
        P.barrier(); P.new_epoch()
        stage_store()
        if DEBUG_DUMP:
            P.barrier()
            for (nm, c0, w_) in DEBUG_DUMP:
                src = DBG[nm]
                if nm.startswith("sb:"):
                    outs.append(P.op("sp", lambda e, src=src, c0=c0, w_=w_: e.dma_start(out=y_d[0:128, c0:c0 + w_], in_=src[:, 0:w_]),
                                     dma=True))
                else:
                    outs.append(P.op("sp", lambda e, src=src, c0=c0, w_=w_: e.dma_start(out=y_d[:, c0:c0 + w_], in_=src[:, 0:w_]),
                                     dma=True))

        P.finalize(nc, st)
        with nc.Block() as block:
            @block.tensor
            def _(e):
                P.emit("pe", e)

            @block.scalar
            def _(e):
                P.emit("act", e)

            @block.vector
            def _(e):
                P.emit("dve", e)

            @block.gpsimd
            def _(e):
                P.emit("pool", e)

            @block.sync
            def _(e):
                P.emit("sp", e)
                P.final_waits("sp", e, outs)
    return nc, list(dr.keys())


def host_consts():
    c = {}
    c["c_ident"] = np.eye(128, dtype=np.float32)
    inv = np.zeros((128, 4, 16), np.float32)
    for g, w in enumerate((2, 4, 8, 16)):
        for t in range(16):
            inv[:, g, t] = 1.0 / min(t + 1, w)
    c["c_poolinv"] = inv
    si = np.arange(128)[:, None, None]
    db = np.arange(4)[None, :, None]
    ti = np.arange(512)[None, None, :]
    c["c_sbmask"] = ((db * 128 + si) < ti).astype(np.float32)
    jj = np.arange(128)[:, None]
    ss = np.arange(128)[None, :]
    c["c_ustrict"] = (jj > ss).astype(np.float32)
    half = 128
    inv = (np.float32(10000.0) ** (-np.arange(half, dtype=np.float32) / np.float32(half))).astype(np.float32)
    ang = (np.arange(4096, dtype=np.float32)[None, :] * inv[:, None]).astype(np.float32)
    c["c_cos"] = np.cos(ang).astype(np.float32)
    c["c_sin"] = np.sin(ang).astype(np.float32)
    lg = np.log1p(-np.exp2(-5.0 - np.arange(4, dtype=np.float64)))
    idx = np.arange(128, dtype=np.float64)
    diff = idx[:, None] - idx[None, :]
    intra = np.where(diff >= 0, np.exp(np.maximum(diff, 0.0)[None] * lg[:, None, None]), 0.0)
    c["c_ret_intraT"] = np.ascontiguousarray(intra.transpose(0, 2, 1)).astype(np.float32)
    qdec = np.exp((idx + 1.0)[None] * lg[:, None])
    c["c_ret_qdec"] = np.ascontiguousarray(np.broadcast_to(qdec[:, None, :], (4, 128, 128))).astype(np.float32)
    kdec = np.exp((127.0 - idx)[None] * lg[:, None])
    c["c_ret_kdec"] = np.ascontiguousarray(kdec.T).astype(np.float32)
    a_ = np.arange(128)
    same = (a_[:, None] // 64) == (a_[None, :] // 64)
    c["c_gdn_maskS"] = (same & (a_[:, None] > a_[None, :])).astype(np.float32)
    c["c_gdn_maskIT"] = (same & (a_[None, :] >= a_[:, None])).astype(np.float32)
    c["c_gdn_half"] = np.stack([(a_ < 64), (a_ >= 64)], axis=1).astype(np.float32)
    sel = np.zeros((128, 2, 128), np.float32)
    sel[63, 0, :] = 1.0
    sel[127, 1, :] = 1.0
    c["c_gdn_sel"] = sel
    return c


CONST_SHAPES = {k: v.shape for k, v in host_consts().items()}


def run(inputs, n_sub=8, cores=8, with_mixers=True, only=None):
    nc, names = build(n_sub, with_mixers, only)
    consts = host_consts()
    x = np.ascontiguousarray(np.asarray(inputs["x"], dtype=np.float32))
    shared = {}
    for k in names:
        if k == "x":
            continue
        shared[k] = consts[k] if k in consts else np.ascontiguousarray(np.asarray(inputs[k], dtype=np.float32))
    in_maps = []
    for b in range(cores):
        m = dict(shared)
        m["x"] = x[b]
        in_maps.append(m)
    res = run_bass_kernel_spmd(nc, in_maps, core_ids=list(range(cores)))
    return np.stack([np.asarray(r["y"]) for r in res.results], axis=0)


def kernel(**inputs):
    return run(inputs, n_sub=8, cores=8).astype(np.float32)
```
